# Optimizing a Trainium2 kernel written in Bass

```python
import math
import jax, jax.numpy as jnp
from jax import lax
import numpy as np


D_MODEL = 2048
BATCH = 4
SEQ = 8192
DEPTH = 4

N_BRANCH = 4
BRANCH_WIDTH = D_MODEL // 4
LRU_WIDTH = BRANCH_WIDTH
LRU_BLOCKS = 8
LRU_BLOCK = LRU_WIDTH // LRU_BLOCKS
LRU_C = 8.0
CONV_A = 4
FOX_HEADS = 4
FOX_HEAD_DIM = BRANCH_WIDTH // FOX_HEADS
FOX_WIDTH = FOX_HEADS * FOX_HEAD_DIM
Q_BLOCK = 128
SSD_HEADS = 8
SSD_HEAD_DIM = 64
SSD_WIDTH = SSD_HEADS * SSD_HEAD_DIM
SSD_GROUPS = 2
SSD_STATE = 128
SSD_CHUNK = 128
CONV_C = 4
SSD_CONV_DIM = SSD_WIDTH + 2 * SSD_GROUPS * SSD_STATE
RWKV_HEADS = 8
RWKV_HEAD = 64
RWKV_WIDTH = RWKV_HEADS * RWKV_HEAD
RWKV_W_RANK = 64
RWKV_A_RANK = 64
RWKV_G_RANK = 128
RWKV_LN_EPS = 64e-5
RWKV_SPLIT = (RWKV_WIDTH, RWKV_WIDTH, RWKV_WIDTH, RWKV_W_RANK, RWKV_A_RANK, RWKV_G_RANK)
RWKV_PROJ = sum(RWKV_SPLIT)
IN_SPLIT = (N_BRANCH * D_MODEL,
            LRU_WIDTH, LRU_WIDTH,
            FOX_WIDTH, FOX_WIDTH, FOX_WIDTH, FOX_HEADS,
            SSD_WIDTH, SSD_CONV_DIM, SSD_HEADS,
            RWKV_PROJ)
IN_COLS = sum(IN_SPLIT)
D_FF = 5632
CONV_F = 3
PLE_DIM = 256
EPS = 1e-6

kernel_name = 'hybrid_gated_parallel_rglru_fox_ssd_rwkv7'


def _offsets(sizes):
    out, s = [], 0
    for n in sizes[:-1]:
        s += n
        out.append(s)
    return out


def rmsnorm(x, w, eps=EPS):
    xf = x.astype(jnp.float32)
    y = xf * lax.rsqrt(jnp.mean(xf * xf, axis=-1, keepdims=True) + eps)
    return (y * w.astype(jnp.float32)).astype(x.dtype)


def causal_dwconv(x, w, b):
    K, C = w.shape
    y = lax.conv_general_dilated(x, w.astype(x.dtype)[:, None, :], window_strides=(1,),
                                 padding=[(K - 1, 0)], dimension_numbers=('NWC', 'WIO', 'NWC'),
                                 feature_group_count=C)
    return y + b.astype(x.dtype)


def token_shift(u):
    return jnp.pad(u, ((0, 0), (1, 0), (0, 0)))[:, :-1]


def rg_lru_branch(x, gate_x, conv_w, conv_b, w_r, b_r, w_i, b_i, lam):
    Bsz, S, W = x.shape
    xc = causal_dwconv(x, conv_w, conv_b).astype(jnp.float32)
    xb = xc.reshape(Bsz, S, LRU_BLOCKS, LRU_BLOCK)
    r = jax.nn.sigmoid(jnp.einsum('bshi,hij->bshj', xb, w_r).reshape(Bsz, S, W) + b_r)
    i = jax.nn.sigmoid(jnp.einsum('bshi,hij->bshj', xb, w_i).reshape(Bsz, S, W) + b_i)
    log_a = -LRU_C * r * jax.nn.softplus(-lam.astype(jnp.float32))
    a = jnp.exp(log_a)
    u = xc * i * jnp.sqrt(-jnp.expm1(2.0 * log_a))

    def combine(c1, c2):
        a1, b1 = c1
        a2, b2 = c2
        return a1 * a2, a2 * b1 + b2

    _, hseq = lax.associative_scan(combine, (a, u), axis=1)
    return (hseq * jax.nn.gelu(gate_x.astype(jnp.float32))).astype(x.dtype)


def forgetting_attention_branch(q, k, v, f_logit, f_bias, q_gain, k_gain):
    Bsz, S, _ = q.shape
    H, Dh = FOX_HEADS, FOX_HEAD_DIM
    q = rmsnorm(q.reshape(Bsz, S, H, Dh), q_gain)
    k = rmsnorm(k.reshape(Bsz, S, H, Dh), k_gain)
    v = v.reshape(Bsz, S, H, Dh)
    log_f = jax.nn.log_sigmoid(f_logit.astype(jnp.float32) + f_bias.astype(jnp.float32))
    c = jnp.cumsum(log_f, axis=1)
    cT = c.transpose(0, 2, 1)
    nblk = S // Q_BLOCK
    qb = q.reshape(Bsz, nblk, Q_BLOCK, H, Dh).transpose(1, 0, 2, 3, 4)
    cb = c.reshape(Bsz, nblk, Q_BLOCK, H).transpose(1, 0, 3, 2)
    kpos = jnp.arange(S)
    scale = Dh ** -0.5

    def block(args):
        qi, ci, n = args
        s = jnp.einsum('bqhd,bkhd->bhqk', qi, k, preferred_element_type=jnp.float32) * scale
        s = s + (ci[..., :, None] - cT[..., None, :])
        qpos = n * Q_BLOCK + jnp.arange(Q_BLOCK)
        s = jnp.where(kpos[None, :] <= qpos[:, None], s, -jnp.inf)
        pr = jax.nn.softmax(s, axis=-1)
        return jnp.einsum('bhqk,bkhd->bqhd', pr.astype(v.dtype), v)

    o = lax.map(block, (qb, cb, jnp.arange(nblk)))
    return o.transpose(1, 0, 2, 3, 4).reshape(Bsz, S, H * Dh)


def segsum(x):
    T = x.shape[-1]
    cs = jnp.cumsum(x, axis=-1)
    out = cs[..., :, None] - cs[..., None, :]
    return jnp.where(jnp.tril(jnp.ones((T, T), dtype=bool)), out, -jnp.inf)


def ssd_chunked(x, dt, A, bm, cm, d_skip):
    Bsz, S, H, P = x.shape
    L = SSD_CHUNK
    nc = S // L
    rep = H // SSD_GROUPS
    N = bm.shape[-1]
    Bc = jnp.repeat(bm, rep, axis=2).reshape(Bsz, nc, L, H, N)
    Cc = jnp.repeat(cm, rep, axis=2).reshape(Bsz, nc, L, H, N)
    X = (x * dt[..., None]).reshape(Bsz, nc, L, H, P)
    dA = (dt * A).reshape(Bsz, nc, L, H).transpose(0, 3, 1, 2)
    A_cum = jnp.cumsum(dA, axis=-1)
    Lmat = jnp.exp(segsum(dA))
    scores = jnp.einsum('bclhn,bcshn->bhcls', Cc, Bc) * Lmat
    y_diag = jnp.einsum('bhcls,bcshp->bclhp', scores, X)
    decay_states = jnp.exp(A_cum[..., -1:] - A_cum)
    states = jnp.einsum('bcshn,bhcs,bcshp->bchpn', Bc, decay_states, X)
    decay_chunk = jnp.exp(segsum(jnp.pad(A_cum[..., -1], ((0, 0), (0, 0), (1, 0)))))
    states = jnp.pad(states, ((0, 0), (1, 0), (0, 0), (0, 0), (0, 0)))
    states = jnp.einsum('bhzc,bchpn->bzhpn', decay_chunk, states)[:, :-1]
    y_off = jnp.einsum('bclhn,bchpn,bhcl->bclhp', Cc, states, jnp.exp(A_cum))
    return (y_diag + y_off).reshape(Bsz, S, H, P) + x * d_skip.astype(jnp.float32)[:, None]


def mamba2_branch(z, xbc, dt_raw, conv_w, conv_b, dt_bias, a_log, d_skip, norm_w):
    Bsz, S, _ = z.shape
    xbc = jax.nn.silu(causal_dwconv(xbc, conv_w, conv_b)).astype(jnp.float32)
    xs, bm, cm = jnp.split(xbc, [SSD_WIDTH, SSD_WIDTH + SSD_GROUPS * SSD_STATE], axis=-1)
    dt = jax.nn.softplus(dt_raw.astype(jnp.float32) + dt_bias.astype(jnp.float32))
    A = -jnp.exp(a_log.astype(jnp.float32))
    y = ssd_chunked(xs.reshape(Bsz, S, SSD_HEADS, SSD_HEAD_DIM), dt, A,
                    bm.reshape(Bsz, S, SSD_GROUPS, SSD_STATE), cm.reshape(Bsz, S, SSD_GROUPS, SSD_STATE), d_skip)
    y = y.reshape(Bsz, S, SSD_WIDTH) * jax.nn.silu(z.astype(jnp.float32))
    y = rmsnorm(y.reshape(Bsz, S, SSD_GROUPS, -1), norm_w.reshape(SSD_GROUPS, -1)).reshape(Bsz, S, SSD_WIDTH)
    return y.astype(z.dtype)


def rwkv7_branch(u_in, mu, w0, w2, a0, a2, g2, k_k, k_a, r_k, ln_w, ln_b):
    Bsz, S, _ = u_in.shape
    H, N = RWKV_HEADS, RWKV_HEAD
    u = u_in.astype(jnp.float32)
    u = u + (token_shift(u) - u) * mu
    r, k, v, wl, al, gl = jnp.split(u, _offsets(RWKV_SPLIT), axis=-1)
    w = -jax.nn.softplus(-(w0 + jnp.tanh(wl) @ w2)) - 0.5
    decay = jnp.exp(-jnp.exp(w))
    a = jax.nn.sigmoid(a0 + al @ a2)
    g = jax.nn.sigmoid(gl) @ g2

    def hd(t):
        return t.reshape(Bsz, S, H, N)

    kk = hd(k * k_k)
    kk = kk / jnp.maximum(jnp.linalg.norm(kk, axis=-1, keepdims=True), 1e-12)
    k = k * (1.0 + (a - 1.0) * k_a)
    r, decay, k, v, a = hd(r), hd(decay), hd(k), hd(v), hd(a)

    def step(state, inp):
        r_t, w_t, k_t, v_t, kk_t, a_t = inp
        sa = jnp.einsum('bhvk,bhk->bhv', state, -kk_t)
        state = (state * w_t[:, :, None, :] + sa[..., None] * (kk_t * a_t)[:, :, None, :]
                 + v_t[..., None] * k_t[:, :, None, :])
        return state, jnp.einsum('bhvk,bhk->bhv', state, r_t)

    xs = tuple(t.transpose(1, 0, 2, 3) for t in (r, decay, k, v, kk, a))
    state0 = jnp.zeros((Bsz, H, N, N), jnp.float32)
    _, y = lax.scan(step, state0, xs)
    y = y.transpose(1, 0, 2, 3)
    mean = jnp.mean(y, axis=-1, keepdims=True)
    var = jnp.mean(jnp.square(y - mean), axis=-1, keepdims=True)
    y = (y - mean) * lax.rsqrt(var + RWKV_LN_EPS) * ln_w.reshape(H, N) + ln_b.reshape(H, N)
    y = y + jnp.sum(r * k * r_k, axis=-1, keepdims=True) * v
    return (y.reshape(Bsz, S, RWKV_WIDTH) * g).astype(u_in.dtype)


def setup_inputs(seed: int = 0) -> dict:
    key = jax.random.key(seed)
    ks = iter(jax.random.split(key, 64))
    L = DEPTH
    f32 = jnp.float32

    def nrm(shape, scale):
        return scale * jax.random.normal(next(ks), shape, f32)

    def unif(shape, lo, hi):
        return jax.random.uniform(next(ks), shape, f32, lo, hi)

    out = {}
    out['x'] = nrm((BATCH, SEQ, D_MODEL), 1.0)
    out['p'] = nrm((DEPTH, BATCH, SEQ, PLE_DIM), 1.0)
    out['mix_norm'] = 1.0 + nrm((L, D_MODEL), 0.05)
    out['w_in'] = nrm((L, D_MODEL, IN_COLS), D_MODEL ** -0.5)
    out['lru_conv_w'] = nrm((L, CONV_A, LRU_WIDTH), CONV_A ** -0.5)
    out['lru_conv_b'] = nrm((L, LRU_WIDTH), 0.01)
    out['lru_wr'] = nrm((L, LRU_BLOCKS, LRU_BLOCK, LRU_BLOCK), LRU_BLOCK ** -0.5)
    out['lru_br'] = nrm((L, LRU_WIDTH), 0.01)
    out['lru_wi'] = nrm((L, LRU_BLOCKS, LRU_BLOCK, LRU_BLOCK), LRU_BLOCK ** -0.5)
    out['lru_bi'] = nrm((L, LRU_WIDTH), 0.01)
    a_c = unif((L, LRU_WIDTH), 0.9, 0.999) ** (1.0 / LRU_C)
    out['lru_lambda'] = jnp.log(a_c) - jnp.log1p(-a_c)
    out['fox_fbias'] = unif((L, FOX_HEADS), 2.0, 5.0)
    out['fox_q_gain'] = 1.0 + nrm((L, FOX_HEAD_DIM), 0.05)
    out['fox_k_gain'] = 1.0 + nrm((L, FOX_HEAD_DIM), 0.05)
    out['ssd_conv_w'] = nrm((L, CONV_C, SSD_CONV_DIM), CONV_C ** -0.5)
    out['ssd_conv_b'] = nrm((L, SSD_CONV_DIM), 0.01)
    dt0 = jnp.exp(unif((L, SSD_HEADS), math.log(1e-3), math.log(1e-1)))
    out['ssd_dt_bias'] = dt0 + jnp.log(-jnp.expm1(-dt0))
    out['ssd_a_log'] = jnp.log(unif((L, SSD_HEADS), 1.0, 16.0))
    out['ssd_d'] = 1.0 + nrm((L, SSD_HEADS), 0.1)
    out['ssd_norm'] = 1.0 + nrm((L, SSD_WIDTH), 0.05)
    out['rwkv_mu'] = unif((L, RWKV_PROJ), 0.0, 1.0)
    out['rwkv_w0'] = unif((L, RWKV_WIDTH), -6.0, -1.0)
    out['rwkv_w2'] = nrm((L, RWKV_W_RANK, RWKV_WIDTH), 0.5 * RWKV_W_RANK ** -0.5)
    out['rwkv_a0'] = nrm((L, RWKV_WIDTH), 0.1)
    out['rwkv_a2'] = nrm((L, RWKV_A_RANK, RWKV_WIDTH), 0.5 * RWKV_A_RANK ** -0.5)
    out['rwkv_g2'] = nrm((L, RWKV_G_RANK, RWKV_WIDTH), RWKV_G_RANK ** -0.5)
    out['rwkv_kk'] = 0.85 + nrm((L, RWKV_WIDTH), 0.1)
    out['rwkv_ka'] = 1.0 + nrm((L, RWKV_WIDTH), 0.1)
    out['rwkv_rk'] = nrm((L, RWKV_HEADS, RWKV_HEAD), 0.1)
    out['rwkv_ln_w'] = 1.0 + nrm((L, RWKV_WIDTH), 0.05)
    out['rwkv_ln_b'] = nrm((L, RWKV_WIDTH), 0.01)
    out['w_branch'] = nrm((L, N_BRANCH, BRANCH_WIDTH, D_MODEL), BRANCH_WIDTH ** -0.5)
    out['w_out'] = nrm((L, D_MODEL, D_MODEL), D_MODEL ** -0.5)
    out['ffn_norm'] = 1.0 + nrm((L, D_MODEL), 0.05)
    out['w_up'] = nrm((L, D_MODEL, 2 * D_FF), D_MODEL ** -0.5)
    out['ffn_conv_w'] = nrm((L, CONV_F, 2 * D_FF), CONV_F ** -0.5)
    out['ffn_conv_b'] = nrm((L, 2 * D_FF), 0.01)
    out['w_down'] = nrm((L, D_FF, D_MODEL), D_FF ** -0.5)
    out['ple_norm'] = 1.0 + nrm((L, D_MODEL), 0.05)
    out['w_ple_gate'] = nrm((L, D_MODEL, D_MODEL), D_MODEL ** -0.5)
    out['w_ple'] = nrm((L, PLE_DIM, D_MODEL), PLE_DIM ** -0.5)
    return out


def reference(x, p, mix_norm, w_in, lru_conv_w, lru_conv_b, lru_wr, lru_br, lru_wi, lru_bi, lru_lambda,
              fox_fbias, fox_q_gain, fox_k_gain, ssd_conv_w, ssd_conv_b, ssd_dt_bias, ssd_a_log, ssd_d,
              ssd_norm, rwkv_mu, rwkv_w0, rwkv_w2, rwkv_a0, rwkv_a2, rwkv_g2, rwkv_kk, rwkv_ka, rwkv_rk,
              rwkv_ln_w, rwkv_ln_b, w_branch, w_out, ffn_norm, w_up, ffn_conv_w, ffn_conv_b, w_down,
              ple_norm, w_ple_gate, w_ple):
    Bsz, S, D = x.shape
    h = x
    for i in range(DEPTH):
        xn = rmsnorm(h, mix_norm[i])
        proj = jnp.einsum('bsd,dn->bsn', xn, w_in[i])
        (g_raw, a_x, a_g, b_q, b_k, b_v, b_f, c_z, c_xbc, c_dt, d_u) = jnp.split(
            proj, _offsets(IN_SPLIT), axis=-1)
        y_a = rg_lru_branch(a_x, a_g, lru_conv_w[i], lru_conv_b[i], lru_wr[i], lru_br[i],
                            lru_wi[i], lru_bi[i], lru_lambda[i])
        y_b = forgetting_attention_branch(b_q, b_k, b_v, b_f, fox_fbias[i], fox_q_gain[i], fox_k_gain[i])
        y_c = mamba2_branch(c_z, c_xbc, c_dt, ssd_conv_w[i], ssd_conv_b[i], ssd_dt_bias[i],
                            ssd_a_log[i], ssd_d[i], ssd_norm[i])
        y_d = rwkv7_branch(d_u, rwkv_mu[i], rwkv_w0[i], rwkv_w2[i], rwkv_a0[i], rwkv_a2[i], rwkv_g2[i],
                           rwkv_kk[i], rwkv_ka[i], rwkv_rk[i], rwkv_ln_w[i], rwkv_ln_b[i])
        gates = jax.nn.sigmoid(g_raw.astype(jnp.float32)).reshape(Bsz, S, N_BRANCH, D)
        branches = (y_a, y_b, y_c, y_d)
        merged = gates[:, :, 0] * jnp.einsum('bsc,cd->bsd', branches[0], w_branch[i, 0])
        for j in range(1, N_BRANCH):
            merged = merged + gates[:, :, j] * jnp.einsum('bsc,cd->bsd', branches[j], w_branch[i, j])
        h = h + jnp.einsum('bsd,de->bse', merged, w_out[i]).astype(h.dtype)
        hn = rmsnorm(h, ffn_norm[i])
        u = causal_dwconv(jnp.einsum('bsd,df->bsf', hn, w_up[i]), ffn_conv_w[i], ffn_conv_b[i])
        u_gate, u_val = jnp.split(u, 2, axis=-1)
        h = h + jnp.einsum('bsf,fd->bsd', jax.nn.gelu(u_gate) * u_val, w_down[i]).astype(h.dtype)
        pg = jax.nn.sigmoid(jnp.einsum('bsd,de->bse', rmsnorm(h, ple_norm[i]), w_ple_gate[i]))
        h = h + (pg * jnp.einsum('bsq,qd->bsd', p[i], w_ple[i])).astype(h.dtype)
    return h
```

```python
import contextlib
import numpy as np
import concourse.bass as bass
import concourse.mybir as mybir

F32 = mybir.dt.float32
F32R = mybir.dt.float32r
AF = mybir.ActivationFunctionType
ALU = mybir.AluOpType
AX = mybir.AxisListType


class Buf:
    __slots__ = ("name", "t", "last_w", "readers")

    def __init__(self, name, t):
        self.name = name
        self.t = t
        self.last_w = None
        self.readers = []


class V:
    __slots__ = ("ap", "bufs")

    def __init__(self, ap, bufs):
        self.ap = ap
        self.bufs = bufs

    def __getitem__(self, k):
        return V(self.ap[k], self.bufs)

    def r(self):
        return V(self.ap.bitcast(F32R), self.bufs)

    def f(self):
        return V(self.ap.bitcast(F32), self.bufs)

    def re(self, s, **kw):
        return V(self.ap.rearrange(s, **kw), self.bufs)

    def bc(self, shape):
        return V(self.ap.to_broadcast(shape), self.bufs)


class K:
    ENG = ("pe", "dve", "act", "pool", "sp")

    def __init__(self, nc, same_engine_sync=True, ndma_sems=12):
        self.nc = nc
        self.es = contextlib.ExitStack()
        self.eng = {"pe": nc.tensor, "dve": nc.vector, "act": nc.scalar, "pool": nc.gpsimd, "sp": nc.sync}
        self.ops = {e: [] for e in self.ENG}
        self.csem = {e: self.es.enter_context(nc.semaphore("c_" + e)) for e in self.ENG}
        self.ccnt = {e: 0 for e in self.ENG}
        self.known = {e: {} for e in self.ENG}
        self.same = same_engine_sync
        self.dsem = {}
        self.dval = {}
        self.dptr = {}
        for q in ("sp", "pool", "act"):
            self.dsem[q] = [self.es.enter_context(nc.semaphore("d_%s%d" % (q, i))) for i in range(ndma_sems)]
            self.dval[q] = [0] * ndma_sems
            self.dptr[q] = 0
        self.n_ops = 0
        self.sb_used = 0

    def sb(self, name, shape, dtype=F32):
        self.n_ops += 1
        name = "%s_u%d" % (name, self.n_ops)
        t = self.es.enter_context(self.nc.sbuf_tensor(name, list(shape), dtype))
        b = Buf(name, t)
        return V(t[:] if hasattr(t, "__getitem__") else t.ap(), [b])

    def ps(self, name, shape, dtype=F32):
        t = self.es.enter_context(self.nc.psum_tensor(name, list(shape), dtype))
        b = Buf(name, t)
        return V(t[:], [b])

    def dram(self, name, shape, dtype=F32, kind="Internal"):
        t = self.nc.dram_tensor(name, list(shape), dtype, kind=kind)
        b = Buf(name, t)
        return V(t.ap(), [b])

    def alias(self, v, name):
        return V(v.ap, [Buf(name, None)])

    def _waits(self, e, reads, writes):
        need = {}

        def add(ev):
            if ev is None:
                return
            s, v = ev
            if need.get(s, 0) < v:
                need[s] = v

        for vw in reads:
            for b in vw.bufs:
                add(b.last_w)
        for vw in writes:
            for b in vw.bufs:
                add(b.last_w)
                for ev in b.readers:
                    add(ev)
        out = []
        kn = self.known[e]
        own = self.csem[e]
        for s, v in need.items():
            if s is own and (not self.same or e == "pe"):
                continue
            if kn.get(s, 0) >= v:
                continue
            kn[s] = v
            out.append((s, v))
        return out

    def _mark(self, ev, reads, writes):
        for vw in reads:
            for b in vw.bufs:
                b.readers.append(ev)
                if len(b.readers) > 24:
                    d = {}
                    for s, v in b.readers:
                        if d.get(s, 0) < v:
                            d[s] = v
                    b.readers = list(d.items())
        for vw in writes:
            for b in vw.bufs:
                b.last_w = ev
                b.readers = []

    def op(self, e, fn, reads, writes, signal=True):
        waits = self._waits(e, reads, writes)
        sem = self.csem[e]
        ev = (sem, self.ccnt[e] + 1)
        if signal:
            self.ccnt[e] += 1
        self._mark(ev, reads, writes)
        self.n_ops += 1

        eng = self.eng[e]
        for s, v in waits:
            eng.wait_ge(s, v)
        ins = fn(eng)
        if signal:
            ins.then_inc(sem, 1)

    def dma(self, q, out, in_, **kw):
        i = self.dptr[q]
        self.dptr[q] = (i + 1) % len(self.dsem[q])
        sem = self.dsem[q][i]
        prev = self.dval[q][i]
        waits = self._waits(q, [in_], [out])
        kn = self.known[q]
        if prev > 0 and kn.get(sem, 0) < prev:
            kn[sem] = prev
            waits.append((sem, prev))
        val = prev + 16
        self.dval[q][i] = val
        ev = (sem, val)
        self._mark(ev, [in_], [out])
        self.n_ops += 1
        oap, iap = out.ap, in_.ap

        eng = self.eng[q]
        for s, v in waits:
            eng.wait_ge(s, v)
        eng.dma_start(out=oap, in_=iap, **kw).then_inc(sem, 16)
        return ev

    def barrier(self):
        evs = [(self.csem[x], self.ccnt[x]) for x in self.ENG if self.ccnt[x] > 0]
        for q in self.dsem:
            for s, v in zip(self.dsem[q], self.dval[q]):
                if v > 0:
                    evs.append((s, v))
        for e in self.ENG:
            kn = self.known[e]
            for s, v in evs:
                if s is self.csem[e]:
                    continue
                if kn.get(s, 0) < v:
                    kn[s] = v
                    self.eng[e].wait_ge(s, v)

    def scope(self):
        k = self

        class _S:
            def __enter__(s):
                s.old = k.es
                k.es = contextlib.ExitStack()
                return s

            def __exit__(s, *a):
                k.barrier()
                k.es.close()
                k.es = s.old
                return False
        return _S()

    def finish(self):
        self.barrier()
        self.es.close()

    def mm(self, out, lhsT, rhs, start=True, stop=True, signal=True):
        self.op("pe", lambda e: e.matmul(out.ap, lhsT.ap, rhs.ap, start=start, stop=stop),
                [lhsT, rhs] + ([] if start else [out]), [out], signal=signal)

    def tr(self, out, in_, ident):
        self.op("pe", lambda e: e.transpose(out.ap, in_.ap, ident.ap), [in_, ident], [out])

    def actv(self, out, in_, func, bias=0.0, scale=1.0, eng="act"):
        rd = [in_]
        b = bias
        s = scale
        if isinstance(bias, V):
            rd.append(bias)
            b = bias.ap
        if isinstance(scale, V):
            rd.append(scale)
            s = scale.ap
        self.op("act", lambda e: e.activation(out.ap, in_.ap, func, bias=b, scale=s), rd, [out])

    def tt(self, out, a, b, op, eng="dve"):
        self.op(eng, lambda e: e.tensor_tensor(out.ap, a.ap, b.ap, op), [a, b], [out])

    def ts(self, out, a, s1, s2, op0, op1=None, eng="dve"):
        rd = [a]
        x1, x2 = s1, s2
        if isinstance(s1, V):
            rd.append(s1)
            x1 = s1.ap
        if isinstance(s2, V):
            rd.append(s2)
            x2 = s2.ap
        if op1 is None:
            self.op(eng, lambda e: e.tensor_scalar(out.ap, a.ap, x1, None, op0), rd, [out])
        else:
            self.op(eng, lambda e: e.tensor_scalar(out.ap, a.ap, x1, x2, op0, op1), rd, [out])

    def stt(self, out, a, s, b, op0, op1):
        rd = [a, b]
        x = s
        if isinstance(s, V):
            rd.append(s)
            x = s.ap
        self.op("dve", lambda e: e.scalar_tensor_tensor(out.ap, a.ap, x, b.ap, op0, op1), rd, [out])

    def scan(self, out, d0, d1, init, op0, op1):
        rd = [d0, d1]
        x = init
        if isinstance(init, V):
            rd.append(init)
            x = init.ap
        self.op("dve", lambda e: e.tensor_tensor_scan(out.ap, d0.ap, d1.ap, x, op0, op1), rd, [out])

    def copy(self, out, in_, eng="dve"):
        if eng == "act":
            self.op("act", lambda e: e.copy(out.ap, in_.ap), [in_], [out])
        else:
            self.op(eng, lambda e: e.tensor_copy(out.ap, in_.ap), [in_], [out])

    def memset(self, out, val, eng="pool"):
        self.op(eng, lambda e: e.memset(out.ap, val), [], [out])

    def recip(self, out, in_):
        self.op("dve", lambda e: e.reciprocal(out.ap, in_.ap), [in_], [out])

from concourse.bass_utils import run_bass_kernel_spmd

EPS = 1e-6
RW_LN_EPS = 64e-5
NEG = -1.0e30


class Cfg:
    def __init__(self, D, T, L, DFF, FGS, dbg=False):
        self.D, self.T, self.L, self.DFF = D, T, L, DFF
        self.KC = D // 128
        self.BW = D // 4
        self.BC = self.BW // 128
        self.FH = self.BW // 128
        self.SH = self.BW // 64
        self.G = 2
        self.RH = self.BW // 64
        self.FB = DFF // 128
        self.FGS = FGS
        assert self.FB % FGS == 0
        self.NT = 512
        self.NTT = T // 512
        BC, G = self.BC, self.G
        o = {}
        o["ax"] = 0; o["ag"] = BC; o["bq"] = 2 * BC; o["bk"] = 3 * BC; o["bv"] = 4 * BC
        o["cz"] = 5 * BC; o["cx"] = 6 * BC; o["cb"] = 7 * BC; o["cc"] = 7 * BC + G
        o["dr"] = 7 * BC + 2 * G; o["dk"] = o["dr"] + BC; o["dv"] = o["dk"] + BC
        o["dwa"] = o["dv"] + BC; o["dgl"] = o["dwa"] + 1; o["small"] = o["dgl"] + 1
        self.off = o
        self.NBLK = o["small"] + 1
        self.NXBC = BC + 2 * G
        self.NU = 3 * BC + 2
        v = {}
        n = 0

        def add(name, c):
            nonlocal n
            v[name] = n
            n += c
        KC = self.KC
        add("mixn", KC); add("ffnn", KC); add("plen", KC)
        add("lru_cw", 4 * BC); add("lru_cb", BC); add("lru_br", BC); add("lru_bi", BC); add("lru_lam", BC)
        add("fox_qg", 1); add("fox_kg", 1)
        add("sm_sign", 1); add("sm_bias", 1); add("ssd_alog", 1)
        add("ssd_cw", 4 * self.NXBC); add("ssd_cb", self.NXBC); add("ssd_d", BC); add("ssd_nw", BC)
        add("rw_mu", self.NU); add("rw_w0", BC); add("rw_a0", BC); add("rw_kk", BC); add("rw_ka", BC)
        add("rw_rk", BC); add("rw_lnw", BC); add("rw_lnb", BC)
        add("ffn_cw", 3 * 2 * self.FB); add("ffn_cb", 2 * self.FB)
        self.vec = v
        self.NV = n
        c = {}
        n = 0

        def addc(name, cc):
            nonlocal n
            c[name] = n
            n += cc
        addc("ident", 128); addc("ones", 128); addc("mneg", 128); addc("m_su", 128); addc("m_iu", 128)
        addc("m_sl", 128); addc("bo64", 128); addc("reset64", 512); addc("reset128", 512)
        addc("sel", self.SH * 128)
        self.cst = c
        self.NCST = n
        self.dbg = dbg


def colvec(v):
    v = np.asarray(v, np.float32).reshape(-1, 128)
    return v.T


def pack_w(W):
    Kd, N = W.shape
    return np.ascontiguousarray(W.reshape(Kd // 128, 128, N // 128, 128).transpose(2, 1, 0, 3))


def host_consts(cfg):
    c = np.zeros((128, cfg.NCST), np.float32)
    o = cfg.cst
    p = np.arange(128)[:, None]
    f = np.arange(128)[None, :]
    c[:, o["ident"]:o["ident"] + 128] = (p == f)
    c[:, o["ones"]:o["ones"] + 128] = 1.0
    c[:, o["mneg"]:o["mneg"] + 128] = np.where(p <= f, 0.0, NEG)
    c[:, o["m_su"]:o["m_su"] + 128] = (p < f)
    c[:, o["m_iu"]:o["m_iu"] + 128] = (p <= f)
    c[:, o["m_sl"]:o["m_sl"] + 128] = (p > f)
    c[:, o["bo64"]:o["bo64"] + 128] = ((p // 64) == (f // 64))
    t = np.arange(512)[None, :]
    c[:, o["reset64"]:o["reset64"] + 512] = (t % 64 != 0)
    c[:, o["reset128"]:o["reset128"] + 512] = (t % 128 != 0)
    for h in range(cfg.SH):
        c[cfg.FH + h, o["sel"] + h * 128: o["sel"] + (h + 1) * 128] = 1.0
    return c


def host_pack(cfg, inp, b):
    D, T, L, BW, BC, G = cfg.D, cfg.T, cfg.L, cfg.BW, cfg.BC, cfg.G
    KC, FB = cfg.KC, cfg.FB
    f32 = np.float32
    m = {}
    m["xT"] = np.ascontiguousarray(np.asarray(inp["x"][b], f32).T)
    m["pT"] = np.ascontiguousarray(np.asarray(inp["p"][:, b], f32).transpose(0, 2, 1))
    m["cst"] = host_consts(cfg)
    w_in = np.asarray(inp["w_in"], f32)
    base = 4 * D
    cols = []
    offs = {}
    cur = base
    for name, wd in (("ax", BW), ("ag", BW), ("bq", BW), ("bk", BW), ("bv", BW), ("bf", cfg.FH), ("cz", BW),
                     ("cxbc", BW + 2 * G * 128), ("cdt", cfg.SH), ("du", 3 * BW + 256)):
        offs[name] = cur
        cur += wd
    assert cur == w_in.shape[2]
    idx = []
    for name, wd in (("ax", BW), ("ag", BW), ("bq", BW), ("bk", BW), ("bv", BW), ("cz", BW),
                     ("cxbc", BW + 2 * G * 128), ("du", 3 * BW + 256)):
        idx += list(range(offs[name], offs[name] + wd))
    small = list(range(offs["bf"], offs["bf"] + cfg.FH)) + list(range(offs["cdt"], offs["cdt"] + cfg.SH))
    nsm = len(small)
    wmix = np.zeros((L, cfg.NBLK, 128, KC, 128), f32)
    wgate = np.zeros((L, KC * 4, 128, KC, 128), f32)
    for l in range(L):
        W = w_in[l]
        Wm = np.zeros((D, cfg.NBLK * 128), f32)
        Wm[:, :len(idx)] = W[:, idx]
        Wm[:, (cfg.NBLK - 1) * 128:(cfg.NBLK - 1) * 128 + nsm] = W[:, small]
        wmix[l] = pack_w(Wm)
        Wg = W[:, :4 * D].reshape(D, 4, KC, 128).transpose(0, 2, 1, 3).reshape(D, 4 * D)
        wgate[l] = pack_w(Wg)
    m["wmix"] = wmix
    m["wgate"] = wgate
    wb = np.asarray(inp["w_branch"], f32)
    m["wbr"] = np.ascontiguousarray(wb.reshape(L, 4, BC, 128, KC, 128).transpose(0, 4, 3, 1, 2, 5))
    m["wout"] = np.stack([pack_w(np.asarray(inp["w_out"][l], f32)) for l in range(L)])
    wu = np.asarray(inp["w_up"], f32)
    m["wup"] = np.stack([pack_w(wu[l]) for l in range(L)])
    wd_ = np.asarray(inp["w_down"], f32)
    m["wdown"] = np.stack([pack_w(wd_[l]) for l in range(L)])
    m["wpg"] = np.stack([pack_w(np.asarray(inp["w_ple_gate"][l], f32)) for l in range(L)])
    m["wple"] = np.stack([pack_w(np.asarray(inp["w_ple"][l], f32)) for l in range(L)])
    wr = np.asarray(inp["lru_wr"], f32)
    wi = np.asarray(inp["lru_wi"], f32)
    bd = np.zeros((L, 2, BC, 128, 128), f32)
    for l in range(L):
        for cc in range(BC):
            for bb in range(2):
                bd[l, 0, cc, bb * 64:(bb + 1) * 64, bb * 64:(bb + 1) * 64] = wr[l, 2 * cc + bb]
                bd[l, 1, cc, bb * 64:(bb + 1) * 64, bb * 64:(bb + 1) * 64] = wi[l, 2 * cc + bb]
    m["lrubd"] = bd
    m["rwwa2"] = np.concatenate([np.asarray(inp["rwkv_w2"], f32), np.asarray(inp["rwkv_a2"], f32)], axis=1)
    m["rwg2"] = np.asarray(inp["rwkv_g2"], f32)
    vec = np.zeros((L, 128, cfg.NV), f32)
    vo = cfg.vec
    for l in range(L):
        def put(name, arr):
            a = colvec(arr)
            vec[l, :, vo[name]:vo[name] + a.shape[1]] = a
        put("mixn", inp["mix_norm"][l]); put("ffnn", inp["ffn_norm"][l]); put("plen", inp["ple_norm"][l])
        cw = np.asarray(inp["lru_conv_w"][l], f32)
        put("lru_cw", cw.reshape(-1))
        put("lru_cb", inp["lru_conv_b"][l]); put("lru_br", inp["lru_br"][l]); put("lru_bi", inp["lru_bi"][l])
        put("lru_lam", inp["lru_lambda"][l])
        put("fox_qg", inp["fox_q_gain"][l]); put("fox_kg", inp["fox_k_gain"][l])
        sg = np.zeros(128, f32); sg[:cfg.FH] = -1.0; sg[cfg.FH:cfg.FH + cfg.SH] = 1.0
        put("sm_sign", sg)
        sbias = np.zeros(128, f32); sbias[:cfg.FH] = np.asarray(inp["fox_fbias"][l]); sbias[cfg.FH:cfg.FH + cfg.SH] = np.asarray(inp["ssd_dt_bias"][l])
        put("sm_bias", sbias)
        al = np.zeros(128, f32); al[cfg.FH:cfg.FH + cfg.SH] = np.asarray(inp["ssd_a_log"][l])
        put("ssd_alog", al)
        put("ssd_cw", np.asarray(inp["ssd_conv_w"][l], f32).reshape(-1)); put("ssd_cb", inp["ssd_conv_b"][l])
        put("ssd_d", np.repeat(np.asarray(inp["ssd_d"][l], f32), 64)); put("ssd_nw", inp["ssd_norm"][l])
        put("rw_mu", inp["rwkv_mu"][l]); put("rw_w0", inp["rwkv_w0"][l]); put("rw_a0", inp["rwkv_a0"][l])
        put("rw_kk", inp["rwkv_kk"][l]); put("rw_ka", inp["rwkv_ka"][l]); put("rw_rk", np.asarray(inp["rwkv_rk"][l], f32).reshape(-1))
        put("rw_lnw", inp["rwkv_ln_w"][l]); put("rw_lnb", inp["rwkv_ln_b"][l])
        put("ffn_cw", np.asarray(inp["ffn_conv_w"][l], f32).reshape(-1))
        put("ffn_cb", inp["ffn_conv_b"][l])
    m["vec"] = vec
    return m


class Ring:
    def __init__(self, items):
        self.items = items
        self.i = 0

    def next(self):
        v = self.items[self.i]
        self.i = (self.i + 1) % len(self.items)
        return v


def build(cfg, upto="all"):
    nc = bass.Bass("TRN2", target_bir_lowering=False)
    k = K(nc)
    D, T, L, BW, BC, G, KC, FB, NT, NTT = cfg.D, cfg.T, cfg.L, cfg.BW, cfg.BC, cfg.G, cfg.KC, cfg.FB, cfg.NT, cfg.NTT
    FH, SH, RH = cfg.FH, cfg.SH, cfg.RH
    off, vo, co = cfg.off, cfg.vec, cfg.cst
    NBLK = cfg.NBLK
    EI = "ExternalInput"
    xT = k.dram("xT", [D, T], kind=EI)
    pT = k.dram("pT", [L, 256, T], kind=EI)
    cstd = k.dram("cst", [128, cfg.NCST], kind=EI)
    wmix = k.dram("wmix", [L, NBLK, 128, KC * 128], kind=EI)
    wgate = k.dram("wgate", [L, KC * 4, 128, KC * 128], kind=EI)
    wbr = k.dram("wbr", [L, KC, 128, 4 * BC * 128], kind=EI)
    wout = k.dram("wout", [L, KC, 128, KC * 128], kind=EI)
    wup = k.dram("wup", [L, 2 * FB, 128, KC * 128], kind=EI)
    wdown = k.dram("wdown", [L, KC, 128, FB * 128], kind=EI)
    wpg = k.dram("wpg", [L, KC, 128, KC * 128], kind=EI)
    wple = k.dram("wple", [L, KC, 128, 2 * 128], kind=EI)
    lrubd = k.dram("lrubd", [L, 2, BC, 128, 128], kind=EI)
    rwwa2 = k.dram("rwwa2", [L, 128, BW], kind=EI)
    rwg2 = k.dram("rwg2", [L, 128, BW], kind=EI)
    vecd = k.dram("vec", [L, 128, cfg.NV], kind=EI)
    outd = k.dram("out", [D, T], kind="ExternalOutput")
    sk = "ExternalOutput" if cfg.dbg else "Internal"
    Hd = k.dram("H", [D, T], kind=sk)
    PMd = k.dram("PM", [NBLK * 128, T], kind=sk)
    Yd = k.dram("Y", [4 * BW, T], kind=sk)
    CSD = k.dram("CSD", [128, T])
    SPD = k.dram("SPD", [128, T])

    def tview(base, name, n):
        return [V(base.ap, [Buf("%s%d" % (name, i), None)]) for i in range(n)]
    Htt = tview(Hd, "Htt", NTT)
    Ott = tview(outd, "Ott", NTT)
    PMb = tview(PMd, "PMb", NBLK)
    Yb = tview(Yd, "Yb", 4 * BC)

    cst = k.sb("cst_sb", [128, cfg.NCST])
    k.dma("sp", cst, cstd)
    ones_r = k.sb("ones_r", [128, 128], F32R)
    k.dma("pool", ones_r, cstd[:, co["ones"]:co["ones"] + 128])
    vec = k.sb("vec_sb", [128, cfg.NV])
    ident = cst[:, co["ident"]:co["ident"] + 128]
    ones = cst[:, co["ones"]:co["ones"] + 128]
    mneg = cst[:, co["mneg"]:co["mneg"] + 128]
    PSb = [k.ps("ps%d" % i, [128, 512]) for i in range(8)]
    PS = Ring(PSb)
    evac_flip = [0]

    def evac(out, ps):
        evac_flip[0] ^= 1
        if evac_flip[0]:
            k.actv(out, ps, AF.Copy)
        else:
            k.copy(out, ps)

    def vc(name, i=0, n=1):
        return vec[:, vo[name] + i: vo[name] + i + n]

    def wring_make(n, width):
        return Ring([k.sb("wr%d" % i, [128, width], F32R) for i in range(n)])

    def rmsnorm(h, xn, wname, rs):
        k.actv(xn, h, AF.Square)
        ps = PS.next()
        for c in range(KC):
            k.mm(ps, ones_r, xn[:, c, :], start=(c == 0), stop=(c == KC - 1), signal=(c == KC - 1))
        k.actv(rs, ps, AF.Sqrt, bias=epsD, scale=1.0 / D)
        k.recip(rs, rs)
        for c in range(KC):
            k.stt(xn[:, c, :], h[:, c, :], vc(wname, c), rs, ALU.mult, ALU.mult)

    epsD = k.sb("epsD", [128, 1])
    k.memset(epsD, EPS)
    eps128 = epsD
    lneps = k.sb("lneps", [128, 1])
    k.memset(lneps, RW_LN_EPS)
    WMAX = max(KC, 4 * BC, cfg.FGS) * 128

    for l in range(L):
        k.dma("sp", vec, vecd[l])
        hsrc = xT if l == 0 else Hd
        hsrc_t = [V(xT.ap, xT.bufs)] * NTT if l == 0 else Htt
        hdst = outd if l == L - 1 else Hd
        hdst_t = Ott if l == L - 1 else Htt

        with k.scope():
            wring = wring_make(6, WMAX)
            h = k.sb("A_h", [128, KC, NT])
            xn = k.sb("A_xn", [128, KC, NT], F32R)
            rs = k.sb("A_rs", [128, NT])
            stage = Ring([k.sb("A_st%d" % i, [128, NT]) for i in range(4)])
            for tt in range(NTT):
                tsl = slice(tt * NT, (tt + 1) * NT)
                k.dma("sp", h, V(hsrc.ap[:, tsl].rearrange("(c p) t -> p c t", p=128), hsrc_t[tt].bufs))
                rmsnorm(h, xn, "mixn", rs)
                for blk in range(NBLK):
                    wt = wring.next()
                    k.dma("pool", wt[:, :KC * 128], wmix[l, blk])
                    ps = PS.next()
                    for c in range(KC):
                        k.mm(ps, wt[:, c * 128:(c + 1) * 128], xn[:, c, :], start=(c == 0), stop=(c == KC - 1),
                             signal=(c == KC - 1))
                    st = stage.next()
                    evac(st, ps)
                    k.dma("act", V(PMd.ap[blk * 128:(blk + 1) * 128, tsl], PMb[blk].bufs), st)
        if upto == "A":
            break

        with k.scope():
            sm = k.sb("B0_sm", [128, T])
            zt = k.sb("B0_zt", [128, T])
            ones512 = k.sb("B0_ones", [128, 512])
            sb2 = k.sb("B0_b2", [128, 1])
            k.memset(ones512, 1.0)
            k.tt(sb2, vc("sm_bias"), vc("sm_sign"), ALU.mult)
            k.dma("sp", sm, V(PMd.ap[off["small"] * 128:(off["small"] + 1) * 128, :], PMb[off["small"]].bufs))
            k.actv(zt, sm, AF.Identity, bias=sb2, scale=vc("sm_sign"))
            k.actv(sm, zt, AF.Abs)
            k.actv(sm, sm, AF.Exp, scale=-1.0)
            k.actv(sm, sm, AF.Ln, bias=1.0)
            k.ts(zt, zt, 0.0, None, ALU.max)
            k.tt(zt, zt, sm, ALU.add)
            for s in range(T // 512):
                init = 0.0 if s == 0 else sm[:, s * 512 - 1:s * 512]
                k.scan(sm[:, s * 512:(s + 1) * 512], ones512, zt[:, s * 512:(s + 1) * 512], init, ALU.mult, ALU.add)
            k.dma("act", CSD, sm)
            k.dma("act", SPD, zt)

        with k.scope():
            TB = min(T, 2048)
            xin = k.sb("L_xin", [128, 3 + TB])
            gin = k.sb("L_gin", [128, TB])
            xc = k.sb("L_xc", [128, TB])
            rt = k.sb("L_rt", [128, TB])
            it = k.sb("L_it", [128, TB])
            at = k.sb("L_at", [128, TB])
            tmp = k.sb("L_tmp", [128, TB])
            hs = k.sb("L_hs", [128, TB])
            xcar = k.sb("L_xcar", [128, 3])
            hcar = k.sb("L_hcar", [128, 1])
            ca = k.sb("L_ca", [128, BC])
            bdr = k.sb("L_bdr", [128, BC, 128])
            bdi = k.sb("L_bdi", [128, BC, 128])
            k.dma("sp", bdr, lrubd[l, 0].re("c p m -> p c m"))
            k.dma("sp", bdi, lrubd[l, 1].re("c p m -> p c m"))
            k.actv(ca, vc("lru_lam", 0, BC), AF.Exp, scale=-1.0)
            k.actv(ca, ca, AF.Ln, bias=1.0)
            k.ts(ca, ca, -8.0, None, ALU.mult)
            for cc in range(BC):
                for tb in range(T // TB):
                    tsl = slice(tb * TB, (tb + 1) * TB)
                    if tb == 0:
                        k.memset(xin[:, 0:3], 0.0, eng="dve")
                    else:
                        k.copy(xin[:, 0:3], xcar)
                    k.dma("sp", xin[:, 3:], V(PMd.ap[(off["ax"] + cc) * 128:(off["ax"] + cc + 1) * 128, tsl], PMb[off["ax"] + cc].bufs))
                    k.dma("sp", gin, V(PMd.ap[(off["ag"] + cc) * 128:(off["ag"] + cc + 1) * 128, tsl], PMb[off["ag"] + cc].bufs))
                    k.ts(xc, xin[:, 3:3 + TB], vc("lru_cw", 3 * BC + cc), vc("lru_cb", cc), ALU.mult, ALU.add)
                    for tap in range(3):
                        k.stt(xc, xin[:, tap:tap + TB], vc("lru_cw", tap * BC + cc), xc, ALU.mult, ALU.add)
                    k.copy(xcar, xin[:, TB:TB + 3])
                    for s in range(TB // 512):
                        ssl = slice(s * 512, (s + 1) * 512)
                        ps = PS.next()
                        k.mm(ps, bdr[:, cc, :], xc[:, ssl])
                        k.actv(rt[:, ssl], ps, AF.Sigmoid, bias=vc("lru_br", cc))
                        ps = PS.next()
                        k.mm(ps, bdi[:, cc, :], xc[:, ssl])
                        k.actv(it[:, ssl], ps, AF.Sigmoid, bias=vc("lru_bi", cc))
                    k.actv(at, rt, AF.Exp, scale=ca[:, cc:cc + 1])
                    k.tt(tmp, at, at, ALU.mult)
                    k.ts(tmp, tmp, -1.0, 1.0, ALU.mult, ALU.add)
                    k.actv(tmp, tmp, AF.Sqrt)
                    k.tt(it, it, xc, ALU.mult)
                    k.tt(it, it, tmp, ALU.mult)
                    k.scan(hs, at, it, 0.0 if tb == 0 else hcar, ALU.mult, ALU.add)
                    k.copy(hcar, hs[:, TB - 1:TB])
                    k.actv(gin, gin, AF.Gelu_apprx_tanh)
                    k.tt(hs, hs, gin, ALU.mult)
                    k.dma("act", V(Yd.ap[cc * 128:(cc + 1) * 128, tsl], Yb[cc].bufs), hs)

        NKC = T // 128
        for hh in range(FH):
            with k.scope():
                qn = k.sb("F_qn", [128, T], F32R)
                kn = k.sb("F_kn", [128, T], F32R)
                vtm = k.sb("F_vtm", [128, NKC, 128], F32R)
                raw = Ring([k.sb("F_raw%d" % i, [128, 512]) for i in range(2)])
                sq = k.sb("F_sq", [128, 512], F32R)
                rs = k.sb("F_rs", [128, 512])
                ckp = k.sb("F_ckp", [128, NKC])
                cref = k.sb("F_cref", [128, NKC])
                bias = k.sb("F_bias", [128, NKC, NKC])
                pts = Ring([k.sb("F_pt%d" % i, [128, 512], F32R) for i in range(3)])
                tm = k.sb("F_tm", [128, 128])
                osb = k.sb("F_osb", [128, 512])
                rd = k.sb("F_rd", [128, 512])
                for (src, dst, gname) in ((off["bq"], qn, "fox_qg"), (off["bk"], kn, "fox_kg")):
                    for s in range(T // 512):
                        ssl = slice(s * 512, (s + 1) * 512)
                        r_ = raw.next()
                        k.dma("sp", r_, V(PMd.ap[(src + hh) * 128:(src + hh + 1) * 128, ssl], PMb[src + hh].bufs))
                        k.actv(sq, r_, AF.Square)
                        ps = PS.next()
                        k.mm(ps, ones_r, sq)
                        k.actv(rs, ps, AF.Sqrt, bias=eps128, scale=1.0 / 128)
                        k.recip(rs, rs)
                        k.stt(dst[:, ssl], r_, vc(gname), rs, ALU.mult, ALU.mult)
                for s in range(T // 512):
                    ssl = slice(s * 512, (s + 1) * 512)
                    r_ = raw.next()
                    k.dma("sp", r_, V(PMd.ap[(off["bv"] + hh) * 128:(off["bv"] + hh + 1) * 128, ssl], PMb[off["bv"] + hh].bufs))
                    ps = PS.next()
                    for j in range(4):
                        k.tr(ps[:, j * 128:(j + 1) * 128], r_[:, j * 128:(j + 1) * 128], ident)
                    evac(vtm[:, s * 4:(s + 1) * 4, :], ps.re("p (j d) -> p j d", j=4))
                k.dma("sp", ckp, V(CSD.ap[hh:hh + 1, :].rearrange("o (j p) -> p (o j)", p=128), CSD.bufs),
                      allow_slow_non_contiguous=True)
                k.memset(cref[:, 0:1], 0.0, eng="dve")
                if NKC > 1:
                    k.dma("sp", cref[:, 1:NKC],
                          V(CSD.ap[hh:hh + 1, 127:T - 1:128].partition_broadcast(128).rearrange("p o j -> p (o j)"), CSD.bufs),
                          allow_slow_non_contiguous=True)
                for qs in range(NKC):
                    k.ts(bias[:, qs, 0:qs + 1], ckp[:, 0:qs + 1], cref[:, qs:qs + 1], None, ALU.subtract)
                scale = 128 ** -0.5
                PS47 = Ring(PSb[4:])
                for qt in range(T // 512):
                    qsl = slice(qt * 512, (qt + 1) * 512)
                    nk = (qt + 1) * 4
                    ops_ = PSb[(qt % 2) * 2]
                    dps = PSb[(qt % 2) * 2 + 1]
                    for j in range(nk):
                        sps = PS47.next()
                        k.mm(sps, kn[:, j * 128:(j + 1) * 128], qn[:, qsl])
                        pt = pts.next()
                        jr = j - qt * 4
                        for qq in range(4):
                            csl = slice(qq * 128, (qq + 1) * 128)
                            qs = qt * 4 + qq
                            if jr > qq:
                                k.memset(pt[:, csl].f(), 0.0, eng="dve")
                            elif jr == qq:
                                k.tt(tm, sps[:, csl], mneg, ALU.add)
                                k.actv(pt[:, csl], tm, AF.Exp, bias=bias[:, qs, j:j + 1], scale=scale)
                            else:
                                k.actv(pt[:, csl], sps[:, csl], AF.Exp, bias=bias[:, qs, j:j + 1], scale=scale)
                        k.mm(ops_, vtm[:, j, :], pt, start=(j == 0), stop=(j == nk - 1), signal=False)
                        k.mm(dps, ones_r, pt, start=(j == 0), stop=(j == nk - 1), signal=True)
                    k.recip(rd, dps)
                    k.tt(osb, ops_, rd, ALU.mult)
                    k.dma("act", V(Yd.ap[BW + hh * 128:BW + (hh + 1) * 128, qsl], Yb[BC + hh].bufs), osb)
        if upto == "B2":
            break

        NXBC = cfg.NXBC
        HPG = SH // G
        with k.scope():
            W = 512
            xin = k.sb("S_xin", [128, NXBC, 3 + W])
            xc = k.sb("S_xc", [128, NXBC, W])
            xcar = k.sb("S_xcar", [128, NXBC, 3])
            zt = k.sb("S_z", [128, BC, W])
            spw = k.sb("S_spw", [128, W])
            dan = k.sb("S_dan", [128, W])
            ac = k.sb("S_ac", [128, W])
            aneg = k.sb("S_aneg", [128, 1])
            atm = k.sb("S_atm", [128, 2, 128])
            xtm = k.sb("S_xtm", [128, BC * 128])
            xdt = k.sb("S_xdt", [128, SH, 64])
            btm = k.sb("S_btm", [128, G, 128])
            gts = k.sb("S_gts", [128, G, 128])
            rbs = k.sb("S_rbs", [128, 128])
            tmpa = k.sb("S_tmpa", [128, 128])
            dm = k.sb("S_dm", [128, 128])
            wts = k.sb("S_wts", [128, 128])
            es = k.sb("S_es", [128, 128])
            cps = k.sb("S_cps", [128, 128])
            naL = k.sb("S_naL", [128, 1])
            ds = k.sb("S_ds", [128, 1])
            bps = k.sb("S_bps", [128, 128])
            stt_ = [k.sb("S_state%d" % i, [128, 64]) for i in range(SH)]
            yw = k.sb("S_yw", [128, BC, W])
            sqr = k.sb("S_sqr", [128, W], F32R)
            bo64r = k.sb("S_bo64r", [128, 128], F32R)
            rsd = k.sb("S_rsd", [128, W])
            k.dma("pool", bo64r, cstd[:, co["bo64"]:co["bo64"] + 128])
            k.actv(aneg, vc("ssd_alog"), AF.Exp)
            for s_ in stt_:
                k.memset(s_, 0.0, eng="dve")
            PS47 = Ring(PSb[4:])
            for w_ in range(T // W):
                wsl = slice(w_ * W, (w_ + 1) * W)
                if w_ == 0:
                    k.memset(xin[:, :, 0:3], 0.0, eng="dve")
                else:
                    k.copy(xin[:, :, 0:3], xcar)
                for b in range(NXBC):
                    k.dma("sp", xin[:, b, 3:], V(PMd.ap[(off["cx"] + b) * 128:(off["cx"] + b + 1) * 128, wsl], PMb[off["cx"] + b].bufs))
                k.dma("sp", zt, V(PMd.ap[off["cz"] * 128:(off["cz"] + BC) * 128, wsl].rearrange("(c p) t -> p c t", p=128),
                                  sum([PMb[off["cz"] + b].bufs for b in range(BC)], [])))
                k.dma("sp", spw, SPD[:, wsl])
                for b in range(NXBC):
                    k.ts(xc[:, b, :], xin[:, b, 3:3 + W], vc("ssd_cw", 3 * NXBC + b), vc("ssd_cb", b), ALU.mult, ALU.add)
                    for tap in range(3):
                        k.stt(xc[:, b, :], xin[:, b, tap:tap + W], vc("ssd_cw", tap * NXBC + b), xc[:, b, :], ALU.mult, ALU.add)
                k.copy(xcar, xin[:, :, W:W + 3])
                k.actv(xc, xc, AF.Silu)
                k.ts(dan, spw, aneg, None, ALU.mult)
                k.scan(ac, cst[:, co["reset128"]:co["reset128"] + W], dan, 0.0, ALU.mult, ALU.add)
                for c in range(W // 128):
                    csl = slice(c * 128, (c + 1) * 128)
                    ps = PS47.next()
                    k.tr(ps[:, 0:128], ac[:, csl], ident)
                    k.tr(ps[:, 128:256], spw[:, csl], ident)
                    evac(atm, ps[:, 0:256].re("p (a b) -> p a b", a=2))
                    ps = PS47.next()
                    for b in range(BC):
                        k.tr(ps[:, b * 128:(b + 1) * 128], xc[:, b, csl], ident)
                    evac(xtm, ps[:, 0:BC * 128])
                    k.tt(xdt, xtm.re("p (h d) -> p h d", d=64),
                         V(atm.ap[:, 1, FH:FH + SH].unsqueeze(2).to_broadcast([128, SH, 64]), atm.bufs), ALU.mult)
                    ps = PS47.next()
                    for g in range(G):
                        k.tr(ps[:, g * 128:(g + 1) * 128], xc[:, BC + g, csl], ident)
                    evac(btm, ps[:, 0:G * 128].re("p (g n) -> p g n", g=G))
                    ps = PS47.next()
                    for g in range(G):
                        k.mm(ps[:, g * 128:(g + 1) * 128], xc[:, BC + g, csl], xc[:, BC + G + g, csl])
                    evac(gts, ps[:, 0:G * 128].re("p (g n) -> p g n", g=G))
                    for hd in range(SH):
                        g = hd // HPG
                        ps = PS47.next()
                        k.mm(ps[:, 0:128], cst[:, co["sel"] + hd * 128: co["sel"] + (hd + 1) * 128], ac[:, csl])
                        k.copy(rbs, ps[:, 0:128])
                        k.ts(tmpa, rbs, atm[:, 0, FH + hd:FH + hd + 1], -1.0, ALU.subtract, ALU.mult)
                        k.tt(tmpa, tmpa, mneg, ALU.add)
                        k.actv(dm, tmpa, AF.Exp)
                        k.tt(wts, gts[:, g, :], dm, ALU.mult)
                        k.actv(es, rbs, AF.Exp, scale=-1.0)
                        k.tt(cps, xc[:, BC + G + g, csl], es, ALU.mult)
                        yps = PSb[hd // 2]
                        yo = yps[(hd % 2) * 64:(hd % 2 + 1) * 64, csl]
                        k.mm(yo, xdt[:, hd, :], wts, start=True, stop=False, signal=False)
                        k.mm(yo, stt_[hd], cps, start=False, stop=True)
                        k.ts(naL, rbs[:, 127:128], -1.0, None, ALU.mult)
                        k.actv(ds, atm[:, 0, FH + hd:FH + hd + 1], AF.Exp, bias=naL)
                        k.ts(bps, btm[:, g, :], ds, None, ALU.mult)
                        ps2 = PS47.next()
                        k.mm(ps2[:, 0:64], bps, xdt[:, hd, :])
                        k.stt(stt_[hd], stt_[hd], es[:, 127:128], ps2[:, 0:64], ALU.mult, ALU.add)
                for b in range(BC):
                    k.copy(yw[:, b, :], PSb[b])
                    k.stt(yw[:, b, :], xc[:, b, :], vc("ssd_d", b), yw[:, b, :], ALU.mult, ALU.add)
                k.actv(zt, zt, AF.Silu)
                k.tt(yw, yw, zt, ALU.mult)
                if BC >= G:
                    bpg = BC // G
                    for g in range(G):
                        ps = PS47.next()
                        for j in range(bpg):
                            k.actv(sqr, yw[:, g * bpg + j, :], AF.Square)
                            k.mm(ps, ones_r, sqr, start=(j == 0), stop=(j == bpg - 1))
                        k.actv(rsd, ps, AF.Sqrt, bias=epsD, scale=1.0 / (128 * bpg))
                        k.recip(rsd, rsd)
                        for j in range(bpg):
                            b = g * bpg + j
                            k.stt(yw[:, b, :], yw[:, b, :], vc("ssd_nw", b), rsd, ALU.mult, ALU.mult)
                else:
                    assert BC == 1 and G == 2
                    ps = PS47.next()
                    k.actv(sqr, yw[:, 0, :], AF.Square)
                    k.mm(ps, bo64r, sqr)
                    k.actv(rsd, ps, AF.Sqrt, bias=epsD, scale=1.0 / 64)
                    k.recip(rsd, rsd)
                    k.stt(yw[:, 0, :], yw[:, 0, :], vc("ssd_nw", 0), rsd, ALU.mult, ALU.mult)
                for b in range(BC):
                    k.dma("act", V(Yd.ap[2 * BW + b * 128:2 * BW + (b + 1) * 128, wsl], Yb[2 * BC + b].bufs), yw[:, b, :])
        if upto == "B3":
            break

        NU = cfg.NU
        C0 = float(np.exp(-0.5))
        with k.scope():
            W = 256
            NCH = W // 64
            uin = k.sb("R_uin", [128, NU, 1 + W])
            ucar = k.sb("R_ucar", [128, NU, 1])
            dd = k.sb("R_dd", [128, NU, W])
            ul = k.sb("R_ul", [128, NU, W])
            twt = k.sb("R_twt", [128, W])
            sgt = k.sb("R_sgt", [128, W])
            wa2 = k.sb("R_wa2", [128, BW])
            g2 = k.sb("R_g2", [128, BW])
            k.dma("sp", wa2, rwwa2[l])
            k.dma("sp", g2, rwg2[l])
            bo64 = cst[:, co["bo64"]:co["bo64"] + 128]
            m_su = cst[:, co["m_su"]:co["m_su"] + 128]
            m_iu = cst[:, co["m_iu"]:co["m_iu"] + 128]
            m_sl = cst[:, co["m_sl"]:co["m_sl"] + 128]

            def per(name, shape):
                return [k.sb("R_%s%d" % (name, b), shape) for b in range(BC)]
            lwn = per("lwn", [128, W]); al = per("al", [128, W]); gg = per("gg", [128, W]); kkn = per("kkn", [128, W])
            kp = per("kp", [128, W]); gc = per("gc", [128, W]); bon = per("bon", [128, W]); yw = per("yw", [128, W])
            e1 = k.sb("R_e1", [128, W]); e2 = k.sb("R_e2", [128, W]); e3 = k.sb("R_e3", [128, W]); e4 = k.sb("R_e4", [128, W])
            t1 = k.sb("R_t1", [128, W]); t2 = k.sb("R_t2", [128, W]); qb = k.sb("R_qb", [128, W])
            gamc = per("gamc", [128, NCH])
            BDn = ("P", "Q", "K", "R", "Qh", "Kh", "V")
            BD = [{n: k.sb("R_bd%s%d" % (n, b), [128, NCH, 128]) for n in BDn} for b in range(BC)]
            for b in range(BC):
                for n in BDn:
                    k.memset(BD[b][n], 0.0, eng="dve")
            Hs = per("H", [128, 128])
            for b in range(BC):
                k.memset(Hs[b], 0.0, eng="dve")
            NTt = per("NT", [128, 128]); Nt = per("N", [128, 128]); AKT = per("AKT", [128, 128])
            RBT = per("RBT", [128, 128]); RKT = per("RKT", [128, 128])
            Vtm = per("Vtm", [128, 128]); Qhtm = per("Qhtm", [128, 128]); Khtm = per("Khtm", [128, 128])
            MT = [per("MT%d" % i, [128, 128]) for i in range(1, 6)]
            Ma = per("Ma", [128, 128]); Mb = per("Mb", [128, 128])
            Wa = per("Wa", [128, 128]); Wb = per("Wb", [128, 128])

            def bdwrite(fn, dst):
                fn(dst[0:64, :, 0:64], 0, 64)
                fn(dst[64:128, :, 64:128], 64, 128)

            def v3(x, lo, hi):
                return x[lo:hi, :].re("p (c t) -> p c t", t=64)

            for w_ in range(T // W):
                wsl = slice(w_ * W, (w_ + 1) * W)
                if w_ == 0:
                    k.memset(uin[:, :, 0:1], 0.0, eng="dve")
                else:
                    k.copy(uin[:, :, 0:1], ucar)
                k.dma("sp", uin[:, :, 1:], V(PMd.ap[off["dr"] * 128:(off["dr"] + NU) * 128, wsl].rearrange("(c p) t -> p c t", p=128),
                                             sum([PMb[off["dr"] + b].bufs for b in range(NU)], [])))
                k.tt(dd, uin[:, :, 0:W], uin[:, :, 1:1 + W], ALU.subtract)
                k.tt(dd, dd, V(vc("rw_mu", 0, NU).ap.unsqueeze(2).to_broadcast([128, NU, W]), vec.bufs), ALU.mult)
                k.tt(ul, dd, uin[:, :, 1:1 + W], ALU.add)
                k.copy(ucar, uin[:, :, W:W + 1])
                k.actv(twt[0:64, :], ul[0:64, 3 * BC, :], AF.Tanh)
                k.actv(sgt, ul[:, 3 * BC + 1, :], AF.Sigmoid)
                for b in range(BC):
                    bs = slice(b * 128, (b + 1) * 128)
                    r_, k_, v_ = ul[:, b, :], ul[:, BC + b, :], ul[:, 2 * BC + b, :]
                    ps = PS.next()
                    k.mm(ps[:, 0:W], wa2[0:64, bs], twt[0:64, :])
                    k.actv(lwn[b], ps[:, 0:W], AF.Sigmoid, bias=vc("rw_w0", b))
                    ps = PS.next()
                    k.mm(ps[:, 0:W], wa2[64:128, bs], ul[64:128, 3 * BC, :])
                    k.actv(al[b], ps[:, 0:W], AF.Sigmoid, bias=vc("rw_a0", b))
                    ps = PS.next()
                    k.mm(ps[:, 0:W], g2[:, bs], sgt)
                    k.copy(gg[b], ps[:, 0:W])
                    k.ts(t1, k_, vc("rw_kk", b), None, ALU.mult)
                    k.tt(t2, t1, t1, ALU.mult)
                    ps = PS.next()
                    k.mm(ps[:, 0:W], bo64, t2)
                    k.actv(t2, ps[:, 0:W], AF.Sqrt)
                    k.ts(t2, t2, 1e-12, None, ALU.max)
                    k.recip(t2, t2)
                    k.tt(kkn[b], t1, t2, ALU.mult)
                    k.ts(t1, al[b], -1.0, vc("rw_ka", b), ALU.add, ALU.mult)
                    k.ts(t1, t1, 1.0, None, ALU.add)
                    k.tt(kp[b], k_, t1, ALU.mult)
                    k.stt(t1, r_, vc("rw_rk", b), kp[b], ALU.mult, ALU.mult)
                    ps = PS.next()
                    k.mm(ps[:, 0:W], bo64, t1)
                    k.tt(bon[b], ps[:, 0:W], v_, ALU.mult)
                    k.scan(gc[b], cst[:, co["reset64"]:co["reset64"] + W], lwn[b], 0.0, ALU.mult, ALU.add)
                    k.actv(e1, gc[b], AF.Exp, scale=-C0)
                    k.actv(e2, gc[b], AF.Exp, scale=C0)
                    k.tt(t1, gc[b], lwn[b], ALU.subtract)
                    k.actv(e3, t1, AF.Exp, scale=-C0)
                    g3 = gc[b].re("p (c t) -> p c t", t=64)
                    k.tt(t1.re("p (c t) -> p c t", t=64), V(g3.ap[:, :, 63:64].to_broadcast([128, NCH, 64]), gc[b].bufs), g3, ALU.subtract)
                    k.actv(e4, t1, AF.Exp, scale=-C0)
                    k.copy(gamc[b], e1.re("p (c t) -> p c t", t=64)[:, :, 63])
                    k.tt(qb, kkn[b], al[b], ALU.mult)
                    bd = BD[b]
                    bdwrite(lambda o, lo, hi: k.stt(o, v3(kkn[b], lo, hi), -1.0, v3(e3, lo, hi), ALU.mult, ALU.mult), bd["P"])
                    bdwrite(lambda o, lo, hi: k.tt(o, v3(qb, lo, hi), v3(e2, lo, hi), ALU.mult), bd["Q"])
                    bdwrite(lambda o, lo, hi: k.tt(o, v3(kp[b], lo, hi), v3(e2, lo, hi), ALU.mult), bd["K"])
                    bdwrite(lambda o, lo, hi: k.tt(o, v3(r_, lo, hi), v3(e1, lo, hi), ALU.mult), bd["R"])
                    bdwrite(lambda o, lo, hi: k.tt(o, v3(qb, lo, hi), v3(e4, lo, hi), ALU.mult), bd["Qh"])
                    bdwrite(lambda o, lo, hi: k.tt(o, v3(kp[b], lo, hi), v3(e4, lo, hi), ALU.mult), bd["Kh"])
                    bdwrite(lambda o, lo, hi: k.copy(o, v3(v_, lo, hi)), bd["V"])
                for c in range(NCH):
                    Q = 128

                    def pq():
                        return PS.next()[:, 0:Q]
                    for b in range(BC):
                        bd = BD[b]
                        for (dst, lt, rh, msk) in ((NTt[b], "Q", "P", m_su), (Nt[b], "P", "Q", m_sl), (AKT[b], "K", "P", m_su),
                                                   (RBT[b], "Q", "R", m_iu), (RKT[b], "K", "R", m_iu)):
                            ps = pq()
                            k.mm(ps, bd[lt][:, c, :], bd[rh][:, c, :])
                            k.tt(dst, ps, msk, ALU.mult)
                    for b in range(BC):
                        bd = BD[b]
                        for (dst, src) in ((Vtm[b], "V"), (Qhtm[b], "Qh"), (Khtm[b], "Kh")):
                            ps = pq()
                            k.tr(ps, bd[src][:, c, :], ident)
                            evac(dst, ps)
                    for lev in range(1, 6):
                        for b in range(BC):
                            mt_prev = NTt[b] if lev == 1 else MT[lev - 2][b]
                            m_prev = Nt[b] if lev == 1 else (Ma[b] if lev % 2 == 0 else Mb[b])
                            m_new = Ma[b] if lev % 2 == 1 else Mb[b]
                            ps = pq()
                            k.mm(ps, m_prev, mt_prev)
                            evac(MT[lev - 1][b], ps)
                            if lev < 5:
                                ps = pq()
                                k.mm(ps, mt_prev, m_prev)
                                evac(m_new, ps)
                    for b in range(BC):
                        bd = BD[b]
                        ps = pq()
                        k.mm(ps, bd["P"][:, c, :], Hs[b], start=True, stop=False, signal=False)
                        k.mm(ps, AKT[b], Vtm[b], start=False, stop=True)
                        evac(Wa[b], ps)
                    for lev in range(6):
                        for b in range(BC):
                            mt = NTt[b] if lev == 0 else MT[lev - 1][b]
                            wc = Wa[b] if lev % 2 == 0 else Wb[b]
                            wn = Wb[b] if lev % 2 == 0 else Wa[b]
                            ps = pq()
                            k.mm(ps, mt, wc)
                            k.tt(wn, wc, ps, ALU.add)
                    for b in range(BC):
                        bd = BD[b]
                        ps = pq()
                        k.mm(ps, Hs[b], bd["R"][:, c, :], start=True, stop=False, signal=False)
                        k.mm(ps, Wa[b], RBT[b], start=False, stop=False, signal=False)
                        k.mm(ps, Vtm[b], RKT[b], start=False, stop=True)
                        k.actv(yw[b][0:64, c * 64:(c + 1) * 64], ps[0:64, 0:64], AF.Copy)
                        k.copy(yw[b][64:128, c * 64:(c + 1) * 64], ps[64:128, 64:128])
                    for b in range(BC):
                        ps = pq()
                        k.mm(ps, Qhtm[b], Wa[b], start=True, stop=False, signal=False)
                        k.mm(ps, Khtm[b], Vtm[b], start=False, stop=True)
                        k.stt(Hs[b], Hs[b], gamc[b][:, c:c + 1], ps, ALU.mult, ALU.add)
                for b in range(BC):
                    ps = PS.next()
                    k.mm(ps[:, 0:W], bo64, yw[b])
                    k.stt(t1, ps[:, 0:W], -1.0 / 64, yw[b], ALU.mult, ALU.add)
                    k.tt(t2, t1, t1, ALU.mult)
                    ps = PS.next()
                    k.mm(ps[:, 0:W], bo64, t2)
                    k.actv(t2, ps[:, 0:W], AF.Sqrt, bias=lneps, scale=1.0 / 64)
                    k.recip(t2, t2)
                    k.tt(t1, t1, t2, ALU.mult)
                    k.ts(t1, t1, vc("rw_lnw", b), vc("rw_lnb", b), ALU.mult, ALU.add)
                    k.tt(t1, t1, bon[b], ALU.add)
                    k.tt(t1, t1, gg[b], ALU.mult)
                    k.dma("act", V(Yd.ap[3 * BW + b * 128:3 * BW + (b + 1) * 128, wsl], Yb[3 * BC + b].bufs), t1)
        if upto == "B4":
            break

        FGS = cfg.FGS
        with k.scope():
            wring = wring_make(4, WMAX)
            h = k.sb("C_h", [128, KC, NT])
            xn = k.sb("C_xn", [128, KC, NT], F32R)
            ybuf = k.sb("C_y", [128, max(KC, FGS), NT], F32R)
            mg = k.sb("C_mg", [128, KC, NT], F32R)
            rs = k.sb("C_rs", [128, NT])
            sigs = Ring([k.sb("C_sig%d" % i, [128, NT]) for i in range(2)])
            tmps = Ring([k.sb("C_tmp%d" % i, [128, NT]) for i in range(2)])
            mgacc = k.sb("C_mgacc", [128, NT])
            ptile = k.sb("C_pt", [128, 2, NT], F32R)
            wbring = Ring([k.sb("C_wb%d" % i, [128, BC * 128], F32R) for i in range(2)])
            wpr = Ring([k.sb("C_wple%d" % i, [128, 256], F32R) for i in range(2)])
            ucar = k.sb("C_ucar", [128, 2 * FB, 2])
            ust = [k.sb("C_ust%d" % i, [128, 2 + NT]) for i in range(2)]
            ucv = [k.sb("C_ucv%d" % i, [128, NT]) for i in range(2)]
            for tt in range(NTT):
                tsl = slice(tt * NT, (tt + 1) * NT)
                k.dma("sp", h, V(hsrc.ap[:, tsl].rearrange("(c p) t -> p c t", p=128), hsrc_t[tt].bufs))
                k.dma("pool", ybuf[:, 0:KC, :], V(Yd.ap[:, tsl].rearrange("(c p) t -> p c t", p=128), sum([y_.bufs for y_ in Yb], [])))
                k.dma("pool", ptile, pT[l][:, tsl].re("(c p) t -> p c t", p=128))
                rmsnorm(h, xn, "mixn", rs)
                for dc in range(KC):
                    for j in range(4):
                        wb = wbring.next()
                        k.dma("pool", wb, wbr[l, dc][:, j * BC * 128:(j + 1) * BC * 128])
                        wg = wring.next()
                        k.dma("pool", wg[:, :KC * 128], wgate[l, dc * 4 + j])
                        gps = PS.next()
                        for c in range(KC):
                            k.mm(gps, wg[:, c * 128:(c + 1) * 128], xn[:, c, :], start=(c == 0), stop=(c == KC - 1), signal=(c == KC - 1))
                        zps = PS.next()
                        for c in range(BC):
                            k.mm(zps, wb[:, c * 128:(c + 1) * 128], ybuf[:, j * BC + c, :], start=(c == 0), stop=(c == BC - 1),
                                 signal=(c == BC - 1))
                        sg_ = sigs.next()
                        k.actv(sg_, gps, AF.Sigmoid)
                        if j == 0:
                            k.tt(mgacc, sg_, zps, ALU.mult)
                        else:
                            tp = tmps.next()
                            k.tt(tp, sg_, zps, ALU.mult)
                            k.tt(mg[:, dc, :] if j == 3 else mgacc, mgacc, tp, ALU.add)
                for dc in range(KC):
                    w = wring.next()
                    k.dma("pool", w[:, :KC * 128], wout[l, dc])
                    ps = PS.next()
                    for c in range(KC):
                        k.mm(ps, w[:, c * 128:(c + 1) * 128], mg[:, c, :], start=(c == 0), stop=(c == KC - 1), signal=(c == KC - 1))
                    k.tt(h[:, dc, :], h[:, dc, :], ps, ALU.add)
                rmsnorm(h, xn, "ffnn", rs)
                for fg in range(FB // FGS):
                    for fi in range(FGS):
                        fb = fg * FGS + fi
                        for half, blk in ((0, fb), (1, FB + fb)):
                            w = wring.next()
                            k.dma("pool", w[:, :KC * 128], wup[l, blk])
                            ps = PS.next()
                            for c in range(KC):
                                k.mm(ps, w[:, c * 128:(c + 1) * 128], xn[:, c, :], start=(c == 0), stop=(c == KC - 1), signal=(c == KC - 1))
                            u_ = ust[half]
                            if tt == 0:
                                k.memset(u_[:, 0:2], 0.0, eng="dve")
                            else:
                                k.copy(u_[:, 0:2], ucar[:, blk, :])
                            evac(u_[:, 2:2 + NT], ps)
                            k.copy(ucar[:, blk, :], u_[:, NT:NT + 2])
                            uc = ucv[half]
                            k.ts(uc, u_[:, 2:2 + NT], vc("ffn_cw", 2 * 2 * FB + blk), vc("ffn_cb", blk), ALU.mult, ALU.add)
                            k.stt(uc, u_[:, 1:1 + NT], vc("ffn_cw", 1 * 2 * FB + blk), uc, ALU.mult, ALU.add)
                            k.stt(uc, u_[:, 0:NT], vc("ffn_cw", 0 * 2 * FB + blk), uc, ALU.mult, ALU.add)
                        k.actv(ucv[0], ucv[0], AF.Gelu_apprx_tanh)
                        k.tt(ybuf[:, fi, :], ucv[0], ucv[1], ALU.mult)
                    for dc in range(KC):
                        w = wring.next()
                        k.dma("pool", w[:, :FGS * 128], wdown[l, dc][:, fg * FGS * 128:(fg + 1) * FGS * 128])
                        ps = PS.next()
                        for fi in range(FGS):
                            k.mm(ps, w[:, fi * 128:(fi + 1) * 128], ybuf[:, fi, :], start=(fi == 0), stop=(fi == FGS - 1), signal=(fi == FGS - 1))
                        k.tt(h[:, dc, :], h[:, dc, :], ps, ALU.add)
                rmsnorm(h, xn, "plen", rs)
                for dc in range(KC):
                    w = wring.next()
                    k.dma("pool", w[:, :KC * 128], wpg[l, dc])
                    gps = PS.next()
                    for c in range(KC):
                        k.mm(gps, w[:, c * 128:(c + 1) * 128], xn[:, c, :], start=(c == 0), stop=(c == KC - 1), signal=(c == KC - 1))
                    eps_ = PS.next()
                    wp = wpr.next()
                    k.dma("pool", wp, wple[l, dc])
                    for c in range(2):
                        k.mm(eps_, wp[:, c * 128:(c + 1) * 128], ptile[:, c, :], start=(c == 0), stop=(c == 1), signal=(c == 1))
                    sg_ = sigs.next()
                    k.actv(sg_, gps, AF.Sigmoid)
                    tp = tmps.next()
                    k.tt(tp, sg_, eps_, ALU.mult)
                    k.tt(h[:, dc, :], h[:, dc, :], tp, ALU.add)
                k.dma("act", V(hdst.ap[:, tsl].rearrange("(c p) t -> p c t", p=128), hdst_t[tt].bufs), h)
    k.finish()
    return nc


FULL = Cfg(D=2048, T=8192, L=4, DFF=5632, FGS=11)
_cache = {}


def kernel(**inputs):
    cfg = FULL
    B = inputs["x"].shape[0]
    if "nc" not in _cache:
        _cache["nc"] = build(cfg)
    nc = _cache["nc"]
    in_maps = [host_pack(cfg, inputs, c % B) for c in range(8)]
    res = run_bass_kernel_spmd(nc, in_maps, core_ids=list(range(8)))
    out = np.stack([np.ascontiguousarray(res.results[b]["out"].T) for b in range(B)], axis=0)
    return out.astype(np.float32)
```

```python
import contextlib
import numpy as np
import concourse.bass as bass
import concourse.mybir as mybir

F32 = mybir.dt.float32
F32R = mybir.dt.float32r
AF = mybir.ActivationFunctionType
ALU = mybir.AluOpType
AX = mybir.AxisListType


class Buf:
    __slots__ = ("name", "t", "w", "r")

    def __init__(self, name, t):
        self.name = name
        self.t = t
        self.w = []
        self.r = []


def _box(ap):
    a = ap.ap
    if a[0][0] <= 0:
        return (0, 1 << 30, 0, 1 << 40)
    pitch = a[0][0]
    off = ap.offset
    p0 = off // pitch
    e0 = off % pitch
    ext = 0
    for st, cnt in a[1:]:
        ext += abs(st) * (cnt - 1)
    return (p0, p0 + a[0][1], e0, e0 + ext + 1)


def _ovl(a, b):
    return a[0] < b[1] and b[0] < a[1] and a[2] < b[3] and b[2] < a[3]


def _inside(a, b):
    return a[0] >= b[0] and a[1] <= b[1] and a[2] >= b[2] and a[3] <= b[3]


def _collapse(lst):
    d = {}
    bb = None
    for bx, (sm, v) in lst:
        if d.get(sm, 0) < v:
            d[sm] = v
        bb = bx if bb is None else (min(bb[0], bx[0]), max(bb[1], bx[1]), min(bb[2], bx[2]), max(bb[3], bx[3]))
    return [(bb, (sm, v)) for sm, v in d.items()]


class V:
    __slots__ = ("ap", "bufs", "_bx")

    def __init__(self, ap, bufs):
        self.ap = ap
        self.bufs = bufs
        self._bx = None

    @property
    def box(self):
        if self._bx is None:
            self._bx = _box(self.ap)
        return self._bx

    def __getitem__(self, k):
        return V(self.ap[k], self.bufs)

    def r(self):
        return V(self.ap.bitcast(F32R), self.bufs)

    def f(self):
        return V(self.ap.bitcast(F32), self.bufs)

    def re(self, s, **kw):
        return V(self.ap.rearrange(s, **kw), self.bufs)

    def bc(self, shape):
        return V(self.ap.to_broadcast(shape), self.bufs)


class K:
    ENG = ("pe", "dve", "act", "pool", "sp")

    def __init__(self, nc, same_engine_sync=True, ndma_sems=12):
        self.nc = nc
        self.es = contextlib.ExitStack()
        self.eng = {"pe": nc.tensor, "dve": nc.vector, "act": nc.scalar, "pool": nc.gpsimd, "sp": nc.sync}
        self.ops = {e: [] for e in self.ENG}
        self.csem = {e: self.es.enter_context(nc.semaphore("c_" + e)) for e in self.ENG}
        self.ccnt = {e: 0 for e in self.ENG}
        self.known = {e: {} for e in self.ENG}
        self.same = same_engine_sync
        self.dsem = {}
        self.dval = {}
        self.dptr = {}
        for q in ("sp", "pool", "act"):
            self.dsem[q] = [self.es.enter_context(nc.semaphore("d_%s%d" % (q, i))) for i in range(ndma_sems)]
            self.dval[q] = [0] * ndma_sems
            self.dptr[q] = 0
        self.n_ops = 0
        self.sb_used = 0

    def sb(self, name, shape, dtype=F32):
        self.n_ops += 1
        name = "%s_u%d" % (name, self.n_ops)
        t = self.es.enter_context(self.nc.sbuf_tensor(name, list(shape), dtype))
        b = Buf(name, t)
        return V(t[:] if hasattr(t, "__getitem__") else t.ap(), [b])

    def ps(self, name, shape, dtype=F32):
        t = self.es.enter_context(self.nc.psum_tensor(name, list(shape), dtype))
        b = Buf(name, t)
        return V(t[:], [b])

    def dram(self, name, shape, dtype=F32, kind="Internal"):
        t = self.nc.dram_tensor(name, list(shape), dtype, kind=kind)
        b = Buf(name, t)
        return V(t.ap(), [b])

    def alias(self, v, name):
        return V(v.ap, [Buf(name, None)])

    def _waits(self, e, reads, writes):
        need = {}

        def add(ev):
            sm, v = ev
            if need.get(sm, 0) < v:
                need[sm] = v

        for vw in reads:
            bx = vw.box
            for b in vw.bufs:
                for wb, ev in b.w:
                    if _ovl(bx, wb):
                        add(ev)
        for vw in writes:
            bx = vw.box
            for b in vw.bufs:
                for wb, ev in b.w:
                    if _ovl(bx, wb):
                        add(ev)
                for rb, ev in b.r:
                    if _ovl(bx, rb):
                        add(ev)
        out = []
        kn = self.known[e]
        own = self.csem[e]
        for sm, v in need.items():
            if sm is own and (not self.same or e == "pe"):
                continue
            if kn.get(sm, 0) >= v:
                continue
            kn[sm] = v
            out.append((sm, v))
        return out

    def _mark(self, ev, reads, writes):
        for vw in reads:
            bx = vw.box
            for b in vw.bufs:
                b.r.append((bx, ev))
                if len(b.r) > 40:
                    b.r = _collapse(b.r)
        for vw in writes:
            bx = vw.box
            for b in vw.bufs:
                b.w = [x for x in b.w if not _inside(x[0], bx)]
                b.r = [x for x in b.r if not _inside(x[0], bx)]
                b.w.append((bx, ev))
                if len(b.w) > 40:
                    b.w = _collapse(b.w)

    def op(self, e, fn, reads, writes, signal=True):
        waits = self._waits(e, reads, writes)
        sem = self.csem[e]
        ev = (sem, self.ccnt[e] + 1)
        if signal:
            self.ccnt[e] += 1
        self._mark(ev, reads, writes)
        self.n_ops += 1

        eng = self.eng[e]
        for s, v in waits:
            eng.wait_ge(s, v)
        ins = fn(eng)
        if signal:
            ins.then_inc(sem, 1)

    def dma(self, q, out, in_, **kw):
        i = self.dptr[q]
        self.dptr[q] = (i + 1) % len(self.dsem[q])
        sem = self.dsem[q][i]
        prev = self.dval[q][i]
        waits = self._waits(q, [in_], [out])
        kn = self.known[q]
        if prev > 0 and kn.get(sem, 0) < prev:
            kn[sem] = prev
            waits.append((sem, prev))
        val = prev + 16
        self.dval[q][i] = val
        ev = (sem, val)
        self._mark(ev, [in_], [out])
        self.n_ops += 1
        oap, iap = out.ap, in_.ap

        eng = self.eng[q]
        for s, v in waits:
            eng.wait_ge(s, v)
        eng.dma_start(out=oap, in_=iap, **kw).then_inc(sem, 16)
        return ev

    def cc_allgather(self, out, in_, groups):
        q = "pool"
        i = self.dptr[q]
        self.dptr[q] = (i + 1) % len(self.dsem[q])
        sem = self.dsem[q][i]
        prev = self.dval[q][i]
        waits = self._waits(q, [in_], [out])
        kn = self.known[q]
        if prev > 0 and kn.get(sem, 0) < prev:
            kn[sem] = prev
            waits.append((sem, prev))
        val = prev + 16
        self.dval[q][i] = val
        ev = (sem, val)
        self._mark(ev, [in_], [out])
        eng = self.eng[q]
        for s_, v in waits:
            eng.wait_ge(s_, v)
        eng.collective_compute("AllGather", ALU.bypass, replica_groups=groups, ins=[in_.ap], outs=[out.ap]).then_inc(sem, 16)
        return ev

    def barrier(self):
        evs = [(self.csem[x], self.ccnt[x]) for x in self.ENG if self.ccnt[x] > 0]
        for q in self.dsem:
            for s, v in zip(self.dsem[q], self.dval[q]):
                if v > 0:
                    evs.append((s, v))
        for e in self.ENG:
            kn = self.known[e]
            for s, v in evs:
                if s is self.csem[e]:
                    continue
                if kn.get(s, 0) < v:
                    kn[s] = v
                    self.eng[e].wait_ge(s, v)

    def scope(self):
        k = self

        class _S:
            def __enter__(s):
                s.old = k.es
                k.es = contextlib.ExitStack()
                return s

            def __exit__(s, *a):
                k.barrier()
                k.es.close()
                k.es = s.old
                return False
        return _S()

    def finish(self):
        self.barrier()
        self.es.close()

    def mm(self, out, lhsT, rhs, start=True, stop=True, signal=True):
        self.op("pe", lambda e: e.matmul(out.ap, lhsT.ap, rhs.ap, start=start, stop=stop),
                [lhsT, rhs] + ([] if start else [out]), [out], signal=signal)

    def tr(self, out, in_, ident):
        self.op("pe", lambda e: e.transpose(out.ap, in_.ap, ident.ap), [in_, ident], [out])

    def actv(self, out, in_, func, bias=0.0, scale=1.0, eng="act"):
        rd = [in_]
        b = bias
        s = scale
        if isinstance(bias, V):
            rd.append(bias)
            b = bias.ap
        if isinstance(scale, V):
            rd.append(scale)
            s = scale.ap
        self.op("act", lambda e: e.activation(out.ap, in_.ap, func, bias=b, scale=s), rd, [out])

    def tt(self, out, a, b, op, eng="dve"):
        self.op(eng, lambda e: e.tensor_tensor(out.ap, a.ap, b.ap, op), [a, b], [out])

    def ts(self, out, a, s1, s2, op0, op1=None, eng="dve"):
        rd = [a]
        x1, x2 = s1, s2
        if isinstance(s1, V):
            rd.append(s1)
            x1 = s1.ap
        if isinstance(s2, V):
            rd.append(s2)
            x2 = s2.ap
        if op1 is None:
            self.op(eng, lambda e: e.tensor_scalar(out.ap, a.ap, x1, None, op0), rd, [out])
        else:
            self.op(eng, lambda e: e.tensor_scalar(out.ap, a.ap, x1, x2, op0, op1), rd, [out])

    def stt(self, out, a, s, b, op0, op1):
        rd = [a, b]
        x = s
        if isinstance(s, V):
            rd.append(s)
            x = s.ap
        self.op("dve", lambda e: e.scalar_tensor_tensor(out.ap, a.ap, x, b.ap, op0, op1), rd, [out])

    def scan(self, out, d0, d1, init, op0, op1):
        rd = [d0, d1]
        x = init
        if isinstance(init, V):
            rd.append(init)
            x = init.ap
        self.op("dve", lambda e: e.tensor_tensor_scan(out.ap, d0.ap, d1.ap, x, op0, op1), rd, [out])

    def copy(self, out, in_, eng="dve"):
        if eng == "act":
            self.op("act", lambda e: e.copy(out.ap, in_.ap), [in_], [out])
        else:
            self.op(eng, lambda e: e.tensor_copy(out.ap, in_.ap), [in_], [out])

    def memset(self, out, val, eng="pool"):
        self.op(eng, lambda e: e.memset(out.ap, val), [], [out])

    def recip(self, out, in_):
        self.op("dve", lambda e: e.reciprocal(out.ap, in_.ap), [in_], [out])

from concourse.bass_utils import run_bass_kernel_spmd

EPS = 1e-6
RW_LN_EPS = 64e-5
NEG = -1.0e30


class Cfg:
    def __init__(self, D, T, L, DFF, FGS, dbg=False):
        self.D, self.T, self.L, self.DFF = D, T, L, DFF
        self.KC = D // 128
        self.BW = D // 4
        self.BC = self.BW // 128
        self.FH = self.BW // 128
        self.SH = self.BW // 64
        self.G = 2
        self.RH = self.BW // 64
        self.FB = DFF // 128
        self.FGS = FGS
        assert self.FB % FGS == 0
        self.NT = 512
        self.NTT = T // 512
        BC, G = self.BC, self.G
        o = {}
        o["ax"] = 0; o["ag"] = BC; o["bq"] = 2 * BC; o["bk"] = 3 * BC; o["bv"] = 4 * BC
        o["cz"] = 5 * BC; o["cx"] = 6 * BC; o["cb"] = 7 * BC; o["cc"] = 7 * BC + G
        o["dr"] = 7 * BC + 2 * G; o["dk"] = o["dr"] + BC; o["dv"] = o["dk"] + BC
        o["dwa"] = o["dv"] + BC; o["dgl"] = o["dwa"] + 1; o["small"] = o["dgl"] + 1
        self.off = o
        self.NBLK = o["small"] + 1
        self.NXBC = BC + 2 * G
        self.NU = 3 * BC + 2
        v = {}
        n = 0

        def add(name, c):
            nonlocal n
            v[name] = n
            n += c
        KC = self.KC
        add("mixn", KC); add("ffnn", KC); add("plen", KC)
        add("lru_cw", 4 * BC); add("lru_cb", BC); add("lru_br", BC); add("lru_bi", BC); add("lru_lam", BC)
        add("fox_qg", 1); add("fox_kg", 1)
        add("sm_sign", 1); add("sm_bias", 1); add("ssd_alog", 1)
        add("ssd_cw", 4 * self.NXBC); add("ssd_cb", self.NXBC); add("ssd_d", BC); add("ssd_nw", BC)
        add("rw_mu", self.NU); add("rw_w0", BC); add("rw_a0", BC); add("rw_kk", BC); add("rw_ka", BC)
        add("rw_rk", BC); add("rw_lnw", BC); add("rw_lnb", BC)
        add("ffn_cw", 3 * 2 * self.FB); add("ffn_cb", 2 * self.FB)
        self.vec = v
        self.NV = n
        c = {}
        n = 0

        def addc(name, cc):
            nonlocal n
            c[name] = n
            n += cc
        addc("ident", 128); addc("ones", 128); addc("mneg", 128); addc("m_su", 128); addc("m_iu", 128)
        addc("m_sl", 128); addc("bo64", 128); addc("reset64", 512); addc("reset128", 512)
        addc("sel", self.SH * 128)
        self.cst = c
        self.NCST = n
        self.dbg = dbg


def colvec(v):
    v = np.asarray(v, np.float32).reshape(-1, 128)
    return v.T


def pack_w(W):
    Kd, N = W.shape
    return np.ascontiguousarray(W.reshape(Kd // 128, 128, N // 128, 128).transpose(2, 1, 0, 3))


def host_consts(cfg):
    c = np.zeros((128, cfg.NCST), np.float32)
    o = cfg.cst
    p = np.arange(128)[:, None]
    f = np.arange(128)[None, :]
    c[:, o["ident"]:o["ident"] + 128] = (p == f)
    c[:, o["ones"]:o["ones"] + 128] = 1.0
    c[:, o["mneg"]:o["mneg"] + 128] = np.where(p <= f, 0.0, NEG)
    c[:, o["m_su"]:o["m_su"] + 128] = (p < f)
    c[:, o["m_iu"]:o["m_iu"] + 128] = (p <= f)
    c[:, o["m_sl"]:o["m_sl"] + 128] = (p > f)
    c[:, o["bo64"]:o["bo64"] + 128] = ((p // 64) == (f // 64))
    t = np.arange(512)[None, :]
    c[:, o["reset64"]:o["reset64"] + 512] = (t % 64 != 0)
    c[:, o["reset128"]:o["reset128"] + 512] = (t % 128 != 0)
    for h in range(cfg.SH):
        c[cfg.FH + h, o["sel"] + h * 128: o["sel"] + (h + 1) * 128] = 1.0
    return c


def host_pack(cfg, inp, b):
    D, T, L, BW, BC, G = cfg.D, cfg.T, cfg.L, cfg.BW, cfg.BC, cfg.G
    KC, FB = cfg.KC, cfg.FB
    f32 = np.float32
    m = {}
    m["xT"] = np.ascontiguousarray(np.asarray(inp["x"][b], f32).T)
    m["pT"] = np.ascontiguousarray(np.asarray(inp["p"][:, b], f32).transpose(0, 2, 1))
    m["cst"] = host_consts(cfg)
    w_in = np.asarray(inp["w_in"], f32)
    base = 4 * D
    cols = []
    offs = {}
    cur = base
    for name, wd in (("ax", BW), ("ag", BW), ("bq", BW), ("bk", BW), ("bv", BW), ("bf", cfg.FH), ("cz", BW),
                     ("cxbc", BW + 2 * G * 128), ("cdt", cfg.SH), ("du", 3 * BW + 256)):
        offs[name] = cur
        cur += wd
    assert cur == w_in.shape[2]
    idx = []
    for name, wd in (("ax", BW), ("ag", BW), ("bq", BW), ("bk", BW), ("bv", BW), ("cz", BW),
                     ("cxbc", BW + 2 * G * 128), ("du", 3 * BW + 256)):
        idx += list(range(offs[name], offs[name] + wd))
    small = list(range(offs["bf"], offs["bf"] + cfg.FH)) + list(range(offs["cdt"], offs["cdt"] + cfg.SH))
    nsm = len(small)
    wmix = np.zeros((L, cfg.NBLK, 128, KC, 128), f32)
    wgate = np.zeros((L, KC * 4, 128, KC, 128), f32)
    for l in range(L):
        W = w_in[l]
        Wm = np.zeros((D, cfg.NBLK * 128), f32)
        Wm[:, :len(idx)] = W[:, idx]
        Wm[:, (cfg.NBLK - 1) * 128:(cfg.NBLK - 1) * 128 + nsm] = W[:, small]
        wmix[l] = pack_w(Wm)
        Wg = W[:, :4 * D].reshape(D, 4, KC, 128).transpose(0, 2, 1, 3).reshape(D, 4 * D)
        wgate[l] = pack_w(Wg)
    m["wmix"] = wmix
    m["wgate"] = wgate
    wb = np.asarray(inp["w_branch"], f32)
    m["wbr"] = np.ascontiguousarray(wb.reshape(L, 4, BC, 128, KC, 128).transpose(0, 4, 3, 1, 2, 5))
    m["wout"] = np.stack([pack_w(np.asarray(inp["w_out"][l], f32)) for l in range(L)])
    wu = np.asarray(inp["w_up"], f32)
    m["wup"] = np.stack([pack_w(wu[l]) for l in range(L)])
    wd_ = np.asarray(inp["w_down"], f32)
    m["wdown"] = np.stack([pack_w(wd_[l]) for l in range(L)])
    m["wpg"] = np.stack([pack_w(np.asarray(inp["w_ple_gate"][l], f32)) for l in range(L)])
    m["wple"] = np.stack([pack_w(np.asarray(inp["w_ple"][l], f32)) for l in range(L)])
    wr = np.asarray(inp["lru_wr"], f32)
    wi = np.asarray(inp["lru_wi"], f32)
    bd = np.zeros((L, 2, BC, 128, 128), f32)
    for l in range(L):
        for cc in range(BC):
            for bb in range(2):
                bd[l, 0, cc, bb * 64:(bb + 1) * 64, bb * 64:(bb + 1) * 64] = wr[l, 2 * cc + bb]
                bd[l, 1, cc, bb * 64:(bb + 1) * 64, bb * 64:(bb + 1) * 64] = wi[l, 2 * cc + bb]
    m["lrubd"] = bd
    m["rwwa2"] = np.concatenate([np.asarray(inp["rwkv_w2"], f32), np.asarray(inp["rwkv_a2"], f32)], axis=1)
    m["rwg2"] = np.asarray(inp["rwkv_g2"], f32)
    vec = np.zeros((L, 128, cfg.NV), f32)
    vo = cfg.vec
    for l in range(L):
        def put(name, arr):
            a = colvec(arr)
            vec[l, :, vo[name]:vo[name] + a.shape[1]] = a
        put("mixn", inp["mix_norm"][l]); put("ffnn", inp["ffn_norm"][l]); put("plen", inp["ple_norm"][l])
        cw = np.asarray(inp["lru_conv_w"][l], f32)
        put("lru_cw", cw.reshape(-1))
        put("lru_cb", inp["lru_conv_b"][l]); put("lru_br", inp["lru_br"][l]); put("lru_bi", inp["lru_bi"][l])
        put("lru_lam", inp["lru_lambda"][l])
        put("fox_qg", inp["fox_q_gain"][l]); put("fox_kg", inp["fox_k_gain"][l])
        sg = np.zeros(128, f32); sg[:cfg.FH] = -1.0; sg[cfg.FH:cfg.FH + cfg.SH] = 1.0
        put("sm_sign", sg)
        sbias = np.zeros(128, f32); sbias[:cfg.FH] = np.asarray(inp["fox_fbias"][l]); sbias[cfg.FH:cfg.FH + cfg.SH] = np.asarray(inp["ssd_dt_bias"][l])
        put("sm_bias", sbias)
        al = np.zeros(128, f32); al[cfg.FH:cfg.FH + cfg.SH] = np.asarray(inp["ssd_a_log"][l])
        put("ssd_alog", al)
        put("ssd_cw", np.asarray(inp["ssd_conv_w"][l], f32).reshape(-1)); put("ssd_cb", inp["ssd_conv_b"][l])
        put("ssd_d", np.repeat(np.asarray(inp["ssd_d"][l], f32), 64)); put("ssd_nw", inp["ssd_norm"][l])
        put("rw_mu", inp["rwkv_mu"][l]); put("rw_w0", inp["rwkv_w0"][l]); put("rw_a0", inp["rwkv_a0"][l])
        put("rw_kk", inp["rwkv_kk"][l]); put("rw_ka", inp["rwkv_ka"][l]); put("rw_rk", np.asarray(inp["rwkv_rk"][l], f32).reshape(-1))
        put("rw_lnw", inp["rwkv_ln_w"][l]); put("rw_lnb", inp["rwkv_ln_b"][l])
        put("ffn_cw", np.asarray(inp["ffn_conv_w"][l], f32).reshape(-1))
        put("ffn_cb", inp["ffn_conv_b"][l])
    m["vec"] = vec
    return m


class Ring:
    def __init__(self, items):
        self.items = items
        self.i = 0

    def next(self):
        v = self.items[self.i]
        self.i = (self.i + 1) % len(self.items)
        return v


def build(cfg, upto="all"):
    nc = bass.Bass("TRN2", target_bir_lowering=False)
    k = K(nc)
    D, T, L, BW, BC, G, KC, FB, NT, NTT = cfg.D, cfg.T, cfg.L, cfg.BW, cfg.BC, cfg.G, cfg.KC, cfg.FB, cfg.NT, cfg.NTT
    FH, SH, RH = cfg.FH, cfg.SH, cfg.RH
    off, vo, co = cfg.off, cfg.vec, cfg.cst
    NBLK = cfg.NBLK
    EI = "ExternalInput"
    xT = k.dram("xT", [D, T], kind=EI)
    pT = k.dram("pT", [L, 256, T], kind=EI)
    cstd = k.dram("cst", [128, cfg.NCST], kind=EI)
    wmix = k.dram("wmix", [L, NBLK, 128, KC * 128], kind=EI)
    wgate = k.dram("wgate", [L, KC * 4, 128, KC * 128], kind=EI)
    wbr = k.dram("wbr", [L, KC, 128, 4 * BC * 128], kind=EI)
    wout = k.dram("wout", [L, KC, 128, KC * 128], kind=EI)
    wup = k.dram("wup", [L, 2 * FB, 128, KC * 128], kind=EI)
    wdown = k.dram("wdown", [L, KC, 128, FB * 128], kind=EI)
    wpg = k.dram("wpg", [L, KC, 128, KC * 128], kind=EI)
    wple = k.dram("wple", [L, KC, 128, 2 * 128], kind=EI)
    lrubd = k.dram("lrubd", [L, 2, BC, 128, 128], kind=EI)
    rwwa2 = k.dram("rwwa2", [L, 128, BW], kind=EI)
    rwg2 = k.dram("rwg2", [L, 128, BW], kind=EI)
    vecd = k.dram("vec", [L, 128, cfg.NV], kind=EI)
    outd = k.dram("out", [D, T], kind="ExternalOutput")
    sk = "ExternalOutput" if cfg.dbg else "Internal"
    Hd = k.dram("H", [D, T], kind=sk)
    PMd = k.dram("PM", [NBLK * 128, T], kind=sk)
    Yd = k.dram("Y", [4 * BW, T], kind=sk)
    CSD = k.dram("CSD", [128, T])
    SPD = k.dram("SPD", [128, T])

    def tview(base, name, n):
        return [V(base.ap, [Buf("%s%d" % (name, i), None)]) for i in range(n)]
    Htt = tview(Hd, "Htt", NTT)
    Ott = tview(outd, "Ott", NTT)
    PMb = tview(PMd, "PMb", NBLK)
    Yb = tview(Yd, "Yb", 4 * BC)

    cst = k.sb("cst_sb", [128, cfg.NCST])
    k.dma("sp", cst, cstd)
    ones_r = k.sb("ones_r", [128, 128], F32R)
    k.dma("pool", ones_r, cstd[:, co["ones"]:co["ones"] + 128])
    vec = k.sb("vec_sb", [128, cfg.NV])
    ident = cst[:, co["ident"]:co["ident"] + 128]
    ones = cst[:, co["ones"]:co["ones"] + 128]
    mneg = cst[:, co["mneg"]:co["mneg"] + 128]
    PSb = [k.ps("ps%d" % i, [128, 512]) for i in range(8)]
    PS = Ring(PSb)
    evac_flip = [0]

    def evac(out, ps):
        evac_flip[0] ^= 1
        if evac_flip[0]:
            k.actv(out, ps, AF.Copy)
        else:
            k.copy(out, ps)

    def vc(name, i=0, n=1):
        return vec[:, vo[name] + i: vo[name] + i + n]

    def wring_make(n, width):
        return Ring([k.sb("wr%d" % i, [128, width], F32R) for i in range(n)])

    def rmsnorm(h, xn, wname, rs):
        k.actv(xn, h, AF.Square)
        ps = PS.next()
        for c in range(KC):
            k.mm(ps, ones_r, xn[:, c, :], start=(c == 0), stop=(c == KC - 1), signal=(c == KC - 1))
        k.actv(rs, ps, AF.Sqrt, bias=epsD, scale=1.0 / D)
        k.recip(rs, rs)
        for c in range(KC):
            k.stt(xn[:, c, :], h[:, c, :], vc(wname, c), rs, ALU.mult, ALU.mult)

    epsD = k.sb("epsD", [128, 1])
    k.memset(epsD, EPS)
    eps128 = epsD
    lneps = k.sb("lneps", [128, 1])
    k.memset(lneps, RW_LN_EPS)
    WMAX = max(KC, 4 * BC, cfg.FGS) * 128

    for l in range(L):
        k.dma("sp", vec, vecd[l])
        hsrc = xT if l == 0 else Hd
        hsrc_t = [V(xT.ap, xT.bufs)] * NTT if l == 0 else Htt
        hdst = outd if l == L - 1 else Hd
        hdst_t = Ott if l == L - 1 else Htt

        with k.scope():
            wring = wring_make(6, WMAX)
            h = k.sb("A_h", [128, KC, NT])
            xn = k.sb("A_xn", [128, KC, NT], F32R)
            rs = k.sb("A_rs", [128, NT])
            stage = Ring([k.sb("A_st%d" % i, [128, NT]) for i in range(4)])
            for tt in range(NTT):
                tsl = slice(tt * NT, (tt + 1) * NT)
                k.dma("sp", h, V(hsrc.ap[:, tsl].rearrange("(c p) t -> p c t", p=128), hsrc_t[tt].bufs))
                rmsnorm(h, xn, "mixn", rs)
                for blk in range(NBLK):
                    wt = wring.next()
                    k.dma("pool", wt[:, :KC * 128], wmix[l, blk])
                    ps = PS.next()
                    for c in range(KC):
                        k.mm(ps, wt[:, c * 128:(c + 1) * 128], xn[:, c, :], start=(c == 0), stop=(c == KC - 1),
                             signal=(c == KC - 1))
                    st = stage.next()
                    evac(st, ps)
                    k.dma("act", V(PMd.ap[blk * 128:(blk + 1) * 128, tsl], PMb[blk].bufs), st)
        if upto == "A":
            break

        with k.scope():
            sm = k.sb("B0_sm", [128, T])
            zt = k.sb("B0_zt", [128, T])
            ones512 = k.sb("B0_ones", [128, 512])
            sb2 = k.sb("B0_b2", [128, 1])
            k.memset(ones512, 1.0)
            k.tt(sb2, vc("sm_bias"), vc("sm_sign"), ALU.mult)
            k.dma("sp", sm, V(PMd.ap[off["small"] * 128:(off["small"] + 1) * 128, :], PMb[off["small"]].bufs))
            k.actv(zt, sm, AF.Identity, bias=sb2, scale=vc("sm_sign"))
            k.actv(sm, zt, AF.Abs)
            k.actv(sm, sm, AF.Exp, scale=-1.0)
            k.actv(sm, sm, AF.Ln, bias=1.0)
            k.ts(zt, zt, 0.0, None, ALU.max)
            k.tt(zt, zt, sm, ALU.add)
            for s in range(T // 512):
                init = 0.0 if s == 0 else sm[:, s * 512 - 1:s * 512]
                k.scan(sm[:, s * 512:(s + 1) * 512], ones512, zt[:, s * 512:(s + 1) * 512], init, ALU.mult, ALU.add)
            k.dma("act", CSD, sm)
            k.dma("act", SPD, zt)

        with k.scope():
            TB = min(T, 2048)
            xin = k.sb("L_xin", [128, 3 + TB])
            gin = k.sb("L_gin", [128, TB])
            xc = k.sb("L_xc", [128, TB])
            rt = k.sb("L_rt", [128, TB])
            it = k.sb("L_it", [128, TB])
            at = k.sb("L_at", [128, TB])
            tmp = k.sb("L_tmp", [128, TB])
            hs = k.sb("L_hs", [128, TB])
            xcar = k.sb("L_xcar", [128, 3])
            hcar = k.sb("L_hcar", [128, 1])
            ca = k.sb("L_ca", [128, BC])
            bdr = k.sb("L_bdr", [128, BC, 128])
            bdi = k.sb("L_bdi", [128, BC, 128])
            k.dma("sp", bdr, lrubd[l, 0].re("c p m -> p c m"))
            k.dma("sp", bdi, lrubd[l, 1].re("c p m -> p c m"))
            k.actv(ca, vc("lru_lam", 0, BC), AF.Exp, scale=-1.0)
            k.actv(ca, ca, AF.Ln, bias=1.0)
            k.ts(ca, ca, -8.0, None, ALU.mult)
            for cc in range(BC):
                for tb in range(T // TB):
                    tsl = slice(tb * TB, (tb + 1) * TB)
                    if tb == 0:
                        k.memset(xin[:, 0:3], 0.0, eng="dve")
                    else:
                        k.copy(xin[:, 0:3], xcar)
                    k.dma("sp", xin[:, 3:], V(PMd.ap[(off["ax"] + cc) * 128:(off["ax"] + cc + 1) * 128, tsl], PMb[off["ax"] + cc].bufs))
                    k.dma("sp", gin, V(PMd.ap[(off["ag"] + cc) * 128:(off["ag"] + cc + 1) * 128, tsl], PMb[off["ag"] + cc].bufs))
                    k.ts(xc, xin[:, 3:3 + TB], vc("lru_cw", 3 * BC + cc), vc("lru_cb", cc), ALU.mult, ALU.add)
                    for tap in range(3):
                        k.stt(xc, xin[:, tap:tap + TB], vc("lru_cw", tap * BC + cc), xc, ALU.mult, ALU.add)
                    k.copy(xcar, xin[:, TB:TB + 3])
                    for s in range(TB // 512):
                        ssl = slice(s * 512, (s + 1) * 512)
                        ps = PS.next()
                        k.mm(ps, bdr[:, cc, :], xc[:, ssl])
                        k.actv(rt[:, ssl], ps, AF.Sigmoid, bias=vc("lru_br", cc))
                        ps = PS.next()
                        k.mm(ps, bdi[:, cc, :], xc[:, ssl])
                        k.actv(it[:, ssl], ps, AF.Sigmoid, bias=vc("lru_bi", cc))
                    k.actv(at, rt, AF.Exp, scale=ca[:, cc:cc + 1])
                    k.tt(tmp, at, at, ALU.mult)
                    k.ts(tmp, tmp, -1.0, 1.0, ALU.mult, ALU.add)
                    k.actv(tmp, tmp, AF.Sqrt)
                    k.tt(it, it, xc, ALU.mult)
                    k.tt(it, it, tmp, ALU.mult)
                    k.scan(hs, at, it, 0.0 if tb == 0 else hcar, ALU.mult, ALU.add)
                    k.copy(hcar, hs[:, TB - 1:TB])
                    k.actv(gin, gin, AF.Gelu_apprx_tanh)
                    k.tt(hs, hs, gin, ALU.mult)
                    k.dma("act", V(Yd.ap[cc * 128:(cc + 1) * 128, tsl], Yb[cc].bufs), hs)

        NKC = T // 128
        for hh in range(FH):
            with k.scope():
                qn = k.sb("F_qn", [128, T], F32R)
                kn = k.sb("F_kn", [128, T], F32R)
                vtm = k.sb("F_vtm", [128, NKC, 128], F32R)
                raw = Ring([k.sb("F_raw%d" % i, [128, 512]) for i in range(2)])
                sq = k.sb("F_sq", [128, 512], F32R)
                rs = k.sb("F_rs", [128, 512])
                ckp = k.sb("F_ckp", [128, NKC])
                cref = k.sb("F_cref", [128, NKC])
                bias = k.sb("F_bias", [128, NKC, NKC])
                pts = Ring([k.sb("F_pt%d" % i, [128, 512], F32R) for i in range(3)])
                tm = k.sb("F_tm", [128, 128])
                osb = k.sb("F_osb", [128, 512])
                rd = k.sb("F_rd", [128, 512])
                for (src, dst, gname) in ((off["bq"], qn, "fox_qg"), (off["bk"], kn, "fox_kg")):
                    for s in range(T // 512):
                        ssl = slice(s * 512, (s + 1) * 512)
                        r_ = raw.next()
                        k.dma("sp", r_, V(PMd.ap[(src + hh) * 128:(src + hh + 1) * 128, ssl], PMb[src + hh].bufs))
                        k.actv(sq, r_, AF.Square)
                        ps = PS.next()
                        k.mm(ps, ones_r, sq)
                        k.actv(rs, ps, AF.Sqrt, bias=eps128, scale=1.0 / 128)
                        k.recip(rs, rs)
                        k.stt(dst[:, ssl], r_, vc(gname), rs, ALU.mult, ALU.mult)
                for s in range(T // 512):
                    ssl = slice(s * 512, (s + 1) * 512)
                    r_ = raw.next()
                    k.dma("sp", r_, V(PMd.ap[(off["bv"] + hh) * 128:(off["bv"] + hh + 1) * 128, ssl], PMb[off["bv"] + hh].bufs))
                    ps = PS.next()
                    for j in range(4):
                        k.tr(ps[:, j * 128:(j + 1) * 128], r_[:, j * 128:(j + 1) * 128], ident)
                    evac(vtm[:, s * 4:(s + 1) * 4, :], ps.re("p (j d) -> p j d", j=4))
                k.dma("sp", ckp, V(CSD.ap[hh:hh + 1, :].rearrange("o (j p) -> p (o j)", p=128), CSD.bufs),
                      allow_slow_non_contiguous=True)
                k.memset(cref[:, 0:1], 0.0, eng="dve")
                if NKC > 1:
                    k.dma("sp", cref[:, 1:NKC],
                          V(CSD.ap[hh:hh + 1, 127:T - 1:128].partition_broadcast(128).rearrange("p o j -> p (o j)"), CSD.bufs),
                          allow_slow_non_contiguous=True)
                for qs in range(NKC):
                    k.ts(bias[:, qs, 0:qs + 1], ckp[:, 0:qs + 1], cref[:, qs:qs + 1], None, ALU.subtract)
                scale = 128 ** -0.5
                PS47 = Ring(PSb[4:])
                for qt in range(T // 512):
                    qsl = slice(qt * 512, (qt + 1) * 512)
                    nk = (qt + 1) * 4
                    ops_ = PSb[(qt % 2) * 2]
                    dps = PSb[(qt % 2) * 2 + 1]
                    for j in range(nk):
                        sps = PS47.next()
                        k.mm(sps, kn[:, j * 128:(j + 1) * 128], qn[:, qsl])
                        pt = pts.next()
                        jr = j - qt * 4
                        for qq in range(4):
                            csl = slice(qq * 128, (qq + 1) * 128)
                            qs = qt * 4 + qq
                            if jr > qq:
                                k.memset(pt[:, csl].f(), 0.0, eng="dve")
                            elif jr == qq:
                                k.tt(tm, sps[:, csl], mneg, ALU.add)
                                k.actv(pt[:, csl], tm, AF.Exp, bias=bias[:, qs, j:j + 1], scale=scale)
                            else:
                                k.actv(pt[:, csl], sps[:, csl], AF.Exp, bias=bias[:, qs, j:j + 1], scale=scale)
                        k.mm(ops_, vtm[:, j, :], pt, start=(j == 0), stop=(j == nk - 1), signal=False)
                        k.mm(dps, ones_r, pt, start=(j == 0), stop=(j == nk - 1), signal=True)
                    k.recip(rd, dps)
                    k.tt(osb, ops_, rd, ALU.mult)
                    k.dma("act", V(Yd.ap[BW + hh * 128:BW + (hh + 1) * 128, qsl], Yb[BC + hh].bufs), osb)
        if upto == "B2":
            break

        NXBC = cfg.NXBC
        HPG = SH // G
        with k.scope():
            W = 512
            xin = k.sb("S_xin", [128, NXBC, 3 + W])
            xc = k.sb("S_xc", [128, NXBC, W])
            xcar = k.sb("S_xcar", [128, NXBC, 3])
            zt = k.sb("S_z", [128, BC, W])
            spw = k.sb("S_spw", [128, W])
            dan = k.sb("S_dan", [128, W])
            ac = k.sb("S_ac", [128, W])
            aneg = k.sb("S_aneg", [128, 1])
            atm = k.sb("S_atm", [128, 2, 128])
            xtm = k.sb("S_xtm", [128, BC * 128])
            xdt = k.sb("S_xdt", [128, SH, 64])
            btm = k.sb("S_btm", [128, G, 128])
            gts = k.sb("S_gts", [128, G, 128])
            rbs = k.sb("S_rbs", [128, 128])
            tmpa = k.sb("S_tmpa", [128, 128])
            dm = k.sb("S_dm", [128, 128])
            wts = k.sb("S_wts", [128, 128])
            es = k.sb("S_es", [128, 128])
            cps = k.sb("S_cps", [128, 128])
            naL = k.sb("S_naL", [128, 1])
            ds = k.sb("S_ds", [128, 1])
            bps = k.sb("S_bps", [128, 128])
            stt_ = [k.sb("S_state%d" % i, [128, 64]) for i in range(SH)]
            yw = k.sb("S_yw", [128, BC, W])
            sqr = k.sb("S_sqr", [128, W], F32R)
            bo64r = k.sb("S_bo64r", [128, 128], F32R)
            rsd = k.sb("S_rsd", [128, W])
            k.dma("pool", bo64r, cstd[:, co["bo64"]:co["bo64"] + 128])
            k.actv(aneg, vc("ssd_alog"), AF.Exp)
            for s_ in stt_:
                k.memset(s_, 0.0, eng="dve")
            PS47 = Ring(PSb[4:])
            for w_ in range(T // W):
                wsl = slice(w_ * W, (w_ + 1) * W)
                if w_ == 0:
                    k.memset(xin[:, :, 0:3], 0.0, eng="dve")
                else:
                    k.copy(xin[:, :, 0:3], xcar)
                for b in range(NXBC):
                    k.dma("sp", xin[:, b, 3:], V(PMd.ap[(off["cx"] + b) * 128:(off["cx"] + b + 1) * 128, wsl], PMb[off["cx"] + b].bufs))
                k.dma("sp", zt, V(PMd.ap[off["cz"] * 128:(off["cz"] + BC) * 128, wsl].rearrange("(c p) t -> p c t", p=128),
                                  sum([PMb[off["cz"] + b].bufs for b in range(BC)], [])))
                k.dma("sp", spw, SPD[:, wsl])
                for b in range(NXBC):
                    k.ts(xc[:, b, :], xin[:, b, 3:3 + W], vc("ssd_cw", 3 * NXBC + b), vc("ssd_cb", b), ALU.mult, ALU.add)
                    for tap in range(3):
                        k.stt(xc[:, b, :], xin[:, b, tap:tap + W], vc("ssd_cw", tap * NXBC + b), xc[:, b, :], ALU.mult, ALU.add)
                k.copy(xcar, xin[:, :, W:W + 3])
                k.actv(xc, xc, AF.Silu)
                k.ts(dan, spw, aneg, None, ALU.mult)
                k.scan(ac, cst[:, co["reset128"]:co["reset128"] + W], dan, 0.0, ALU.mult, ALU.add)
                for c in range(W // 128):
                    csl = slice(c * 128, (c + 1) * 128)
                    ps = PS47.next()
                    k.tr(ps[:, 0:128], ac[:, csl], ident)
                    k.tr(ps[:, 128:256], spw[:, csl], ident)
                    evac(atm, ps[:, 0:256].re("p (a b) -> p a b", a=2))
                    ps = PS47.next()
                    for b in range(BC):
                        k.tr(ps[:, b * 128:(b + 1) * 128], xc[:, b, csl], ident)
                    evac(xtm, ps[:, 0:BC * 128])
                    k.tt(xdt, xtm.re("p (h d) -> p h d", d=64),
                         V(atm.ap[:, 1, FH:FH + SH].unsqueeze(2).to_broadcast([128, SH, 64]), atm.bufs), ALU.mult)
                    ps = PS47.next()
                    for g in range(G):
                        k.tr(ps[:, g * 128:(g + 1) * 128], xc[:, BC + g, csl], ident)
                    evac(btm, ps[:, 0:G * 128].re("p (g n) -> p g n", g=G))
                    ps = PS47.next()
                    for g in range(G):
                        k.mm(ps[:, g * 128:(g + 1) * 128], xc[:, BC + g, csl], xc[:, BC + G + g, csl])
                    evac(gts, ps[:, 0:G * 128].re("p (g n) -> p g n", g=G))
                    for hd in range(SH):
                        g = hd // HPG
                        ps = PS47.next()
                        k.mm(ps[:, 0:128], cst[:, co["sel"] + hd * 128: co["sel"] + (hd + 1) * 128], ac[:, csl])
                        k.copy(rbs, ps[:, 0:128])
                        k.ts(tmpa, rbs, atm[:, 0, FH + hd:FH + hd + 1], -1.0, ALU.subtract, ALU.mult)
                        k.tt(tmpa, tmpa, mneg, ALU.add)
                        k.actv(dm, tmpa, AF.Exp)
                        k.tt(wts, gts[:, g, :], dm, ALU.mult)
                        k.actv(es, rbs, AF.Exp, scale=-1.0)
                        k.tt(cps, xc[:, BC + G + g, csl], es, ALU.mult)
                        yps = PSb[hd // 2]
                        yo = yps[(hd % 2) * 64:(hd % 2 + 1) * 64, csl]
                        k.mm(yo, xdt[:, hd, :], wts, start=True, stop=False, signal=False)
                        k.mm(yo, stt_[hd], cps, start=False, stop=True)
                        k.ts(naL, rbs[:, 127:128], -1.0, None, ALU.mult)
                        k.actv(ds, atm[:, 0, FH + hd:FH + hd + 1], AF.Exp, bias=naL)
                        k.ts(bps, btm[:, g, :], ds, None, ALU.mult)
                        ps2 = PS47.next()
                        k.mm(ps2[:, 0:64], bps, xdt[:, hd, :])
                        k.stt(stt_[hd], stt_[hd], es[:, 127:128], ps2[:, 0:64], ALU.mult, ALU.add)
                for b in range(BC):
                    k.copy(yw[:, b, :], PSb[b])
                    k.stt(yw[:, b, :], xc[:, b, :], vc("ssd_d", b), yw[:, b, :], ALU.mult, ALU.add)
                k.actv(zt, zt, AF.Silu)
                k.tt(yw, yw, zt, ALU.mult)
                if BC >= G:
                    bpg = BC // G
                    for g in range(G):
                        ps = PS47.next()
                        for j in range(bpg):
                            k.actv(sqr, yw[:, g * bpg + j, :], AF.Square)
                            k.mm(ps, ones_r, sqr, start=(j == 0), stop=(j == bpg - 1))
                        k.actv(rsd, ps, AF.Sqrt, bias=epsD, scale=1.0 / (128 * bpg))
                        k.recip(rsd, rsd)
                        for j in range(bpg):
                            b = g * bpg + j
                            k.stt(yw[:, b, :], yw[:, b, :], vc("ssd_nw", b), rsd, ALU.mult, ALU.mult)
                else:
                    assert BC == 1 and G == 2
                    ps = PS47.next()
                    k.actv(sqr, yw[:, 0, :], AF.Square)
                    k.mm(ps, bo64r, sqr)
                    k.actv(rsd, ps, AF.Sqrt, bias=epsD, scale=1.0 / 64)
                    k.recip(rsd, rsd)
                    k.stt(yw[:, 0, :], yw[:, 0, :], vc("ssd_nw", 0), rsd, ALU.mult, ALU.mult)
                for b in range(BC):
                    k.dma("act", V(Yd.ap[2 * BW + b * 128:2 * BW + (b + 1) * 128, wsl], Yb[2 * BC + b].bufs), yw[:, b, :])
        if upto == "B3":
            break

        NU = cfg.NU
        C0 = float(np.exp(-0.5))
        with k.scope():
            W = 256
            NCH = W // 64
            uin = k.sb("R_uin", [128, NU, 1 + W])
            ucar = k.sb("R_ucar", [128, NU, 1])
            dd = k.sb("R_dd", [128, NU, W])
            ul = k.sb("R_ul", [128, NU, W])
            twt = k.sb("R_twt", [128, W])
            sgt = k.sb("R_sgt", [128, W])
            wa2 = k.sb("R_wa2", [128, BW])
            g2 = k.sb("R_g2", [128, BW])
            k.dma("sp", wa2, rwwa2[l])
            k.dma("sp", g2, rwg2[l])
            bo64 = cst[:, co["bo64"]:co["bo64"] + 128]
            m_su = cst[:, co["m_su"]:co["m_su"] + 128]
            m_iu = cst[:, co["m_iu"]:co["m_iu"] + 128]
            m_sl = cst[:, co["m_sl"]:co["m_sl"] + 128]

            def per(name, shape):
                return [k.sb("R_%s%d" % (name, b), shape) for b in range(BC)]
            lwn = per("lwn", [128, W]); al = per("al", [128, W]); gg = per("gg", [128, W]); kkn = per("kkn", [128, W])
            kp = per("kp", [128, W]); gc = per("gc", [128, W]); bon = per("bon", [128, W]); yw = per("yw", [128, W])
            e1 = k.sb("R_e1", [128, W]); e2 = k.sb("R_e2", [128, W]); e3 = k.sb("R_e3", [128, W]); e4 = k.sb("R_e4", [128, W])
            t1 = k.sb("R_t1", [128, W]); t2 = k.sb("R_t2", [128, W]); qb = k.sb("R_qb", [128, W])
            gamc = per("gamc", [128, NCH])
            BDn = ("P", "Q", "K", "R", "Qh", "Kh", "V")
            BD = [{n: k.sb("R_bd%s%d" % (n, b), [128, NCH, 128]) for n in BDn} for b in range(BC)]
            for b in range(BC):
                for n in BDn:
                    k.memset(BD[b][n], 0.0, eng="dve")
            Hs = per("H", [128, 128])
            for b in range(BC):
                k.memset(Hs[b], 0.0, eng="dve")
            NTt = per("NT", [128, 128]); Nt = per("N", [128, 128]); AKT = per("AKT", [128, 128])
            RBT = per("RBT", [128, 128]); RKT = per("RKT", [128, 128])
            Vtm = per("Vtm", [128, 128]); Qhtm = per("Qhtm", [128, 128]); Khtm = per("Khtm", [128, 128])
            MT = [per("MT%d" % i, [128, 128]) for i in range(1, 6)]
            Ma = per("Ma", [128, 128]); Mb = per("Mb", [128, 128])
            Wa = per("Wa", [128, 128]); Wb = per("Wb", [128, 128])

            def bdwrite(fn, dst):
                fn(dst[0:64, :, 0:64], 0, 64)
                fn(dst[64:128, :, 64:128], 64, 128)

            def v3(x, lo, hi):
                return x[lo:hi, :].re("p (c t) -> p c t", t=64)

            for w_ in range(T // W):
                wsl = slice(w_ * W, (w_ + 1) * W)
                if w_ == 0:
                    k.memset(uin[:, :, 0:1], 0.0, eng="dve")
                else:
                    k.copy(uin[:, :, 0:1], ucar)
                k.dma("sp", uin[:, :, 1:], V(PMd.ap[off["dr"] * 128:(off["dr"] + NU) * 128, wsl].rearrange("(c p) t -> p c t", p=128),
                                             sum([PMb[off["dr"] + b].bufs for b in range(NU)], [])))
                k.tt(dd, uin[:, :, 0:W], uin[:, :, 1:1 + W], ALU.subtract)
                k.tt(dd, dd, V(vc("rw_mu", 0, NU).ap.unsqueeze(2).to_broadcast([128, NU, W]), vec.bufs), ALU.mult)
                k.tt(ul, dd, uin[:, :, 1:1 + W], ALU.add)
                k.copy(ucar, uin[:, :, W:W + 1])
                k.actv(twt[0:64, :], ul[0:64, 3 * BC, :], AF.Tanh)
                k.actv(sgt, ul[:, 3 * BC + 1, :], AF.Sigmoid)
                for b in range(BC):
                    bs = slice(b * 128, (b + 1) * 128)
                    r_, k_, v_ = ul[:, b, :], ul[:, BC + b, :], ul[:, 2 * BC + b, :]
                    ps = PS.next()
                    k.mm(ps[:, 0:W], wa2[0:64, bs], twt[0:64, :])
                    k.actv(lwn[b], ps[:, 0:W], AF.Sigmoid, bias=vc("rw_w0", b))
                    ps = PS.next()
                    k.mm(ps[:, 0:W], wa2[64:128, bs], ul[64:128, 3 * BC, :])
                    k.actv(al[b], ps[:, 0:W], AF.Sigmoid, bias=vc("rw_a0", b))
                    ps = PS.next()
                    k.mm(ps[:, 0:W], g2[:, bs], sgt)
                    k.copy(gg[b], ps[:, 0:W])
                    k.ts(t1, k_, vc("rw_kk", b), None, ALU.mult)
                    k.tt(t2, t1, t1, ALU.mult)
                    ps = PS.next()
                    k.mm(ps[:, 0:W], bo64, t2)
                    k.actv(t2, ps[:, 0:W], AF.Sqrt)
                    k.ts(t2, t2, 1e-12, None, ALU.max)
                    k.recip(t2, t2)
                    k.tt(kkn[b], t1, t2, ALU.mult)
                    k.ts(t1, al[b], -1.0, vc("rw_ka", b), ALU.add, ALU.mult)
                    k.ts(t1, t1, 1.0, None, ALU.add)
                    k.tt(kp[b], k_, t1, ALU.mult)
                    k.stt(t1, r_, vc("rw_rk", b), kp[b], ALU.mult, ALU.mult)
                    ps = PS.next()
                    k.mm(ps[:, 0:W], bo64, t1)
                    k.tt(bon[b], ps[:, 0:W], v_, ALU.mult)
                    k.scan(gc[b], cst[:, co["reset64"]:co["reset64"] + W], lwn[b], 0.0, ALU.mult, ALU.add)
                    k.actv(e1, gc[b], AF.Exp, scale=-C0)
                    k.actv(e2, gc[b], AF.Exp, scale=C0)
                    k.tt(t1, gc[b], lwn[b], ALU.subtract)
                    k.actv(e3, t1, AF.Exp, scale=-C0)
                    g3 = gc[b].re("p (c t) -> p c t", t=64)
                    k.tt(t1.re("p (c t) -> p c t", t=64), V(g3.ap[:, :, 63:64].to_broadcast([128, NCH, 64]), gc[b].bufs), g3, ALU.subtract)
                    k.actv(e4, t1, AF.Exp, scale=-C0)
                    k.copy(gamc[b], e1.re("p (c t) -> p c t", t=64)[:, :, 63])
                    k.tt(qb, kkn[b], al[b], ALU.mult)
                    bd = BD[b]
                    bdwrite(lambda o, lo, hi: k.stt(o, v3(kkn[b], lo, hi), -1.0, v3(e3, lo, hi), ALU.mult, ALU.mult), bd["P"])
                    bdwrite(lambda o, lo, hi: k.tt(o, v3(qb, lo, hi), v3(e2, lo, hi), ALU.mult), bd["Q"])
                    bdwrite(lambda o, lo, hi: k.tt(o, v3(kp[b], lo, hi), v3(e2, lo, hi), ALU.mult), bd["K"])
                    bdwrite(lambda o, lo, hi: k.tt(o, v3(r_, lo, hi), v3(e1, lo, hi), ALU.mult), bd["R"])
                    bdwrite(lambda o, lo, hi: k.tt(o, v3(qb, lo, hi), v3(e4, lo, hi), ALU.mult), bd["Qh"])
                    bdwrite(lambda o, lo, hi: k.tt(o, v3(kp[b], lo, hi), v3(e4, lo, hi), ALU.mult), bd["Kh"])
                    bdwrite(lambda o, lo, hi: k.copy(o, v3(v_, lo, hi)), bd["V"])
                for c in range(NCH):
                    Q = 128

                    def pq():
                        return PS.next()[:, 0:Q]
                    for b in range(BC):
                        bd = BD[b]
                        for (dst, lt, rh, msk) in ((NTt[b], "Q", "P", m_su), (Nt[b], "P", "Q", m_sl), (AKT[b], "K", "P", m_su),
                                                   (RBT[b], "Q", "R", m_iu), (RKT[b], "K", "R", m_iu)):
                            ps = pq()
                            k.mm(ps, bd[lt][:, c, :], bd[rh][:, c, :])
                            k.tt(dst, ps, msk, ALU.mult)
                    for b in range(BC):
                        bd = BD[b]
                        for (dst, src) in ((Vtm[b], "V"), (Qhtm[b], "Qh"), (Khtm[b], "Kh")):
                            ps = pq()
                            k.tr(ps, bd[src][:, c, :], ident)
                            evac(dst, ps)
                    for lev in range(1, 6):
                        for b in range(BC):
                            mt_prev = NTt[b] if lev == 1 else MT[lev - 2][b]
                            m_prev = Nt[b] if lev == 1 else (Ma[b] if lev % 2 == 0 else Mb[b])
                            m_new = Ma[b] if lev % 2 == 1 else Mb[b]
                            ps = pq()
                            k.mm(ps, m_prev, mt_prev)
                            evac(MT[lev - 1][b], ps)
                            if lev < 5:
                                ps = pq()
                                k.mm(ps, mt_prev, m_prev)
                                evac(m_new, ps)
                    for b in range(BC):
                        bd = BD[b]
                        ps = pq()
                        k.mm(ps, bd["P"][:, c, :], Hs[b], start=True, stop=False, signal=False)
                        k.mm(ps, AKT[b], Vtm[b], start=False, stop=True)
                        evac(Wa[b], ps)
                    for lev in range(6):
                        for b in range(BC):
                            mt = NTt[b] if lev == 0 else MT[lev - 1][b]
                            wc = Wa[b] if lev % 2 == 0 else Wb[b]
                            wn = Wb[b] if lev % 2 == 0 else Wa[b]
                            ps = pq()
                            k.mm(ps, mt, wc)
                            k.tt(wn, wc, ps, ALU.add)
                    for b in range(BC):
                        bd = BD[b]
                        ps = pq()
                        k.mm(ps, Hs[b], bd["R"][:, c, :], start=True, stop=False, signal=False)
                        k.mm(ps, Wa[b], RBT[b], start=False, stop=False, signal=False)
                        k.mm(ps, Vtm[b], RKT[b], start=False, stop=True)
                        k.actv(yw[b][0:64, c * 64:(c + 1) * 64], ps[0:64, 0:64], AF.Copy)
                        k.copy(yw[b][64:128, c * 64:(c + 1) * 64], ps[64:128, 64:128])
                    for b in range(BC):
                        ps = pq()
                        k.mm(ps, Qhtm[b], Wa[b], start=True, stop=False, signal=False)
                        k.mm(ps, Khtm[b], Vtm[b], start=False, stop=True)
                        k.stt(Hs[b], Hs[b], gamc[b][:, c:c + 1], ps, ALU.mult, ALU.add)
                for b in range(BC):
                    ps = PS.next()
                    k.mm(ps[:, 0:W], bo64, yw[b])
                    k.stt(t1, ps[:, 0:W], -1.0 / 64, yw[b], ALU.mult, ALU.add)
                    k.tt(t2, t1, t1, ALU.mult)
                    ps = PS.next()
                    k.mm(ps[:, 0:W], bo64, t2)
                    k.actv(t2, ps[:, 0:W], AF.Sqrt, bias=lneps, scale=1.0 / 64)
                    k.recip(t2, t2)
                    k.tt(t1, t1, t2, ALU.mult)
                    k.ts(t1, t1, vc("rw_lnw", b), vc("rw_lnb", b), ALU.mult, ALU.add)
                    k.tt(t1, t1, bon[b], ALU.add)
                    k.tt(t1, t1, gg[b], ALU.mult)
                    k.dma("act", V(Yd.ap[3 * BW + b * 128:3 * BW + (b + 1) * 128, wsl], Yb[3 * BC + b].bufs), t1)
        if upto == "B4":
            break

        FGS = cfg.FGS
        with k.scope():
            wring = wring_make(4, WMAX)
            h = k.sb("C_h", [128, KC, NT])
            xn = k.sb("C_xn", [128, KC, NT], F32R)
            ybuf = k.sb("C_y", [128, max(KC, FGS), NT], F32R)
            mg = k.sb("C_mg", [128, KC, NT], F32R)
            rs = k.sb("C_rs", [128, NT])
            sigs = Ring([k.sb("C_sig%d" % i, [128, NT]) for i in range(2)])
            tmps = Ring([k.sb("C_tmp%d" % i, [128, NT]) for i in range(2)])
            mgacc = k.sb("C_mgacc", [128, NT])
            ptile = k.sb("C_pt", [128, 2, NT], F32R)
            wbring = Ring([k.sb("C_wb%d" % i, [128, BC * 128], F32R) for i in range(2)])
            wpr = Ring([k.sb("C_wple%d" % i, [128, 256], F32R) for i in range(2)])
            ucar = k.sb("C_ucar", [128, 2 * FB, 2])
            ust = [k.sb("C_ust%d" % i, [128, 2 + NT]) for i in range(2)]
            ucv = [k.sb("C_ucv%d" % i, [128, NT]) for i in range(2)]
            for tt in range(NTT):
                tsl = slice(tt * NT, (tt + 1) * NT)
                k.dma("sp", h, V(hsrc.ap[:, tsl].rearrange("(c p) t -> p c t", p=128), hsrc_t[tt].bufs))
                k.dma("pool", ybuf[:, 0:KC, :], V(Yd.ap[:, tsl].rearrange("(c p) t -> p c t", p=128), sum([y_.bufs for y_ in Yb], [])))
                k.dma("pool", ptile, pT[l][:, tsl].re("(c p) t -> p c t", p=128))
                rmsnorm(h, xn, "mixn", rs)
                for dc in range(KC):
                    for j in range(4):
                        wb = wbring.next()
                        k.dma("pool", wb, wbr[l, dc][:, j * BC * 128:(j + 1) * BC * 128])
                        wg = wring.next()
                        k.dma("pool", wg[:, :KC * 128], wgate[l, dc * 4 + j])
                        gps = PS.next()
                        for c in range(KC):
                            k.mm(gps, wg[:, c * 128:(c + 1) * 128], xn[:, c, :], start=(c == 0), stop=(c == KC - 1), signal=(c == KC - 1))
                        zps = PS.next()
                        for c in range(BC):
                            k.mm(zps, wb[:, c * 128:(c + 1) * 128], ybuf[:, j * BC + c, :], start=(c == 0), stop=(c == BC - 1),
                                 signal=(c == BC - 1))
                        sg_ = sigs.next()
                        k.actv(sg_, gps, AF.Sigmoid)
                        if j == 0:
                            k.tt(mgacc, sg_, zps, ALU.mult)
                        else:
                            tp = tmps.next()
                            k.tt(tp, sg_, zps, ALU.mult)
                            k.tt(mg[:, dc, :] if j == 3 else mgacc, mgacc, tp, ALU.add)
                for dc in range(KC):
                    w = wring.next()
                    k.dma("pool", w[:, :KC * 128], wout[l, dc])
                    ps = PS.next()
                    for c in range(KC):
                        k.mm(ps, w[:, c * 128:(c + 1) * 128], mg[:, c, :], start=(c == 0), stop=(c == KC - 1), signal=(c == KC - 1))
                    k.tt(h[:, dc, :], h[:, dc, :], ps, ALU.add)
                rmsnorm(h, xn, "ffnn", rs)
                for fg in range(FB // FGS):
                    for fi in range(FGS):
                        fb = fg * FGS + fi
                        for half, blk in ((0, fb), (1, FB + fb)):
                            w = wring.next()
                            k.dma("pool", w[:, :KC * 128], wup[l, blk])
                            ps = PS.next()
                            for c in range(KC):
                                k.mm(ps, w[:, c * 128:(c + 1) * 128], xn[:, c, :], start=(c == 0), stop=(c == KC - 1), signal=(c == KC - 1))
                            u_ = ust[half]
                            if tt == 0:
                                k.memset(u_[:, 0:2], 0.0, eng="dve")
                            else:
                                k.copy(u_[:, 0:2], ucar[:, blk, :])
                            k.actv(u_[:, 2:2 + NT], ps, AF.Copy)
                            k.copy(ucar[:, blk, :], u_[:, NT:NT + 2])
                            uc = ucv[half]
                            k.actv(uc, ps, AF.Identity, bias=vc("ffn_cb", blk), scale=vc("ffn_cw", 2 * 2 * FB + blk))
                            k.stt(uc, u_[:, 1:1 + NT], vc("ffn_cw", 1 * 2 * FB + blk), uc, ALU.mult, ALU.add)
                            k.stt(uc, u_[:, 0:NT], vc("ffn_cw", 0 * 2 * FB + blk), uc, ALU.mult, ALU.add)
                        k.actv(ucv[0], ucv[0], AF.Gelu_apprx_tanh)
                        k.tt(ybuf[:, fi, :], ucv[0], ucv[1], ALU.mult)
                    for dc in range(KC):
                        w = wring.next()
                        k.dma("pool", w[:, :FGS * 128], wdown[l, dc][:, fg * FGS * 128:(fg + 1) * FGS * 128])
                        ps = PS.next()
                        for fi in range(FGS):
                            k.mm(ps, w[:, fi * 128:(fi + 1) * 128], ybuf[:, fi, :], start=(fi == 0), stop=(fi == FGS - 1), signal=(fi == FGS - 1))
                        k.tt(h[:, dc, :], h[:, dc, :], ps, ALU.add)
                rmsnorm(h, xn, "plen", rs)
                for dc in range(KC):
                    w = wring.next()
                    k.dma("pool", w[:, :KC * 128], wpg[l, dc])
                    gps = PS.next()
                    for c in range(KC):
                        k.mm(gps, w[:, c * 128:(c + 1) * 128], xn[:, c, :], start=(c == 0), stop=(c == KC - 1), signal=(c == KC - 1))
                    eps_ = PS.next()
                    wp = wpr.next()
                    k.dma("pool", wp, wple[l, dc])
                    for c in range(2):
                        k.mm(eps_, wp[:, c * 128:(c + 1) * 128], ptile[:, c, :], start=(c == 0), stop=(c == 1), signal=(c == 1))
                    sg_ = sigs.next()
                    k.actv(sg_, gps, AF.Sigmoid)
                    tp = tmps.next()
                    k.tt(tp, sg_, eps_, ALU.mult)
                    k.tt(h[:, dc, :], h[:, dc, :], tp, ALU.add)
                k.dma("act", V(hdst.ap[:, tsl].rearrange("(c p) t -> p c t", p=128), hdst_t[tt].bufs), h)
    k.finish()
    return nc


FULL = Cfg(D=2048, T=8192, L=4, DFF=5632, FGS=11)
_cache = {}


def kernel(**inputs):
    cfg = FULL
    B = inputs["x"].shape[0]
    if "nc" not in _cache:
        _cache["nc"] = build(cfg)
    nc = _cache["nc"]
    in_maps = [host_pack(cfg, inputs, c % B) for c in range(8)]
    res = run_bass_kernel_spmd(nc, in_maps, core_ids=list(range(8)))
    out = np.stack([np.ascontiguousarray(res.results[b]["out"].T) for b in range(B)], axis=0)
    return out.astype(np.float32)
```

```python
import contextlib
import numpy as np
import concourse.bass as bass
import concourse.mybir as mybir

F32 = mybir.dt.float32
F32R = mybir.dt.float32r
AF = mybir.ActivationFunctionType
ALU = mybir.AluOpType
AX = mybir.AxisListType


class Buf:
    __slots__ = ("name", "t", "w", "r")

    def __init__(self, name, t):
        self.name = name
        self.t = t
        self.w = []
        self.r = []


def _box(ap):
    a = ap.ap
    if a[0][0] <= 0:
        return (0, 1 << 30, 0, 1 << 40)
    pitch = a[0][0]
    off = ap.offset
    p0 = off // pitch
    e0 = off % pitch
    ext = 0
    for st, cnt in a[1:]:
        ext += abs(st) * (cnt - 1)
    return (p0, p0 + a[0][1], e0, e0 + ext + 1)


def _ovl(a, b):
    return a[0] < b[1] and b[0] < a[1] and a[2] < b[3] and b[2] < a[3]


def _inside(a, b):
    return a[0] >= b[0] and a[1] <= b[1] and a[2] >= b[2] and a[3] <= b[3]


def _collapse(lst):
    d = {}
    bb = None
    for bx, (sm, v) in lst:
        if d.get(sm, 0) < v:
            d[sm] = v
        bb = bx if bb is None else (min(bb[0], bx[0]), max(bb[1], bx[1]), min(bb[2], bx[2]), max(bb[3], bx[3]))
    return [(bb, (sm, v)) for sm, v in d.items()]


class V:
    __slots__ = ("ap", "bufs", "_bx")

    def __init__(self, ap, bufs):
        self.ap = ap
        self.bufs = bufs
        self._bx = None

    @property
    def box(self):
        if self._bx is None:
            self._bx = _box(self.ap)
        return self._bx

    def __getitem__(self, k):
        return V(self.ap[k], self.bufs)

    def r(self):
        return V(self.ap.bitcast(F32R), self.bufs)

    def f(self):
        return V(self.ap.bitcast(F32), self.bufs)

    def re(self, s, **kw):
        return V(self.ap.rearrange(s, **kw), self.bufs)

    def bc(self, shape):
        return V(self.ap.to_broadcast(shape), self.bufs)


class K:
    ENG = ("pe", "dve", "act", "pool", "sp")

    def __init__(self, nc, same_engine_sync=True, ndma_sems=12):
        self.nc = nc
        self.es = contextlib.ExitStack()
        self.eng = {"pe": nc.tensor, "dve": nc.vector, "act": nc.scalar, "pool": nc.gpsimd, "sp": nc.sync}
        self.ops = {e: [] for e in self.ENG}
        self.csem = {e: self.es.enter_context(nc.semaphore("c_" + e)) for e in self.ENG}
        self.ccnt = {e: 0 for e in self.ENG}
        self.known = {e: {} for e in self.ENG}
        self.same = same_engine_sync
        self.dsem = {}
        self.dval = {}
        self.dptr = {}
        for q in ("sp", "pool", "act"):
            self.dsem[q] = [self.es.enter_context(nc.semaphore("d_%s%d" % (q, i))) for i in range(ndma_sems)]
            self.dval[q] = [0] * ndma_sems
            self.dptr[q] = 0
        self.n_ops = 0
        self.sb_used = 0

    def sb(self, name, shape, dtype=F32):
        self.n_ops += 1
        name = "%s_u%d" % (name, self.n_ops)
        t = self.es.enter_context(self.nc.sbuf_tensor(name, list(shape), dtype))
        b = Buf(name, t)
        return V(t[:] if hasattr(t, "__getitem__") else t.ap(), [b])

    def ps(self, name, shape, dtype=F32):
        t = self.es.enter_context(self.nc.psum_tensor(name, list(shape), dtype))
        b = Buf(name, t)
        return V(t[:], [b])

    def dram(self, name, shape, dtype=F32, kind="Internal"):
        t = self.nc.dram_tensor(name, list(shape), dtype, kind=kind)
        b = Buf(name, t)
        return V(t.ap(), [b])

    def alias(self, v, name):
        return V(v.ap, [Buf(name, None)])

    def _waits(self, e, reads, writes):
        need = {}

        def add(ev):
            sm, v = ev
            if need.get(sm, 0) < v:
                need[sm] = v

        for vw in reads:
            bx = vw.box
            for b in vw.bufs:
                for wb, ev in b.w:
                    if _ovl(bx, wb):
                        add(ev)
        for vw in writes:
            bx = vw.box
            for b in vw.bufs:
                for wb, ev in b.w:
                    if _ovl(bx, wb):
                        add(ev)
                for rb, ev in b.r:
                    if _ovl(bx, rb):
                        add(ev)
        out = []
        kn = self.known[e]
        own = self.csem[e]
        for sm, v in need.items():
            if sm is own and (not self.same or e == "pe"):
                continue
            if kn.get(sm, 0) >= v:
                continue
            kn[sm] = v
            out.append((sm, v))
        return out

    def _mark(self, ev, reads, writes):
        for vw in reads:
            bx = vw.box
            for b in vw.bufs:
                b.r.append((bx, ev))
                if len(b.r) > 40:
                    b.r = _collapse(b.r)
        for vw in writes:
            bx = vw.box
            for b in vw.bufs:
                b.w = [x for x in b.w if not _inside(x[0], bx)]
                b.r = [x for x in b.r if not _inside(x[0], bx)]
                b.w.append((bx, ev))
                if len(b.w) > 40:
                    b.w = _collapse(b.w)

    def op(self, e, fn, reads, writes, signal=True):
        waits = self._waits(e, reads, writes)
        sem = self.csem[e]
        ev = (sem, self.ccnt[e] + 1)
        if signal:
            self.ccnt[e] += 1
        self._mark(ev, reads, writes)
        self.n_ops += 1

        eng = self.eng[e]
        for s, v in waits:
            eng.wait_ge(s, v)
        ins = fn(eng)
        if signal:
            ins.then_inc(sem, 1)

    def dma(self, q, out, in_, **kw):
        i = self.dptr[q]
        self.dptr[q] = (i + 1) % len(self.dsem[q])
        sem = self.dsem[q][i]
        prev = self.dval[q][i]
        waits = self._waits(q, [in_], [out])
        kn = self.known[q]
        if prev > 0 and kn.get(sem, 0) < prev:
            kn[sem] = prev
            waits.append((sem, prev))
        val = prev + 16
        self.dval[q][i] = val
        ev = (sem, val)
        self._mark(ev, [in_], [out])
        self.n_ops += 1
        oap, iap = out.ap, in_.ap

        eng = self.eng[q]
        for s, v in waits:
            eng.wait_ge(s, v)
        eng.dma_start(out=oap, in_=iap, **kw).then_inc(sem, 16)
        return ev

    def cc_allgather(self, out, in_, groups):
        q = "pool"
        i = self.dptr[q]
        self.dptr[q] = (i + 1) % len(self.dsem[q])
        sem = self.dsem[q][i]
        prev = self.dval[q][i]
        waits = self._waits(q, [in_], [out])
        kn = self.known[q]
        if prev > 0 and kn.get(sem, 0) < prev:
            kn[sem] = prev
            waits.append((sem, prev))
        val = prev + 16
        self.dval[q][i] = val
        ev = (sem, val)
        self._mark(ev, [in_], [out])
        eng = self.eng[q]
        for s_, v in waits:
            eng.wait_ge(s_, v)
        eng.collective_compute("AllGather", ALU.bypass, replica_groups=groups, ins=[in_.ap], outs=[out.ap]).then_inc(sem, 16)
        return ev

    def barrier(self):
        evs = [(self.csem[x], self.ccnt[x]) for x in self.ENG if self.ccnt[x] > 0]
        for q in self.dsem:
            for s, v in zip(self.dsem[q], self.dval[q]):
                if v > 0:
                    evs.append((s, v))
        for e in self.ENG:
            kn = self.known[e]
            for s, v in evs:
                if s is self.csem[e]:
                    continue
                if kn.get(s, 0) < v:
                    kn[s] = v
                    self.eng[e].wait_ge(s, v)

    def scope(self):
        k = self

        class _S:
            def __enter__(s):
                s.old = k.es
                k.es = contextlib.ExitStack()
                return s

            def __exit__(s, *a):
                k.barrier()
                k.es.close()
                k.es = s.old
                return False
        return _S()

    def finish(self):
        self.barrier()
        self.es.close()

    def mm(self, out, lhsT, rhs, start=True, stop=True, signal=True):
        self.op("pe", lambda e: e.matmul(out.ap, lhsT.ap, rhs.ap, start=start, stop=stop),
                [lhsT, rhs] + ([] if start else [out]), [out], signal=signal)

    def tr(self, out, in_, ident):
        self.op("pe", lambda e: e.transpose(out.ap, in_.ap, ident.ap), [in_, ident], [out])

    def actv(self, out, in_, func, bias=0.0, scale=1.0, eng="act"):
        rd = [in_]
        b = bias
        s = scale
        if isinstance(bias, V):
            rd.append(bias)
            b = bias.ap
        if isinstance(scale, V):
            rd.append(scale)
            s = scale.ap
        self.op("act", lambda e: e.activation(out.ap, in_.ap, func, bias=b, scale=s), rd, [out])

    def tt(self, out, a, b, op, eng="dve"):
        self.op(eng, lambda e: e.tensor_tensor(out.ap, a.ap, b.ap, op), [a, b], [out])

    def ts(self, out, a, s1, s2, op0, op1=None, eng="dve"):
        rd = [a]
        x1, x2 = s1, s2
        if isinstance(s1, V):
            rd.append(s1)
            x1 = s1.ap
        if isinstance(s2, V):
            rd.append(s2)
            x2 = s2.ap
        if op1 is None:
            self.op(eng, lambda e: e.tensor_scalar(out.ap, a.ap, x1, None, op0), rd, [out])
        else:
            self.op(eng, lambda e: e.tensor_scalar(out.ap, a.ap, x1, x2, op0, op1), rd, [out])

    def stt(self, out, a, s, b, op0, op1):
        rd = [a, b]
        x = s
        if isinstance(s, V):
            rd.append(s)
            x = s.ap
        self.op("dve", lambda e: e.scalar_tensor_tensor(out.ap, a.ap, x, b.ap, op0, op1), rd, [out])

    def scan(self, out, d0, d1, init, op0, op1):
        rd = [d0, d1]
        x = init
        if isinstance(init, V):
            rd.append(init)
            x = init.ap
        self.op("dve", lambda e: e.tensor_tensor_scan(out.ap, d0.ap, d1.ap, x, op0, op1), rd, [out])

    def copy(self, out, in_, eng="dve"):
        if eng == "act":
            self.op("act", lambda e: e.copy(out.ap, in_.ap), [in_], [out])
        else:
            self.op(eng, lambda e: e.tensor_copy(out.ap, in_.ap), [in_], [out])

    def memset(self, out, val, eng="pool"):
        self.op(eng, lambda e: e.memset(out.ap, val), [], [out])

    def recip(self, out, in_):
        self.op("dve", lambda e: e.reciprocal(out.ap, in_.ap), [in_], [out])

from concourse.bass_utils import run_bass_kernel_spmd

EPS = 1e-6
RW_LN_EPS = 64e-5
NEG = -1.0e30


class Cfg:
    def __init__(self, D, T, L, DFF, FGS, dbg=False):
        self.D, self.T, self.L, self.DFF = D, T, L, DFF
        self.KC = D // 128
        self.BW = D // 4
        self.BC = self.BW // 128
        self.FH = self.BW // 128
        self.SH = self.BW // 64
        self.G = 2
        self.RH = self.BW // 64
        self.FB = DFF // 128
        self.FGS = FGS
        assert self.FB % FGS == 0
        self.NT = 512
        self.NTT = T // 512
        BC, G = self.BC, self.G
        o = {}
        o["ax"] = 0; o["ag"] = BC; o["bq"] = 2 * BC; o["bk"] = 3 * BC; o["bv"] = 4 * BC
        o["cz"] = 5 * BC; o["cx"] = 6 * BC; o["cb"] = 7 * BC; o["cc"] = 7 * BC + G
        o["dr"] = 7 * BC + 2 * G; o["dk"] = o["dr"] + BC; o["dv"] = o["dk"] + BC
        o["dwa"] = o["dv"] + BC; o["dgl"] = o["dwa"] + 1; o["small"] = o["dgl"] + 1
        self.off = o
        self.NBLK = o["small"] + 1
        self.NXBC = BC + 2 * G
        self.NU = 3 * BC + 2
        v = {}
        n = 0

        def add(name, c):
            nonlocal n
            v[name] = n
            n += c
        KC = self.KC
        add("mixn", KC); add("ffnn", KC); add("plen", KC)
        add("lru_cw", 4 * BC); add("lru_cb", BC); add("lru_br", BC); add("lru_bi", BC); add("lru_lam", BC)
        add("fox_qg", 1); add("fox_kg", 1)
        add("sm_sign", 1); add("sm_bias", 1); add("ssd_alog", 1)
        add("ssd_cw", 4 * self.NXBC); add("ssd_cb", self.NXBC); add("ssd_d", BC); add("ssd_nw", BC)
        add("rw_mu", self.NU); add("rw_w0", BC); add("rw_a0", BC); add("rw_kk", BC); add("rw_ka", BC)
        add("rw_rk", BC); add("rw_lnw", BC); add("rw_lnb", BC)
        add("ffn_cw", 3 * 2 * self.FB); add("ffn_cb", 2 * self.FB)
        self.vec = v
        self.NV = n
        c = {}
        n = 0

        def addc(name, cc):
            nonlocal n
            c[name] = n
            n += cc
        addc("ident", 128); addc("ones", 128); addc("mneg", 128); addc("m_su", 128); addc("m_iu", 128)
        addc("m_sl", 128); addc("bo64", 128); addc("reset64", 512); addc("reset128", 512)
        addc("sel", self.SH * 128)
        self.cst = c
        self.NCST = n
        self.dbg = dbg


def colvec(v):
    v = np.asarray(v, np.float32).reshape(-1, 128)
    return v.T


def pack_w(W):
    Kd, N = W.shape
    return np.ascontiguousarray(W.reshape(Kd // 128, 128, N // 128, 128).transpose(2, 1, 0, 3))


def host_consts(cfg):
    c = np.zeros((128, cfg.NCST), np.float32)
    o = cfg.cst
    p = np.arange(128)[:, None]
    f = np.arange(128)[None, :]
    c[:, o["ident"]:o["ident"] + 128] = (p == f)
    c[:, o["ones"]:o["ones"] + 128] = 1.0
    c[:, o["mneg"]:o["mneg"] + 128] = np.where(p <= f, 0.0, NEG)
    c[:, o["m_su"]:o["m_su"] + 128] = (p < f)
    c[:, o["m_iu"]:o["m_iu"] + 128] = (p <= f)
    c[:, o["m_sl"]:o["m_sl"] + 128] = (p > f)
    c[:, o["bo64"]:o["bo64"] + 128] = ((p // 64) == (f // 64))
    t = np.arange(512)[None, :]
    c[:, o["reset64"]:o["reset64"] + 512] = (t % 64 != 0)
    c[:, o["reset128"]:o["reset128"] + 512] = (t % 128 != 0)
    for h in range(cfg.SH):
        c[cfg.FH + h, o["sel"] + h * 128: o["sel"] + (h + 1) * 128] = 1.0
    return c


def host_pack(cfg, inp, b):
    D, T, L, BW, BC, G = cfg.D, cfg.T, cfg.L, cfg.BW, cfg.BC, cfg.G
    KC, FB = cfg.KC, cfg.FB
    f32 = np.float32
    m = {}
    m["xT"] = np.ascontiguousarray(np.asarray(inp["x"][b], f32).T)
    m["pT"] = np.ascontiguousarray(np.asarray(inp["p"][:, b], f32).transpose(0, 2, 1))
    m["cst"] = host_consts(cfg)
    w_in = np.asarray(inp["w_in"], f32)
    base = 4 * D
    cols = []
    offs = {}
    cur = base
    for name, wd in (("ax", BW), ("ag", BW), ("bq", BW), ("bk", BW), ("bv", BW), ("bf", cfg.FH), ("cz", BW),
                     ("cxbc", BW + 2 * G * 128), ("cdt", cfg.SH), ("du", 3 * BW + 256)):
        offs[name] = cur
        cur += wd
    assert cur == w_in.shape[2]
    idx = []
    for name, wd in (("ax", BW), ("ag", BW), ("bq", BW), ("bk", BW), ("bv", BW), ("cz", BW),
                     ("cxbc", BW + 2 * G * 128), ("du", 3 * BW + 256)):
        idx += list(range(offs[name], offs[name] + wd))
    small = list(range(offs["bf"], offs["bf"] + cfg.FH)) + list(range(offs["cdt"], offs["cdt"] + cfg.SH))
    nsm = len(small)
    wmix = np.zeros((L, cfg.NBLK, 128, KC, 128), f32)
    wgate = np.zeros((L, KC * 4, 128, KC, 128), f32)
    for l in range(L):
        W = w_in[l]
        Wm = np.zeros((D, cfg.NBLK * 128), f32)
        Wm[:, :len(idx)] = W[:, idx]
        Wm[:, (cfg.NBLK - 1) * 128:(cfg.NBLK - 1) * 128 + nsm] = W[:, small]
        wmix[l] = pack_w(Wm)
        Wg = W[:, :4 * D].reshape(D, 4, KC, 128).transpose(0, 2, 1, 3).reshape(D, 4 * D)
        wgate[l] = pack_w(Wg)
    m["wmix"] = wmix
    m["wgate"] = wgate
    wb = np.asarray(inp["w_branch"], f32)
    m["wbr"] = np.ascontiguousarray(wb.reshape(L, 4, BC, 128, KC, 128).transpose(0, 4, 3, 1, 2, 5))
    m["wout"] = np.stack([pack_w(np.asarray(inp["w_out"][l], f32)) for l in range(L)])
    wu = np.asarray(inp["w_up"], f32)
    m["wup"] = np.stack([pack_w(wu[l]) for l in range(L)])
    wd_ = np.asarray(inp["w_down"], f32)
    m["wdown"] = np.stack([pack_w(wd_[l]) for l in range(L)])
    m["wpg"] = np.stack([pack_w(np.asarray(inp["w_ple_gate"][l], f32)) for l in range(L)])
    m["wple"] = np.stack([pack_w(np.asarray(inp["w_ple"][l], f32)) for l in range(L)])
    wr = np.asarray(inp["lru_wr"], f32)
    wi = np.asarray(inp["lru_wi"], f32)
    bd = np.zeros((L, 2, BC, 128, 128), f32)
    for l in range(L):
        for cc in range(BC):
            for bb in range(2):
                bd[l, 0, cc, bb * 64:(bb + 1) * 64, bb * 64:(bb + 1) * 64] = wr[l, 2 * cc + bb]
                bd[l, 1, cc, bb * 64:(bb + 1) * 64, bb * 64:(bb + 1) * 64] = wi[l, 2 * cc + bb]
    m["lrubd"] = bd
    m["rwwa2"] = np.concatenate([np.asarray(inp["rwkv_w2"], f32), np.asarray(inp["rwkv_a2"], f32)], axis=1)
    m["rwg2"] = np.asarray(inp["rwkv_g2"], f32)
    vec = np.zeros((L, 128, cfg.NV), f32)
    vo = cfg.vec
    for l in range(L):
        def put(name, arr):
            a = colvec(arr)
            vec[l, :, vo[name]:vo[name] + a.shape[1]] = a
        put("mixn", inp["mix_norm"][l]); put("ffnn", inp["ffn_norm"][l]); put("plen", inp["ple_norm"][l])
        cw = np.asarray(inp["lru_conv_w"][l], f32)
        put("lru_cw", cw.reshape(-1))
        put("lru_cb", inp["lru_conv_b"][l]); put("lru_br", inp["lru_br"][l]); put("lru_bi", inp["lru_bi"][l])
        put("lru_lam", inp["lru_lambda"][l])
        put("fox_qg", inp["fox_q_gain"][l]); put("fox_kg", inp["fox_k_gain"][l])
        sg = np.zeros(128, f32); sg[:cfg.FH] = -1.0; sg[cfg.FH:cfg.FH + cfg.SH] = 1.0
        put("sm_sign", sg)
        sbias = np.zeros(128, f32); sbias[:cfg.FH] = np.asarray(inp["fox_fbias"][l]); sbias[cfg.FH:cfg.FH + cfg.SH] = np.asarray(inp["ssd_dt_bias"][l])
        put("sm_bias", sbias)
        al = np.zeros(128, f32); al[cfg.FH:cfg.FH + cfg.SH] = np.asarray(inp["ssd_a_log"][l])
        put("ssd_alog", al)
        put("ssd_cw", np.asarray(inp["ssd_conv_w"][l], f32).reshape(-1)); put("ssd_cb", inp["ssd_conv_b"][l])
        put("ssd_d", np.repeat(np.asarray(inp["ssd_d"][l], f32), 64)); put("ssd_nw", inp["ssd_norm"][l])
        put("rw_mu", inp["rwkv_mu"][l]); put("rw_w0", inp["rwkv_w0"][l]); put("rw_a0", inp["rwkv_a0"][l])
        put("rw_kk", inp["rwkv_kk"][l]); put("rw_ka", inp["rwkv_ka"][l]); put("rw_rk", np.asarray(inp["rwkv_rk"][l], f32).reshape(-1))
        put("rw_lnw", inp["rwkv_ln_w"][l]); put("rw_lnb", inp["rwkv_ln_b"][l])
        put("ffn_cw", np.asarray(inp["ffn_conv_w"][l], f32).reshape(-1))
        put("ffn_cb", inp["ffn_conv_b"][l])
    m["vec"] = vec
    return m


class Ring:
    def __init__(self, items):
        self.items = items
        self.i = 0

    def next(self):
        v = self.items[self.i]
        self.i = (self.i + 1) % len(self.items)
        return v


def build(cfg, upto="all"):
    nc = bass.Bass("TRN2", target_bir_lowering=False)
    k = K(nc)
    D, T, L, BW, BC, G, KC, FB, NT, NTT = cfg.D, cfg.T, cfg.L, cfg.BW, cfg.BC, cfg.G, cfg.KC, cfg.FB, cfg.NT, cfg.NTT
    FH, SH, RH = cfg.FH, cfg.SH, cfg.RH
    off, vo, co = cfg.off, cfg.vec, cfg.cst
    NBLK = cfg.NBLK
    EI = "ExternalInput"
    xT = k.dram("xT", [D, T], kind=EI)
    pT = k.dram("pT", [L, 256, T], kind=EI)
    cstd = k.dram("cst", [128, cfg.NCST], kind=EI)
    wmix = k.dram("wmix", [L, NBLK, 128, KC * 128], kind=EI)
    wgate = k.dram("wgate", [L, KC * 4, 128, KC * 128], kind=EI)
    wbr = k.dram("wbr", [L, KC, 128, 4 * BC * 128], kind=EI)
    wout = k.dram("wout", [L, KC, 128, KC * 128], kind=EI)
    wup = k.dram("wup", [L, 2 * FB, 128, KC * 128], kind=EI)
    wdown = k.dram("wdown", [L, KC, 128, FB * 128], kind=EI)
    wpg = k.dram("wpg", [L, KC, 128, KC * 128], kind=EI)
    wple = k.dram("wple", [L, KC, 128, 2 * 128], kind=EI)
    lrubd = k.dram("lrubd", [L, 2, BC, 128, 128], kind=EI)
    rwwa2 = k.dram("rwwa2", [L, 128, BW], kind=EI)
    rwg2 = k.dram("rwg2", [L, 128, BW], kind=EI)
    vecd = k.dram("vec", [L, 128, cfg.NV], kind=EI)
    outd = k.dram("out", [D, T], kind="ExternalOutput")
    sk = "ExternalOutput" if cfg.dbg else "Internal"
    Hd = k.dram("H", [D, T], kind=sk)
    PMd = k.dram("PM", [NBLK * 128, T], kind=sk)
    Yd = k.dram("Y", [4 * BW, T], kind=sk)
    CSD = k.dram("CSD", [128, T])
    SPD = k.dram("SPD", [128, T])

    def tview(base, name, n):
        return [V(base.ap, [Buf("%s%d" % (name, i), None)]) for i in range(n)]
    Htt = tview(Hd, "Htt", NTT)
    Ott = tview(outd, "Ott", NTT)
    PMb = tview(PMd, "PMb", NBLK)
    Yb = tview(Yd, "Yb", 4 * BC)

    NC1 = co["reset64"]
    cst = k.sb("cst_sb", [128, NC1])
    k.dma("sp", cst, cstd[:, 0:NC1])
    ones_r = k.sb("ones_r", [128, 128], F32R)
    k.dma("pool", ones_r, cstd[:, co["ones"]:co["ones"] + 128])
    vec = k.sb("vec_sb", [128, cfg.NV])
    ident = cst[:, co["ident"]:co["ident"] + 128]
    ones = cst[:, co["ones"]:co["ones"] + 128]
    mneg = cst[:, co["mneg"]:co["mneg"] + 128]
    PSb = [k.ps("ps%d" % i, [128, 512]) for i in range(8)]
    PS = Ring(PSb)
    evac_flip = [0]

    def evac(out, ps):
        evac_flip[0] ^= 1
        if evac_flip[0]:
            k.actv(out, ps, AF.Copy)
        else:
            k.copy(out, ps)

    def vc(name, i=0, n=1):
        return vec[:, vo[name] + i: vo[name] + i + n]

    def wring_make(n, width):
        return Ring([k.sb("wr%d" % i, [128, width], F32R) for i in range(n)])

    def rmsnorm(h, xn, wname, rs):
        k.actv(xn, h, AF.Square)
        ps = PS.next()
        for c in range(KC):
            k.mm(ps, ones_r, xn[:, c, :], start=(c == 0), stop=(c == KC - 1), signal=(c == KC - 1))
        k.actv(rs, ps, AF.Sqrt, bias=epsD, scale=1.0 / D)
        k.recip(rs, rs)
        for c in range(KC):
            k.stt(xn[:, c, :], h[:, c, :], vc(wname, c), rs, ALU.mult, ALU.mult)

    epsD = k.sb("epsD", [128, 1])
    k.memset(epsD, EPS)
    eps128 = epsD
    lneps = k.sb("lneps", [128, 1])
    k.memset(lneps, RW_LN_EPS)
    WMAX = max(KC, 4 * BC, cfg.FGS) * 128

    for l in range(L):
        k.dma("sp", vec, vecd[l])
        hsrc = xT if l == 0 else Hd
        hsrc_t = [V(xT.ap, xT.bufs)] * NTT if l == 0 else Htt
        hdst = outd if l == L - 1 else Hd
        hdst_t = Ott if l == L - 1 else Htt

        with k.scope():
            wring = wring_make(6, WMAX)
            h = k.sb("A_h", [128, KC, NT])
            xn = k.sb("A_xn", [128, KC, NT], F32R)
            rs = k.sb("A_rs", [128, NT])
            stage = Ring([k.sb("A_st%d" % i, [128, NT]) for i in range(4)])
            for tt in range(NTT):
                tsl = slice(tt * NT, (tt + 1) * NT)
                k.dma("sp", h, V(hsrc.ap[:, tsl].rearrange("(c p) t -> p c t", p=128), hsrc_t[tt].bufs))
                rmsnorm(h, xn, "mixn", rs)
                for blk in range(NBLK):
                    wt = wring.next()
                    k.dma("pool", wt[:, :KC * 128], wmix[l, blk])
                    ps = PS.next()
                    for c in range(KC):
                        k.mm(ps, wt[:, c * 128:(c + 1) * 128], xn[:, c, :], start=(c == 0), stop=(c == KC - 1),
                             signal=(c == KC - 1))
                    st = stage.next()
                    evac(st, ps)
                    k.dma("act", V(PMd.ap[blk * 128:(blk + 1) * 128, tsl], PMb[blk].bufs), st)
        if upto == "A":
            break

        with k.scope():
            sm = k.sb("B0_sm", [128, T])
            zt = k.sb("B0_zt", [128, T])
            ones512 = k.sb("B0_ones", [128, 512])
            sb2 = k.sb("B0_b2", [128, 1])
            k.memset(ones512, 1.0)
            k.tt(sb2, vc("sm_bias"), vc("sm_sign"), ALU.mult)
            k.dma("sp", sm, V(PMd.ap[off["small"] * 128:(off["small"] + 1) * 128, :], PMb[off["small"]].bufs))
            k.actv(zt, sm, AF.Identity, bias=sb2, scale=vc("sm_sign"))
            k.actv(sm, zt, AF.Abs)
            k.actv(sm, sm, AF.Exp, scale=-1.0)
            k.actv(sm, sm, AF.Ln, bias=1.0)
            k.ts(zt, zt, 0.0, None, ALU.max)
            k.tt(zt, zt, sm, ALU.add)
            for s in range(T // 512):
                init = 0.0 if s == 0 else sm[:, s * 512 - 1:s * 512]
                k.scan(sm[:, s * 512:(s + 1) * 512], ones512, zt[:, s * 512:(s + 1) * 512], init, ALU.mult, ALU.add)
            k.dma("act", CSD, sm)
            k.dma("act", SPD, zt)

        with k.scope():
            TB = min(T, 2048)
            xin = k.sb("L_xin", [128, 3 + TB])
            gin = k.sb("L_gin", [128, TB])
            xc = k.sb("L_xc", [128, TB])
            rt = k.sb("L_rt", [128, TB])
            it = k.sb("L_it", [128, TB])
            at = k.sb("L_at", [128, TB])
            tmp = k.sb("L_tmp", [128, TB])
            hs = k.sb("L_hs", [128, TB])
            xcar = k.sb("L_xcar", [128, 3])
            hcar = k.sb("L_hcar", [128, 1])
            ca = k.sb("L_ca", [128, BC])
            bdr = k.sb("L_bdr", [128, BC, 128])
            bdi = k.sb("L_bdi", [128, BC, 128])
            k.dma("sp", bdr, lrubd[l, 0].re("c p m -> p c m"))
            k.dma("sp", bdi, lrubd[l, 1].re("c p m -> p c m"))
            k.actv(ca, vc("lru_lam", 0, BC), AF.Exp, scale=-1.0)
            k.actv(ca, ca, AF.Ln, bias=1.0)
            k.ts(ca, ca, -8.0, None, ALU.mult)
            for cc in range(BC):
                for tb in range(T // TB):
                    tsl = slice(tb * TB, (tb + 1) * TB)
                    if tb == 0:
                        k.memset(xin[:, 0:3], 0.0, eng="dve")
                    else:
                        k.copy(xin[:, 0:3], xcar)
                    k.dma("sp", xin[:, 3:], V(PMd.ap[(off["ax"] + cc) * 128:(off["ax"] + cc + 1) * 128, tsl], PMb[off["ax"] + cc].bufs))
                    k.dma("sp", gin, V(PMd.ap[(off["ag"] + cc) * 128:(off["ag"] + cc + 1) * 128, tsl], PMb[off["ag"] + cc].bufs))
                    k.ts(xc, xin[:, 3:3 + TB], vc("lru_cw", 3 * BC + cc), vc("lru_cb", cc), ALU.mult, ALU.add)
                    for tap in range(3):
                        k.stt(xc, xin[:, tap:tap + TB], vc("lru_cw", tap * BC + cc), xc, ALU.mult, ALU.add)
                    k.copy(xcar, xin[:, TB:TB + 3])
                    for s in range(TB // 512):
                        ssl = slice(s * 512, (s + 1) * 512)
                        ps = PS.next()
                        k.mm(ps, bdr[:, cc, :], xc[:, ssl])
                        k.actv(rt[:, ssl], ps, AF.Sigmoid, bias=vc("lru_br", cc))
                        ps = PS.next()
                        k.mm(ps, bdi[:, cc, :], xc[:, ssl])
                        k.actv(it[:, ssl], ps, AF.Sigmoid, bias=vc("lru_bi", cc))
                    k.actv(at, rt, AF.Exp, scale=ca[:, cc:cc + 1])
                    k.tt(tmp, at, at, ALU.mult)
                    k.ts(tmp, tmp, -1.0, 1.0, ALU.mult, ALU.add)
                    k.actv(tmp, tmp, AF.Sqrt)
                    k.tt(it, it, xc, ALU.mult)
                    k.tt(it, it, tmp, ALU.mult)
                    k.scan(hs, at, it, 0.0 if tb == 0 else hcar, ALU.mult, ALU.add)
                    k.copy(hcar, hs[:, TB - 1:TB])
                    k.actv(gin, gin, AF.Gelu_apprx_tanh)
                    k.tt(hs, hs, gin, ALU.mult)
                    k.dma("act", V(Yd.ap[cc * 128:(cc + 1) * 128, tsl], Yb[cc].bufs), hs)

        NKC = T // 128
        for hh in range(FH):
            with k.scope():
                qn = k.sb("F_qn", [128, T], F32R)
                kn = k.sb("F_kn", [128, T], F32R)
                vtm = k.sb("F_vtm", [128, NKC, 128], F32R)
                raw = Ring([k.sb("F_raw%d" % i, [128, 512]) for i in range(2)])
                sq = k.sb("F_sq", [128, 512], F32R)
                rs = k.sb("F_rs", [128, 512])
                ckp = k.sb("F_ckp", [128, NKC])
                cref = k.sb("F_cref", [128, NKC])
                bias = k.sb("F_bias", [128, NKC, NKC])
                pts = Ring([k.sb("F_pt%d" % i, [128, 512], F32R) for i in range(4)])
                tm = k.sb("F_tm", [128, 128])
                osb = k.sb("F_osb", [128, 512])
                rd = k.sb("F_rd", [128, 512])
                for (src, dst, gname) in ((off["bq"], qn, "fox_qg"), (off["bk"], kn, "fox_kg")):
                    for s in range(T // 512):
                        ssl = slice(s * 512, (s + 1) * 512)
                        r_ = raw.next()
                        k.dma("sp", r_, V(PMd.ap[(src + hh) * 128:(src + hh + 1) * 128, ssl], PMb[src + hh].bufs))
                        k.actv(sq, r_, AF.Square)
                        ps = PS.next()
                        k.mm(ps, ones_r, sq)
                        k.actv(rs, ps, AF.Sqrt, bias=eps128, scale=1.0 / 128)
                        k.recip(rs, rs)
                        k.stt(dst[:, ssl], r_, vc(gname), rs, ALU.mult, ALU.mult)
                for s in range(T // 512):
                    ssl = slice(s * 512, (s + 1) * 512)
                    r_ = raw.next()
                    k.dma("sp", r_, V(PMd.ap[(off["bv"] + hh) * 128:(off["bv"] + hh + 1) * 128, ssl], PMb[off["bv"] + hh].bufs))
                    ps = PS.next()
                    for j in range(4):
                        k.tr(ps[:, j * 128:(j + 1) * 128], r_[:, j * 128:(j + 1) * 128], ident)
                    evac(vtm[:, s * 4:(s + 1) * 4, :], ps.re("p (j d) -> p j d", j=4))
                k.dma("sp", ckp, V(CSD.ap[hh:hh + 1, :].rearrange("o (j p) -> p (o j)", p=128), CSD.bufs),
                      allow_slow_non_contiguous=True)
                k.memset(cref[:, 0:1], 0.0, eng="dve")
                if NKC > 1:
                    k.dma("sp", cref[:, 1:NKC],
                          V(CSD.ap[hh:hh + 1, 127:T - 1:128].partition_broadcast(128).rearrange("p o j -> p (o j)"), CSD.bufs),
                          allow_slow_non_contiguous=True)
                for qs in range(NKC):
                    k.ts(bias[:, qs, 0:qs + 1], ckp[:, 0:qs + 1], cref[:, qs:qs + 1], None, ALU.subtract)
                scale = 128 ** -0.5
                PS47 = Ring(PSb[4:])
                pairs = [(qt, j) for qt in range(T // 512) for j in range((qt + 1) * 4)]
                spsl = {}

                def emit_s(i):
                    qt, j = pairs[i]
                    sp_ = PS47.next()
                    k.mm(sp_, kn[:, j * 128:(j + 1) * 128], qn[:, qt * 512:(qt + 1) * 512])
                    spsl[i] = sp_
                emit_s(0)
                for i, (qt, j) in enumerate(pairs):
                    if i + 1 < len(pairs):
                        emit_s(i + 1)
                    qsl = slice(qt * 512, (qt + 1) * 512)
                    nk = (qt + 1) * 4
                    ops_ = PSb[(qt % 2) * 2]
                    dps = PSb[(qt % 2) * 2 + 1]
                    sps = spsl.pop(i)
                    pt = pts.next()
                    jr = j - qt * 4
                    for qq in range(4):
                        csl = slice(qq * 128, (qq + 1) * 128)
                        qs = qt * 4 + qq
                        if jr > qq:
                            k.memset(pt[:, csl].f(), 0.0, eng="dve")
                        elif jr == qq:
                            k.tt(tm, sps[:, csl], mneg, ALU.add)
                            k.actv(pt[:, csl], tm, AF.Exp, bias=bias[:, qs, j:j + 1], scale=scale)
                        else:
                            k.actv(pt[:, csl], sps[:, csl], AF.Exp, bias=bias[:, qs, j:j + 1], scale=scale)
                    k.mm(ops_, vtm[:, j, :], pt, start=(j == 0), stop=(j == nk - 1), signal=False)
                    k.mm(dps, ones_r, pt, start=(j == 0), stop=(j == nk - 1), signal=True)
                    if j == nk - 1:
                        k.recip(rd, dps)
                        k.tt(osb, ops_, rd, ALU.mult)
                        k.dma("act", V(Yd.ap[BW + hh * 128:BW + (hh + 1) * 128, qsl], Yb[BC + hh].bufs), osb)
        if upto == "B2":
            break

        NXBC = cfg.NXBC
        HPG = SH // G
        with k.scope():
            W = 512
            xin = k.sb("S_xin", [128, NXBC, 3 + W])
            xc = k.sb("S_xc", [128, NXBC, W])
            xcar = k.sb("S_xcar", [128, NXBC, 3])
            zt = k.sb("S_z", [128, BC, W])
            spw = k.sb("S_spw", [128, W])
            dan = k.sb("S_dan", [128, W])
            ac = k.sb("S_ac", [128, W])
            aneg = k.sb("S_aneg", [128, 1])
            atm = k.sb("S_atm", [128, 2, 128])
            xtm = k.sb("S_xtm", [128, BC * 128])
            xdt = k.sb("S_xdt", [128, SH, 64])
            btm = k.sb("S_btm", [128, G, 128])
            gts = k.sb("S_gts", [128, G, 128])
            rbs = k.sb("S_rbs", [128, 128])
            tmpa = k.sb("S_tmpa", [128, 128])
            dm = k.sb("S_dm", [128, 128])
            wts = k.sb("S_wts", [128, 128])
            es = k.sb("S_es", [128, 128])
            cps = k.sb("S_cps", [128, 128])
            naL = k.sb("S_naL", [128, 1])
            ds = k.sb("S_ds", [128, 1])
            bps = k.sb("S_bps", [128, 128])
            stt_ = [k.sb("S_state%d" % i, [128, 64]) for i in range(SH)]
            yw = k.sb("S_yw", [128, BC, W])
            sqr = k.sb("S_sqr", [128, W], F32R)
            bo64r = k.sb("S_bo64r", [128, 128], F32R)
            rsd = k.sb("S_rsd", [128, W])
            k.dma("pool", bo64r, cstd[:, co["bo64"]:co["bo64"] + 128])
            c2 = k.sb("S_c2", [128, cfg.NCST - NC1])
            k.dma("sp", c2, cstd[:, NC1:])
            k.actv(aneg, vc("ssd_alog"), AF.Exp)
            for s_ in stt_:
                k.memset(s_, 0.0, eng="dve")
            PS47 = Ring(PSb[4:])
            for w_ in range(T // W):
                wsl = slice(w_ * W, (w_ + 1) * W)
                if w_ == 0:
                    k.memset(xin[:, :, 0:3], 0.0, eng="dve")
                else:
                    k.copy(xin[:, :, 0:3], xcar)
                for b in range(NXBC):
                    k.dma("sp", xin[:, b, 3:], V(PMd.ap[(off["cx"] + b) * 128:(off["cx"] + b + 1) * 128, wsl], PMb[off["cx"] + b].bufs))
                k.dma("sp", zt, V(PMd.ap[off["cz"] * 128:(off["cz"] + BC) * 128, wsl].rearrange("(c p) t -> p c t", p=128),
                                  sum([PMb[off["cz"] + b].bufs for b in range(BC)], [])))
                k.dma("sp", spw, SPD[:, wsl])
                for b in range(NXBC):
                    k.ts(xc[:, b, :], xin[:, b, 3:3 + W], vc("ssd_cw", 3 * NXBC + b), vc("ssd_cb", b), ALU.mult, ALU.add)
                    for tap in range(3):
                        k.stt(xc[:, b, :], xin[:, b, tap:tap + W], vc("ssd_cw", tap * NXBC + b), xc[:, b, :], ALU.mult, ALU.add)
                k.copy(xcar, xin[:, :, W:W + 3])
                k.actv(xc, xc, AF.Silu)
                k.ts(dan, spw, aneg, None, ALU.mult)
                k.scan(ac, c2[:, co["reset128"] - NC1:co["reset128"] - NC1 + W], dan, 0.0, ALU.mult, ALU.add)
                for c in range(W // 128):
                    csl = slice(c * 128, (c + 1) * 128)
                    ps = PS47.next()
                    k.tr(ps[:, 0:128], ac[:, csl], ident)
                    k.tr(ps[:, 128:256], spw[:, csl], ident)
                    evac(atm, ps[:, 0:256].re("p (a b) -> p a b", a=2))
                    ps = PS47.next()
                    for b in range(BC):
                        k.tr(ps[:, b * 128:(b + 1) * 128], xc[:, b, csl], ident)
                    evac(xtm, ps[:, 0:BC * 128])
                    k.tt(xdt, xtm.re("p (h d) -> p h d", d=64),
                         V(atm.ap[:, 1, FH:FH + SH].unsqueeze(2).to_broadcast([128, SH, 64]), atm.bufs), ALU.mult)
                    ps = PS47.next()
                    for g in range(G):
                        k.tr(ps[:, g * 128:(g + 1) * 128], xc[:, BC + g, csl], ident)
                    evac(btm, ps[:, 0:G * 128].re("p (g n) -> p g n", g=G))
                    ps = PS47.next()
                    for g in range(G):
                        k.mm(ps[:, g * 128:(g + 1) * 128], xc[:, BC + g, csl], xc[:, BC + G + g, csl])
                    evac(gts, ps[:, 0:G * 128].re("p (g n) -> p g n", g=G))
                    for hd in range(SH):
                        g = hd // HPG
                        ps = PS47.next()
                        k.mm(ps[:, 0:128], c2[:, co["sel"] - NC1 + hd * 128: co["sel"] - NC1 + (hd + 1) * 128], ac[:, csl])
                        k.copy(rbs, ps[:, 0:128])
                        k.ts(tmpa, rbs, atm[:, 0, FH + hd:FH + hd + 1], -1.0, ALU.subtract, ALU.mult)
                        k.tt(tmpa, tmpa, mneg, ALU.add)
                        k.actv(dm, tmpa, AF.Exp)
                        k.tt(wts, gts[:, g, :], dm, ALU.mult)
                        k.actv(es, rbs, AF.Exp, scale=-1.0)
                        k.tt(cps, xc[:, BC + G + g, csl], es, ALU.mult)
                        yps = PSb[hd // 2]
                        yo = yps[(hd % 2) * 64:(hd % 2 + 1) * 64, csl]
                        k.mm(yo, xdt[:, hd, :], wts, start=True, stop=False, signal=False)
                        k.mm(yo, stt_[hd], cps, start=False, stop=True)
                        k.ts(naL, rbs[:, 127:128], -1.0, None, ALU.mult)
                        k.actv(ds, atm[:, 0, FH + hd:FH + hd + 1], AF.Exp, bias=naL)
                        k.ts(bps, btm[:, g, :], ds, None, ALU.mult)
                        ps2 = PS47.next()
                        k.mm(ps2[:, 0:64], bps, xdt[:, hd, :])
                        k.stt(stt_[hd], stt_[hd], es[:, 127:128], ps2[:, 0:64], ALU.mult, ALU.add)
                for b in range(BC):
                    k.copy(yw[:, b, :], PSb[b])
                    k.stt(yw[:, b, :], xc[:, b, :], vc("ssd_d", b), yw[:, b, :], ALU.mult, ALU.add)
                k.actv(zt, zt, AF.Silu)
                k.tt(yw, yw, zt, ALU.mult)
                if BC >= G:
                    bpg = BC // G
                    for g in range(G):
                        ps = PS47.next()
                        for j in range(bpg):
                            k.actv(sqr, yw[:, g * bpg + j, :], AF.Square)
                            k.mm(ps, ones_r, sqr, start=(j == 0), stop=(j == bpg - 1))
                        k.actv(rsd, ps, AF.Sqrt, bias=epsD, scale=1.0 / (128 * bpg))
                        k.recip(rsd, rsd)
                        for j in range(bpg):
                            b = g * bpg + j
                            k.stt(yw[:, b, :], yw[:, b, :], vc("ssd_nw", b), rsd, ALU.mult, ALU.mult)
                else:
                    assert BC == 1 and G == 2
                    ps = PS47.next()
                    k.actv(sqr, yw[:, 0, :], AF.Square)
                    k.mm(ps, bo64r, sqr)
                    k.actv(rsd, ps, AF.Sqrt, bias=epsD, scale=1.0 / 64)
                    k.recip(rsd, rsd)
                    k.stt(yw[:, 0, :], yw[:, 0, :], vc("ssd_nw", 0), rsd, ALU.mult, ALU.mult)
                for b in range(BC):
                    k.dma("act", V(Yd.ap[2 * BW + b * 128:2 * BW + (b + 1) * 128, wsl], Yb[2 * BC + b].bufs), yw[:, b, :])
        if upto == "B3":
            break

        NU = cfg.NU
        C0 = float(np.exp(-0.5))
        with k.scope():
            W = 256
            NCH = W // 64
            uin = k.sb("R_uin", [128, NU, 1 + W])
            ucar = k.sb("R_ucar", [128, NU, 1])
            dd = k.sb("R_dd", [128, NU, W])
            ul = k.sb("R_ul", [128, NU, W])
            twt = k.sb("R_twt", [128, W])
            sgt = k.sb("R_sgt", [128, W])
            wa2 = k.sb("R_wa2", [128, BW])
            g2 = k.sb("R_g2", [128, BW])
            c2 = k.sb("R_c2", [128, 512])
            k.dma("sp", c2, cstd[:, co["reset64"]:co["reset64"] + 512])
            k.dma("sp", wa2, rwwa2[l])
            k.dma("sp", g2, rwg2[l])
            bo64 = cst[:, co["bo64"]:co["bo64"] + 128]
            m_su = cst[:, co["m_su"]:co["m_su"] + 128]
            m_iu = cst[:, co["m_iu"]:co["m_iu"] + 128]
            m_sl = cst[:, co["m_sl"]:co["m_sl"] + 128]

            def per(name, shape):
                return [k.sb("R_%s%d" % (name, b), shape) for b in range(BC)]
            lwn = per("lwn", [128, W]); al = per("al", [128, W]); gg = per("gg", [128, W]); kkn = per("kkn", [128, W])
            kp = per("kp", [128, W]); gc = per("gc", [128, W]); bon = per("bon", [128, W]); yw = per("yw", [128, W])
            e1 = k.sb("R_e1", [128, W]); e2 = k.sb("R_e2", [128, W]); e3 = k.sb("R_e3", [128, W]); e4 = k.sb("R_e4", [128, W])
            t1 = k.sb("R_t1", [128, W]); t2 = k.sb("R_t2", [128, W]); qb = k.sb("R_qb", [128, W])
            gamc = per("gamc", [128, NCH])
            BDn = ("P", "Q", "K", "R", "Qh", "Kh", "V")
            BD = [{n: k.sb("R_bd%s%d" % (n, b), [128, NCH, 128]) for n in BDn} for b in range(BC)]
            for b in range(BC):
                for n in BDn:
                    k.memset(BD[b][n], 0.0, eng="dve")
            Hs = per("H", [128, 128])
            for b in range(BC):
                k.memset(Hs[b], 0.0, eng="dve")
            NTt = per("NT", [128, 128]); Nt = per("N", [128, 128]); AKT = per("AKT", [128, 128])
            RBT = per("RBT", [128, 128]); RKT = per("RKT", [128, 128])
            Vtm = per("Vtm", [128, 128]); Qhtm = per("Qhtm", [128, 128]); Khtm = per("Khtm", [128, 128])
            MT = [per("MT%d" % i, [128, 128]) for i in range(1, 6)]
            Ma = per("Ma", [128, 128]); Mb = per("Mb", [128, 128])
            Wa = per("Wa", [128, 128]); Wb = per("Wb", [128, 128])

            def bdwrite(fn, dst):
                fn(dst[0:64, :, 0:64], 0, 64)
                fn(dst[64:128, :, 64:128], 64, 128)

            def v3(x, lo, hi):
                return x[lo:hi, :].re("p (c t) -> p c t", t=64)

            for w_ in range(T // W):
                wsl = slice(w_ * W, (w_ + 1) * W)
                if w_ == 0:
                    k.memset(uin[:, :, 0:1], 0.0, eng="dve")
                else:
                    k.copy(uin[:, :, 0:1], ucar)
                k.dma("sp", uin[:, :, 1:], V(PMd.ap[off["dr"] * 128:(off["dr"] + NU) * 128, wsl].rearrange("(c p) t -> p c t", p=128),
                                             sum([PMb[off["dr"] + b].bufs for b in range(NU)], [])))
                k.tt(dd, uin[:, :, 0:W], uin[:, :, 1:1 + W], ALU.subtract)
                k.tt(dd, dd, V(vc("rw_mu", 0, NU).ap.unsqueeze(2).to_broadcast([128, NU, W]), vec.bufs), ALU.mult)
                k.tt(ul, dd, uin[:, :, 1:1 + W], ALU.add)
                k.copy(ucar, uin[:, :, W:W + 1])
                k.actv(twt[0:64, :], ul[0:64, 3 * BC, :], AF.Tanh)
                k.actv(sgt, ul[:, 3 * BC + 1, :], AF.Sigmoid)
                for b in range(BC):
                    bs = slice(b * 128, (b + 1) * 128)
                    r_, k_, v_ = ul[:, b, :], ul[:, BC + b, :], ul[:, 2 * BC + b, :]
                    ps = PS.next()
                    k.mm(ps[:, 0:W], wa2[0:64, bs], twt[0:64, :])
                    k.actv(lwn[b], ps[:, 0:W], AF.Sigmoid, bias=vc("rw_w0", b))
                    ps = PS.next()
                    k.mm(ps[:, 0:W], wa2[64:128, bs], ul[64:128, 3 * BC, :])
                    k.actv(al[b], ps[:, 0:W], AF.Sigmoid, bias=vc("rw_a0", b))
                    ps = PS.next()
                    k.mm(ps[:, 0:W], g2[:, bs], sgt)
                    k.copy(gg[b], ps[:, 0:W])
                    k.ts(t1, k_, vc("rw_kk", b), None, ALU.mult)
                    k.tt(t2, t1, t1, ALU.mult)
                    ps = PS.next()
                    k.mm(ps[:, 0:W], bo64, t2)
                    k.actv(t2, ps[:, 0:W], AF.Sqrt)
                    k.ts(t2, t2, 1e-12, None, ALU.max)
                    k.recip(t2, t2)
                    k.tt(kkn[b], t1, t2, ALU.mult)
                    k.ts(t1, al[b], -1.0, vc("rw_ka", b), ALU.add, ALU.mult)
                    k.ts(t1, t1, 1.0, None, ALU.add)
                    k.tt(kp[b], k_, t1, ALU.mult)
                    k.stt(t1, r_, vc("rw_rk", b), kp[b], ALU.mult, ALU.mult)
                    ps = PS.next()
                    k.mm(ps[:, 0:W], bo64, t1)
                    k.tt(bon[b], ps[:, 0:W], v_, ALU.mult)
                    k.scan(gc[b], c2[:, 0:W], lwn[b], 0.0, ALU.mult, ALU.add)
                    k.actv(e1, gc[b], AF.Exp, scale=-C0)
                    k.actv(e2, gc[b], AF.Exp, scale=C0)
                    k.tt(t1, gc[b], lwn[b], ALU.subtract)
                    k.actv(e3, t1, AF.Exp, scale=-C0)
                    g3 = gc[b].re("p (c t) -> p c t", t=64)
                    k.tt(t1.re("p (c t) -> p c t", t=64), V(g3.ap[:, :, 63:64].to_broadcast([128, NCH, 64]), gc[b].bufs), g3, ALU.subtract)
                    k.actv(e4, t1, AF.Exp, scale=-C0)
                    k.copy(gamc[b], e1.re("p (c t) -> p c t", t=64)[:, :, 63])
                    k.tt(qb, kkn[b], al[b], ALU.mult)
                    bd = BD[b]
                    bdwrite(lambda o, lo, hi: k.stt(o, v3(kkn[b], lo, hi), -1.0, v3(e3, lo, hi), ALU.mult, ALU.mult), bd["P"])
                    bdwrite(lambda o, lo, hi: k.tt(o, v3(qb, lo, hi), v3(e2, lo, hi), ALU.mult), bd["Q"])
                    bdwrite(lambda o, lo, hi: k.tt(o, v3(kp[b], lo, hi), v3(e2, lo, hi), ALU.mult), bd["K"])
                    bdwrite(lambda o, lo, hi: k.tt(o, v3(r_, lo, hi), v3(e1, lo, hi), ALU.mult), bd["R"])
                    bdwrite(lambda o, lo, hi: k.tt(o, v3(qb, lo, hi), v3(e4, lo, hi), ALU.mult), bd["Qh"])
                    bdwrite(lambda o, lo, hi: k.tt(o, v3(kp[b], lo, hi), v3(e4, lo, hi), ALU.mult), bd["Kh"])
                    bdwrite(lambda o, lo, hi: k.copy(o, v3(v_, lo, hi)), bd["V"])
                for c in range(NCH):
                    Q = 128

                    def pq():
                        return PS.next()[:, 0:Q]
                    for b in range(BC):
                        bd = BD[b]
                        for (dst, lt, rh, msk) in ((NTt[b], "Q", "P", m_su), (Nt[b], "P", "Q", m_sl), (AKT[b], "K", "P", m_su),
                                                   (RBT[b], "Q", "R", m_iu), (RKT[b], "K", "R", m_iu)):
                            ps = pq()
                            k.mm(ps, bd[lt][:, c, :], bd[rh][:, c, :])
                            k.tt(dst, ps, msk, ALU.mult)
                    for b in range(BC):
                        bd = BD[b]
                        for (dst, src) in ((Vtm[b], "V"), (Qhtm[b], "Qh"), (Khtm[b], "Kh")):
                            ps = pq()
                            k.tr(ps, bd[src][:, c, :], ident)
                            evac(dst, ps)
                    for lev in range(1, 6):
                        for b in range(BC):
                            mt_prev = NTt[b] if lev == 1 else MT[lev - 2][b]
                            m_prev = Nt[b] if lev == 1 else (Ma[b] if lev % 2 == 0 else Mb[b])
                            m_new = Ma[b] if lev % 2 == 1 else Mb[b]
                            ps = pq()
                            k.mm(ps, m_prev, mt_prev)
                            evac(MT[lev - 1][b], ps)
                            if lev < 5:
                                ps = pq()
                                k.mm(ps, mt_prev, m_prev)
                                evac(m_new, ps)
                    for b in range(BC):
                        bd = BD[b]
                        ps = pq()
                        k.mm(ps, bd["P"][:, c, :], Hs[b], start=True, stop=False, signal=False)
                        k.mm(ps, AKT[b], Vtm[b], start=False, stop=True)
                        evac(Wa[b], ps)
                    for lev in range(6):
                        for b in range(BC):
                            mt = NTt[b] if lev == 0 else MT[lev - 1][b]
                            wc = Wa[b] if lev % 2 == 0 else Wb[b]
                            wn = Wb[b] if lev % 2 == 0 else Wa[b]
                            ps = pq()
                            k.mm(ps, mt, wc)
                            k.tt(wn, wc, ps, ALU.add)
                    for b in range(BC):
                        bd = BD[b]
                        ps = pq()
                        k.mm(ps, Hs[b], bd["R"][:, c, :], start=True, stop=False, signal=False)
                        k.mm(ps, Wa[b], RBT[b], start=False, stop=False, signal=False)
                        k.mm(ps, Vtm[b], RKT[b], start=False, stop=True)
                        k.actv(yw[b][0:64, c * 64:(c + 1) * 64], ps[0:64, 0:64], AF.Copy)
                        k.copy(yw[b][64:128, c * 64:(c + 1) * 64], ps[64:128, 64:128])
                    for b in range(BC):
                        ps = pq()
                        k.mm(ps, Qhtm[b], Wa[b], start=True, stop=False, signal=False)
                        k.mm(ps, Khtm[b], Vtm[b], start=False, stop=True)
                        k.stt(Hs[b], Hs[b], gamc[b][:, c:c + 1], ps, ALU.mult, ALU.add)
                for b in range(BC):
                    ps = PS.next()
                    k.mm(ps[:, 0:W], bo64, yw[b])
                    k.stt(t1, ps[:, 0:W], -1.0 / 64, yw[b], ALU.mult, ALU.add)
                    k.tt(t2, t1, t1, ALU.mult)
                    ps = PS.next()
                    k.mm(ps[:, 0:W], bo64, t2)
                    k.actv(t2, ps[:, 0:W], AF.Sqrt, bias=lneps, scale=1.0 / 64)
                    k.recip(t2, t2)
                    k.tt(t1, t1, t2, ALU.mult)
                    k.ts(t1, t1, vc("rw_lnw", b), vc("rw_lnb", b), ALU.mult, ALU.add)
                    k.tt(t1, t1, bon[b], ALU.add)
                    k.tt(t1, t1, gg[b], ALU.mult)
                    k.dma("act", V(Yd.ap[3 * BW + b * 128:3 * BW + (b + 1) * 128, wsl], Yb[3 * BC + b].bufs), t1)
        if upto == "B4":
            break

        FGS = cfg.FGS
        with k.scope():
            wring = wring_make(4, WMAX)
            h = k.sb("C_h", [128, KC, NT])
            xn = k.sb("C_xn", [128, KC, NT], F32R)
            ybuf = k.sb("C_y", [128, max(KC, FGS), NT], F32R)
            mg = k.sb("C_mg", [128, KC, NT], F32R)
            rs = k.sb("C_rs", [128, NT])
            sigs = Ring([k.sb("C_sig%d" % i, [128, NT]) for i in range(2)])
            tmps = Ring([k.sb("C_tmp%d" % i, [128, NT]) for i in range(2)])
            mgacc = k.sb("C_mgacc", [128, NT])
            ptile = k.sb("C_pt", [128, 2, NT], F32R)
            wbring = Ring([k.sb("C_wb%d" % i, [128, BC * 128], F32R) for i in range(2)])
            wpr = Ring([k.sb("C_wple%d" % i, [128, 256], F32R) for i in range(2)])
            ucar = k.sb("C_ucar", [128, 2 * FB, 2])
            ust = [Ring([k.sb("C_ust%d_%d" % (i, r_), [128, 2 + NT]) for r_ in range(2)]) for i in range(2)]
            ucv = [Ring([k.sb("C_ucv%d_%d" % (i, r_), [128, NT]) for r_ in range(2)]) for i in range(2)]
            ucs = [None, None]
            for tt in range(NTT):
                tsl = slice(tt * NT, (tt + 1) * NT)
                k.dma("sp", h, V(hsrc.ap[:, tsl].rearrange("(c p) t -> p c t", p=128), hsrc_t[tt].bufs))
                k.dma("pool", ybuf[:, 0:KC, :], V(Yd.ap[:, tsl].rearrange("(c p) t -> p c t", p=128), sum([y_.bufs for y_ in Yb], [])))
                k.dma("pool", ptile, pT[l][:, tsl].re("(c p) t -> p c t", p=128))
                rmsnorm(h, xn, "mixn", rs)
                for dc in range(KC):
                    for j in range(4):
                        wb = wbring.next()
                        k.dma("pool", wb, wbr[l, dc][:, j * BC * 128:(j + 1) * BC * 128])
                        wg = wring.next()
                        k.dma("pool", wg[:, :KC * 128], wgate[l, dc * 4 + j])
                        gps = PS.next()
                        for c in range(KC):
                            k.mm(gps, wg[:, c * 128:(c + 1) * 128], xn[:, c, :], start=(c == 0), stop=(c == KC - 1), signal=(c == KC - 1))
                        zps = PS.next()
                        for c in range(BC):
                            k.mm(zps, wb[:, c * 128:(c + 1) * 128], ybuf[:, j * BC + c, :], start=(c == 0), stop=(c == BC - 1),
                                 signal=(c == BC - 1))
                        sg_ = sigs.next()
                        k.actv(sg_, gps, AF.Sigmoid)
                        if j == 0:
                            k.tt(mgacc, sg_, zps, ALU.mult)
                        else:
                            tp = tmps.next()
                            k.tt(tp, sg_, zps, ALU.mult)
                            k.tt(mg[:, dc, :] if j == 3 else mgacc, mgacc, tp, ALU.add)
                for dc in range(KC):
                    w = wring.next()
                    k.dma("pool", w[:, :KC * 128], wout[l, dc])
                    ps = PS.next()
                    for c in range(KC):
                        k.mm(ps, w[:, c * 128:(c + 1) * 128], mg[:, c, :], start=(c == 0), stop=(c == KC - 1), signal=(c == KC - 1))
                    k.tt(h[:, dc, :], h[:, dc, :], ps, ALU.add)
                rmsnorm(h, xn, "ffnn", rs)
                for fg in range(FB // FGS):
                    for fi in range(FGS):
                        fb = fg * FGS + fi
                        for half, blk in ((0, fb), (1, FB + fb)):
                            w = wring.next()
                            k.dma("pool", w[:, :KC * 128], wup[l, blk])
                            ps = PS.next()
                            for c in range(KC):
                                k.mm(ps, w[:, c * 128:(c + 1) * 128], xn[:, c, :], start=(c == 0), stop=(c == KC - 1), signal=(c == KC - 1))
                            u_ = ust[half].next()
                            if tt == 0:
                                k.memset(u_[:, 0:2], 0.0, eng="dve")
                            else:
                                k.copy(u_[:, 0:2], ucar[:, blk, :])
                            k.actv(u_[:, 2:2 + NT], ps, AF.Copy)
                            k.copy(ucar[:, blk, :], u_[:, NT:NT + 2])
                            uc = ucv[half].next()
                            ucs[half] = uc
                            k.actv(uc, ps, AF.Identity, bias=vc("ffn_cb", blk), scale=vc("ffn_cw", 2 * 2 * FB + blk))
                            k.stt(uc, u_[:, 1:1 + NT], vc("ffn_cw", 1 * 2 * FB + blk), uc, ALU.mult, ALU.add)
                            k.stt(uc, u_[:, 0:NT], vc("ffn_cw", 0 * 2 * FB + blk), uc, ALU.mult, ALU.add)
                        k.actv(ucs[0], ucs[0], AF.Gelu_apprx_tanh)
                        k.tt(ybuf[:, fi, :], ucs[0], ucs[1], ALU.mult)
                    for dc in range(KC):
                        w = wring.next()
                        k.dma("pool", w[:, :FGS * 128], wdown[l, dc][:, fg * FGS * 128:(fg + 1) * FGS * 128])
                        ps = PS.next()
                        for fi in range(FGS):
                            k.mm(ps, w[:, fi * 128:(fi + 1) * 128], ybuf[:, fi, :], start=(fi == 0), stop=(fi == FGS - 1), signal=(fi == FGS - 1))
                        k.tt(h[:, dc, :], h[:, dc, :], ps, ALU.add)
                rmsnorm(h, xn, "plen", rs)
                for dc in range(KC):
                    w = wring.next()
                    k.dma("pool", w[:, :KC * 128], wpg[l, dc])
                    gps = PS.next()
                    for c in range(KC):
                        k.mm(gps, w[:, c * 128:(c + 1) * 128], xn[:, c, :], start=(c == 0), stop=(c == KC - 1), signal=(c == KC - 1))
                    eps_ = PS.next()
                    wp = wpr.next()
                    k.dma("pool", wp, wple[l, dc])
                    for c in range(2):
                        k.mm(eps_, wp[:, c * 128:(c + 1) * 128], ptile[:, c, :], start=(c == 0), stop=(c == 1), signal=(c == 1))
                    sg_ = sigs.next()
                    k.actv(sg_, gps, AF.Sigmoid)
                    tp = tmps.next()
                    k.tt(tp, sg_, eps_, ALU.mult)
                    k.tt(h[:, dc, :], h[:, dc, :], tp, ALU.add)
                k.dma("act", V(hdst.ap[:, tsl].rearrange("(c p) t -> p c t", p=128), hdst_t[tt].bufs), h)
    k.finish()
    return nc


FULL = Cfg(D=2048, T=8192, L=4, DFF=5632, FGS=11)
_cache = {}


def kernel(**inputs):
    cfg = FULL
    B = inputs["x"].shape[0]
    if "nc" not in _cache:
        _cache["nc"] = build(cfg)
    nc = _cache["nc"]
    in_maps = [host_pack(cfg, inputs, c % B) for c in range(8)]
    res = run_bass_kernel_spmd(nc, in_maps, core_ids=list(range(8)))
    out = np.stack([np.ascontiguousarray(res.results[b]["out"].T) for b in range(B)], axis=0)
    return out.astype(np.float32)
```
